# Optimizing a Trainium2 kernel written in Bass

```python
import jax, jax.numpy as jnp
from jax import lax
import numpy as np

D_MODEL = 1024
BATCH = 8
SEQ = 4096
DEPTH = 1
DEC_BATCH = 128
DEC_SEQ = 1
PAST_LEN = 16384
PAGE_SIZE = 128

HEAD_DIM = 64
MIX_WIDTH = D_MODEL
A_HEADS = MIX_WIDTH // 2 // HEAD_DIM
B_HEADS = MIX_WIDTH // 2 // HEAD_DIM
B_KV_HEADS = 2
B_GROUP = B_HEADS // B_KV_HEADS
DILATED_CONFIGS = ((128, 1), (512, 4), (2048, 16))
BAND = 128
A_WINDOW = 2048
B_WINDOW = 128
ROT_DIM = HEAD_DIM // 4
ROPE_THETA = 500000.0
ALPHA = (2 * DEPTH) ** 0.25
BETA = (8 * DEPTH) ** -0.25
LN_EPS = 1e-5
A_W = A_HEADS * HEAD_DIM
B_W = B_HEADS * HEAD_DIM
B_KV_W = B_KV_HEADS * HEAD_DIM
IN_SPLITS = (A_W, A_W, A_W, A_W, B_W, B_KV_W, B_KV_W, B_W)
IN_WIDTH = sum(IN_SPLITS)
SPLIT_IDX = tuple(int(c) for c in np.cumsum(IN_SPLITS)[:-1])

kernel_name = "hymba_dilated_swa_sink_deepnorm_step"


def _layer_norm(x, g, b):
    xf = x.astype(jnp.float32)
    mu = xf.mean(-1, keepdims=True)
    var = jnp.square(xf - mu).mean(-1, keepdims=True)
    return ((xf - mu) * lax.rsqrt(var + LN_EPS) * g.astype(jnp.float32) + b.astype(jnp.float32)).astype(x.dtype)


def _rope(x, pos):
    inv = ROPE_THETA ** (-jnp.arange(0, ROT_DIM, 2, dtype=jnp.float32) / ROT_DIM)
    ang = pos.astype(jnp.float32)[:, None] * inv[None, :]
    cos = jnp.cos(ang)[:, None, :]
    sin = jnp.sin(ang)[:, None, :]
    xf = x.astype(jnp.float32)
    x1 = xf[..., :ROT_DIM // 2]
    x2 = xf[..., ROT_DIM // 2:ROT_DIM]
    out = jnp.concatenate([x1 * cos - x2 * sin, x2 * cos + x1 * sin, xf[..., ROT_DIM:]], axis=-1)
    return out.astype(x.dtype)


def _project(x, pos, w_in):
    b, s, _ = x.shape
    h = jnp.einsum('bsd,de->bse', x, w_in)
    qa, ka, va, ga, qb, kb, vb, gb = jnp.split(h, SPLIT_IDX, axis=-1)
    qa = _rope(qa.reshape(b, s, A_HEADS, HEAD_DIM), pos)
    ka = _rope(ka.reshape(b, s, A_HEADS, HEAD_DIM), pos)
    va = va.reshape(b, s, A_HEADS, HEAD_DIM)
    qb = _rope(qb.reshape(b, s, B_HEADS, HEAD_DIM), pos)
    kb = _rope(kb.reshape(b, s, B_KV_HEADS, HEAD_DIM), pos)
    vb = vb.reshape(b, s, B_KV_HEADS, HEAD_DIM)
    return qa, ka, va, ga, qb, kb, vb, gb


def _dilate(x, d):
    b, s = x.shape[:2]
    rest = x.shape[2:]
    L = s // d
    xs = jnp.moveaxis(x.reshape(b, L, d, *rest), 2, 1).reshape(b * d, L, *rest)
    lp = -(-L // BAND) * BAND
    return jnp.pad(xs, [(0, 0), (0, lp - L)] + [(0, 0)] * len(rest))


def _undilate(y, b, s, d):
    L = s // d
    rest = y.shape[2:]
    y = y[:, :L].reshape(b, d, L, *rest)
    return jnp.moveaxis(y, 1, 2).reshape(b, s, *rest)


def _band_partial(q, k, v):
    n, L, hk, g, dh = q.shape
    nb = L // BAND
    qb = q.reshape(n, nb, BAND, hk, g, dh).astype(jnp.float32)
    kb = k.reshape(n, nb, BAND, hk, dh).astype(jnp.float32)
    vb = v.reshape(n, nb, BAND, hk, dh).astype(jnp.float32)
    kprev = jnp.pad(kb, ((0, 0), (1, 0), (0, 0), (0, 0), (0, 0)))[:, :-1]
    vprev = jnp.pad(vb, ((0, 0), (1, 0), (0, 0), (0, 0), (0, 0)))[:, :-1]
    kk = jnp.concatenate([kprev, kb], axis=2)
    vv = jnp.concatenate([vprev, vb], axis=2)
    sc = jnp.einsum('nbqhgd,nbkhd->nbhgqk', qb, kk) * (HEAD_DIM ** -0.5)
    qi = jnp.arange(BAND)[:, None]
    ki = jnp.arange(2 * BAND)[None, :]
    dist = BAND + qi - ki
    kpos = jnp.arange(nb)[:, None, None] * BAND - BAND + ki[None]
    valid = (dist >= 0) & (dist <= BAND) & (kpos >= 0)
    sc = jnp.where(valid[None, :, None, None], sc, -jnp.inf)
    m = sc.max(-1)
    p = jnp.exp(sc - m[..., None])
    den = p.sum(-1)
    o = jnp.einsum('nbhgqk,nbkhd->nbqhgd', p, vv).reshape(n, L, hk, g, dh)
    m = jnp.transpose(m, (0, 1, 4, 2, 3)).reshape(n, L, hk, g)
    den = jnp.transpose(den, (0, 1, 4, 2, 3)).reshape(n, L, hk, g)
    return m, den, o


def _strided_partial(q, kv_all, buf_len, dilation):
    t = q.shape[1]
    j = jnp.arange(BAND + 1)
    idx = buf_len + jnp.arange(t)[:, None] - dilation * j[None, :]
    valid = idx >= 0
    kv = kv_all[:, jnp.clip(idx, 0)].astype(jnp.float32)
    k = kv[:, :, :, 0]
    v = kv[:, :, :, 1]
    sc = jnp.einsum('nqhgd,nqkhd->nqhgk', q.astype(jnp.float32), k) * (HEAD_DIM ** -0.5)
    sc = jnp.where(valid[None, :, None, None, :], sc, -jnp.inf)
    m = sc.max(-1)
    p = jnp.exp(sc - m[..., None])
    den = p.sum(-1)
    o = jnp.einsum('nqhgk,nqkhd->nqhgd', p, v)
    return m, den, o


def _merge(parts, sink=None):
    m = parts[0][0]
    for pm, _, _ in parts[1:]:
        m = jnp.maximum(m, pm)
    if sink is not None:
        m = jnp.maximum(m, sink)
    den = 0.0
    num = 0.0
    for pm, ps, po in parts:
        w = jnp.exp(pm - m)
        den = den + ps * w
        num = num + po * w[..., None]
    if sink is not None:
        den = den + jnp.exp(sink - m)
    return num / den[..., None]


def _output(x, oa, ob, ga, gb, w_o, ln_g, ln_b):
    b, s, _ = x.shape
    mix = jnp.concatenate([oa.reshape(b, s, A_W).astype(x.dtype) * jax.nn.silu(ga),
                           ob.reshape(b, s, B_W).astype(x.dtype) * jax.nn.silu(gb)], axis=-1)
    out = jnp.einsum('bse,ed->bsd', mix, w_o)
    return _layer_norm(ALPHA * x + out, ln_g, ln_b)


def _prompt_layer(x, w_in, sinks, w_o, ln_g, ln_b):
    b, s, _ = x.shape
    pos = jnp.arange(s)
    qa, ka, va, ga, qb, kb, vb, gb = _project(x, pos, w_in)
    parts = []
    for _, d in DILATED_CONFIGS:
        m, den, o = _band_partial(_dilate(qa[:, :, :, None], d), _dilate(ka, d), _dilate(va, d))
        parts.append((_undilate(m, b, s, d), _undilate(den, b, s, d), _undilate(o, b, s, d)))
    oa = _merge(parts)
    qbg = qb.reshape(b, s, B_KV_HEADS, B_GROUP, HEAD_DIM)
    m, den, o = _band_partial(_dilate(qbg, 1), _dilate(kb, 1), _dilate(vb, 1))
    sink = sinks.astype(jnp.float32).reshape(B_KV_HEADS, B_GROUP)
    ob = _merge([(_undilate(m, b, s, 1), _undilate(den, b, s, 1), _undilate(o, b, s, 1))], sink)
    y = _output(x, oa, ob, ga, gb, w_o, ln_g, ln_b)
    wa = min(A_WINDOW, s)
    wb = min(B_WINDOW, s)
    kv_a = jnp.stack([ka[:, s - wa:], va[:, s - wa:]], axis=2)
    kv_b = jnp.stack([kb[:, s - wb:], vb[:, s - wb:]], axis=2)
    return y, kv_a, kv_b


def _sample_layer(x, kv_a_cache, kv_b_cache, w_in, sinks, w_o, ln_g, ln_b):
    b, t, _ = x.shape
    pos = PAST_LEN + jnp.arange(t)
    qa, ka, va, ga, qb, kb, vb, gb = _project(x, pos, w_in)
    new_a = jnp.stack([ka, va], axis=2)
    new_b = jnp.stack([kb, vb], axis=2)
    all_a = jnp.concatenate([kv_a_cache.astype(new_a.dtype), new_a], axis=1)
    all_b = jnp.concatenate([kv_b_cache.astype(new_b.dtype), new_b], axis=1)
    buf_a = kv_a_cache.shape[1]
    buf_b = kv_b_cache.shape[1]
    parts = [_strided_partial(qa[:, :, :, None], all_a, buf_a, d) for _, d in DILATED_CONFIGS]
    oa = _merge(parts)
    qbg = qb.reshape(b, t, B_KV_HEADS, B_GROUP, HEAD_DIM)
    sink = sinks.astype(jnp.float32).reshape(B_KV_HEADS, B_GROUP)
    ob = _merge([_strided_partial(qbg, all_b, buf_b, 1)], sink)
    y = _output(x, oa, ob, ga, gb, w_o, ln_g, ln_b)
    return y, new_a, new_b


def setup_inputs(seed: int = 0) -> dict:
    key = jax.random.key(seed)
    ks = jax.random.split(key, 9)
    buf_a = min(A_WINDOW, PAST_LEN)
    buf_b = min(B_WINDOW, PAST_LEN)
    x_prompt = jax.random.normal(ks[0], (BATCH, SEQ, D_MODEL), jnp.float32)
    x_sample = jax.random.normal(ks[1], (DEC_BATCH, DEC_SEQ, D_MODEL), jnp.float32)
    cache_a_kv = jax.random.normal(ks[2], (DEPTH, DEC_BATCH, buf_a, 2, A_HEADS, HEAD_DIM), jnp.float32)
    cache_b_kv = jax.random.normal(ks[3], (DEPTH, DEC_BATCH, buf_b, 2, B_KV_HEADS, HEAD_DIM), jnp.float32)
    w_in = jax.random.normal(ks[4], (DEPTH, D_MODEL, IN_WIDTH), jnp.float32) * D_MODEL ** -0.5
    attn_sinks = 0.5 * jax.random.normal(ks[5], (DEPTH, B_HEADS), jnp.float32)
    w_o = jax.random.normal(ks[6], (DEPTH, MIX_WIDTH, D_MODEL), jnp.float32) * (MIX_WIDTH ** -0.5 * BETA)
    ln_g = 1.0 + 0.05 * jax.random.normal(ks[7], (DEPTH, D_MODEL), jnp.float32)
    ln_b = 0.05 * jax.random.normal(ks[8], (DEPTH, D_MODEL), jnp.float32)
    return {"x_prompt": x_prompt, "x_sample": x_sample, "cache_a_kv": cache_a_kv, "cache_b_kv": cache_b_kv,
            "w_in": w_in, "attn_sinks": attn_sinks, "w_o": w_o, "ln_g": ln_g, "ln_b": ln_b}


def reference(x_prompt, x_sample, cache_a_kv, cache_b_kv, w_in, attn_sinks, w_o, ln_g, ln_b):
    yp = x_prompt
    ys = x_sample
    pa, pb, sa, sb = [], [], [], []
    for l in range(DEPTH):
        yp, kva, kvb = _prompt_layer(yp, w_in[l], attn_sinks[l], w_o[l], ln_g[l], ln_b[l])
        ys, nka, nkb = _sample_layer(ys, cache_a_kv[l], cache_b_kv[l], w_in[l], attn_sinks[l], w_o[l], ln_g[l], ln_b[l])
        pa.append(kva)
        pb.append(kvb)
        sa.append(nka)
        sb.append(nkb)
    prompt_a_kv = jnp.stack(pa)
    prompt_b_kv = jnp.stack(pb)
    sample_a_kv = jnp.stack(sa)
    sample_b_kv = jnp.stack(sb)
    return (yp, ys, prompt_a_kv, prompt_b_kv, sample_a_kv, sample_b_kv)
```

```python
import os
import numpy as np
import concourse.bass as bass
import concourse.mybir as mybir
from concourse.bass_utils import run_bass_kernel_spmd

F32 = mybir.dt.float32
BF16 = mybir.dt.bfloat16
ALU = mybir.AluOpType
AF = mybir.ActivationFunctionType
AX = mybir.AxisListType

NCORES = 8
D = 1024
S = 4096
NS = 16
PAST = 16384
ALPHA = 2.0 ** 0.25
EPS = 1e-5
DILS = (1, 4, 16)
NPAIR = 8
NDMA_SEM = 24

_CACHE = {}


class Res:
    __slots__ = ("w", "r")
    default_w = None

    def __init__(self):
        self.w = Res.default_w
        self.r = []


class Prog:
    ENG = ("pe", "act", "dve", "pool", "sp")

    def __init__(self):
        self.q = {e: [] for e in self.ENG}
        self.needed = set()
        self.dma_slot_tok = [None] * NDMA_SEM
        self.dma_slot_cnt = [0] * NDMA_SEM
        self.dma_next = {"sp": 0, "pool": 0, "act": 0}
        self.dma_rng = {"sp": (0, 14), "pool": (14, 10), "act": (0, 14)}

    def emit(self, eng, fn, waits, dma=False):
        ws = [w for w in waits if w is not None]
        if dma:
            base, cnt = self.dma_rng[eng]
            slot = base + self.dma_next[eng]
            self.dma_next[eng] = (self.dma_next[eng] + 1) % cnt
            if self.dma_slot_tok[slot] is not None:
                ws.append(self.dma_slot_tok[slot])
            self.dma_slot_cnt[slot] += 16
            tok = ("dma", slot, self.dma_slot_cnt[slot])
            self.dma_slot_tok[slot] = tok
        else:
            tok = (eng, len(self.q[eng]))
        ws2 = []
        for w in ws:
            if eng == "pe" and w[0] == "pe":
                continue
            ws2.append(w)
            if w[0] != "dma":
                self.needed.add(w)
        self.q[eng].append((fn, ws2, tok if dma else None))
        return tok

    def op(self, eng, fn, reads=(), writes=(), dma=False, extra=()):
        ws = list(extra)
        for r in reads:
            ws.append(r.w)
        for r in writes:
            ws.append(r.w)
            ws.extend(r.r)
        tok = self.emit(eng, fn, ws, dma)
        for r in reads:
            if tok[0] != "dma":
                r.r = [t for t in r.r if t[0] != tok[0]]
            r.r.append(tok)
        for r in writes:
            r.w = tok
            r.r = []
        return tok


def _consts():
    c = {}
    c["ident_bf"] = np.eye(128, dtype=np.float32)
    k = np.arange(128)[:, None]
    q = np.arange(256)[None, :]
    m = np.where(q < 128, (q >= k), ((q - 128) <= k)).astype(np.float32)
    c["mask2"] = np.concatenate([m, m, m, m], axis=1)
    inv = 500000.0 ** (-np.arange(0, 16, 2, dtype=np.float32) / 16.0)
    inv = inv.astype(np.float32)
    pos = np.arange(S, dtype=np.float32)
    ang = pos[:, None] * inv[None, :]
    cos = np.cos(ang).astype(np.float32)
    sin = np.sin(ang).astype(np.float32)
    rows = np.arange(64)
    f = rows % 8
    sign = np.where((rows % 16) < 8, -1.0, 1.0).astype(np.float32)
    c["CC"] = np.ascontiguousarray(cos[:, f].T)
    c["SS"] = np.ascontiguousarray((sin[:, f] * sign[None, :]).T)
    c["cosT"] = np.ascontiguousarray(cos[2048:].reshape(16, 128, 8).transpose(1, 0, 2))
    c["sinT"] = np.ascontiguousarray(sin[2048:].reshape(16, 128, 8).transpose(1, 0, 2))
    c["cosTb"] = np.ascontiguousarray(cos[S - 128:])
    c["sinTb"] = np.ascontiguousarray(sin[S - 128:])
    angs = (np.float32(PAST) * inv).astype(np.float32)
    c["cs_s"] = np.repeat(np.concatenate([np.cos(angs), np.sin(angs)])[None, :].astype(np.float32), NS, 0)
    misc = np.zeros((128, 128), np.float32)
    misc[64, 0:64] = 1.0
    misc[65, 64:128] = 1.0
    c["sel2"] = misc
    c["neg1"] = -np.ones((128, 512), np.float32)
    en = np.zeros((8, NS, NS), np.float32)
    for n in range(NS):
        en[:, n, n] = 1.0
    c["en"] = en.reshape(8, NS * NS)
    bd = np.zeros((8, 8, 64), np.float32)
    for h in range(8):
        bd[h, h, :] = 1.0
    c["bd"] = bd.reshape(8, 512)
    c["i8"] = np.eye(8, dtype=np.float32)
    c["i16"] = np.eye(16, dtype=np.float32)
    c["onescol"] = np.ones((128, 1), np.float32)
    return c


CONST_SHAPES = None


def _pair_cols(j):
    def rope_rows(base_a, base_b):
        return list(range(base_a, base_a + 16)) + list(range(base_b, base_b + 16))

    def swap16(cols):
        out = []
        for g in range(0, len(cols), 16):
            blk = cols[g:g + 16]
            out += blk[8:16] + blk[0:8]
        return out
    if j < 4:
        ha, hb = 2 * j, 2 * j + 1
        q = list(range(0 + 128 * j, 128 * j + 128))
        k = list(range(512 + 128 * j, 512 + 128 * j + 128))
        v = list(range(1024 + 128 * j, 1024 + 128 * j + 128))
        g = list(range(1536 + 128 * j, 1536 + 128 * j + 128))
        r = rope_rows(64 * ha, 64 * hb) + rope_rows(512 + 64 * ha, 512 + 64 * hb)
    else:
        i = j - 4
        ha, hb = i, 4 + i
        q = list(range(2048 + 64 * ha, 2048 + 64 * ha + 64)) + list(range(2048 + 64 * hb, 2048 + 64 * hb + 64))
        k = list(range(2560, 2688))
        v = list(range(2688, 2816))
        g = list(range(2816 + 64 * ha, 2816 + 64 * ha + 64)) + list(range(2816 + 64 * hb, 2816 + 64 * hb + 64))
        r = rope_rows(2048 + 64 * ha, 2048 + 64 * hb) + rope_rows(2560, 2560 + 64)
    r = r + swap16(r)
    return q + k + v + g + r


def _wo_rows():
    rows = list(range(512))
    for i in range(4):
        ha, hb = i, 4 + i
        rows += list(range(512 + 64 * ha, 512 + 64 * ha + 64)) + list(range(512 + 64 * hb, 512 + 64 * hb + 64))
    return rows


def build():
    nc = bass.Bass("TRN2", target_bir_lowering=False)
    consts = _consts()
    P = Prog()

    def din(name, shape):
        return nc.dram_tensor(name, list(shape), F32, kind="ExternalInput").ap()

    def dout(name, shape):
        return nc.dram_tensor(name, list(shape), F32, kind="ExternalOutput").ap()

    d_xT = din("xT", [D, S])
    d_x = din("x", [S, D])
    d_wc = din("wc", [D, NPAIR * 640])
    d_wkv = din("wkv", [D, 1280])
    d_wo = din("wo", [D, D])
    d_wos = din("wos", [D, D])
    d_wins = din("wins", [D, 3328])
    d_lng = din("lng", [1, D])
    d_lnb = din("lnb", [1, D])
    d_sink = din("sink", [1, 8])
    d_sinkp = din("sinkp", [128, 4])
    d_xsT = din("xsT", [D, NS])
    d_xs = din("xs", [NS, D])
    d_ca = din("ca", [NS, 2048, 1024])
    d_cb = din("cb", [NS, 128, 256])
    dc = {k: din("c_" + k, v.shape) for k, v in consts.items()}
    o_y = dout("y", [S, D])
    o_pa = dout("pkva", [2048, 1024])
    o_pb = dout("pkvb", [128, 256])
    o_ys = dout("ys", [NS, D])
    o_sa = dout("skva", [NS, 1024])
    o_sb = dout("skvb", [NS, 256])

    import contextlib
    es = contextlib.ExitStack()

    def sb(name, shape, dt=F32):
        return es.enter_context(nc.sbuf_tensor(name, list(shape), dt))

    xT = sb("xT_sb", [128, 8, S], BF16)
    mixT = sb("mixT", [128, 8, S], BF16)
    ident = sb("ident", [128, 128], BF16)
    mask2 = sb("mask2", [128, 1024], BF16)
    sel2 = sb("sel2", [128, 128], F32)
    onescol = sb("onescol", [128, 1], F32)
    esink = sb("esink", [128, 4], F32)
    bar = sb("bar", [128, 2], F32)
    SCR = 76 * 1024
    scr = sb("scr", [128, SCR // 2], BF16)
    scr_off = [0]

    def carve(shape, dt, reset=False):
        if reset:
            scr_off[0] = 0
        n = int(np.prod(shape[1:]))
        esz = 2 if dt == BF16 else 4
        nbytes = n * esz
        off = (scr_off[0] + 3) // 4 * 4
        assert off + nbytes <= SCR, (off, nbytes)
        scr_off[0] = off + nbytes
        ap = scr[:, off // 2: (off + nbytes) // 2]
        if dt != BF16:
            ap = ap.bitcast(dt)
        if len(shape) > 2:
            names = " ".join("a%d" % i for i in range(len(shape) - 1))
            kw = {"a%d" % i: shape[i + 1] for i in range(len(shape) - 1)}
            ap = ap.rearrange("p (%s) -> p %s" % (names, names), **kw)
        return ap

    banks = [es.enter_context(nc.psum_tensor("ps%d" % i, [128, 512], F32)) for i in range(8)]
    R_bank = [Res() for _ in range(8)]
    sem_eng = {e: es.enter_context(nc.semaphore("s_" + e)) for e in Prog.ENG}
    sem_dma = [es.enter_context(nc.semaphore("s_dma%d" % i)) for i in range(NDMA_SEM)]

    out_tokens = []
    Res.default_w = None

    def phase_barrier(prev):
        tok = P.op("dve", lambda e: e.memset(bar[:, 0:1], 0.0), writes=list(prev))
        Res.default_w = tok

    def dma(eng, out, in_, reads=(), writes=(), **kw):
        return P.op(eng, lambda e: e.dma_start(out=out, in_=in_, **kw), reads, writes, dma=True)

    def strided(ap2, start, step, n):
        if step == 1:
            return ap2[:, start:start + n]
        v = ap2.rearrange("p (m r) -> p m r", r=step)
        return v[:, start // step: start // step + n, start % step]

    R_const = Res()
    R_xT = [Res() for _ in range(8)]
    dma("pool", ident[:], dc["ident_bf"][:, :], writes=[R_const])
    dma("pool", mask2[:], dc["mask2"][:, :], writes=[R_const])
    dma("sp", sel2[:], dc["sel2"][:, :], writes=[R_const])
    dma("sp", onescol[:], dc["onescol"][:, :], writes=[R_const])
    dma("sp", esink[:], d_sinkp[:, :], writes=[R_const])
    P.op("act", lambda e: e.activation(out=esink[:], in_=esink[:], func=AF.Exp), writes=[R_const])
    d_xT3 = d_xT.rearrange("(kc p) t -> p kc t", p=128)
    xT_next = [0]

    def load_xT(npieces=16):
        for _ in range(npieces):
            i = xT_next[0]
            if i >= 16:
                return
            xT_next[0] += 1
            kc, hh = i // 2, i % 2
            dma("pool", xT[:, kc, hh * 2048:(hh + 1) * 2048], d_xT3[:, kc, hh * 2048:(hh + 1) * 2048],
                writes=[R_xT[kc]], max_dma_last_dim=8192)

    def layer_norm_rows(np_, zs, stat, R_z, R_stat, ys, R_y, lng, lnb, R_ln, pool_affine=True):
        for hh in range(2):
            P.op("dve", lambda e, hh=hh: e.bn_stats(out=stat[0:np_, hh * 6:(hh + 1) * 6], in_=zs[0:np_, hh * 512:(hh + 1) * 512]),
                 reads=[R_z], writes=[R_stat])
        P.op("dve", lambda e: e.bn_aggr(out=stat[0:np_, 12:14], in_=stat[0:np_, 0:12]), writes=[R_stat])
        P.op("dve", lambda e: e.tensor_scalar(out=stat[0:np_, 14:15], in0=stat[0:np_, 13:14], scalar1=EPS, scalar2=None, op0=ALU.add),
             writes=[R_stat])
        P.op("act", lambda e: e.activation(out=stat[0:np_, 15:16], in_=stat[0:np_, 14:15], func=AF.Sqrt), writes=[R_stat])
        P.op("dve", lambda e: e.reciprocal(out=stat[0:np_, 16:17], in_=stat[0:np_, 15:16]), writes=[R_stat])
        P.op("dve", lambda e: e.tensor_scalar(out=stat[0:np_, 17:18], in0=stat[0:np_, 12:13], scalar1=stat[0:np_, 16:17], scalar2=-1.0,
                                              op0=ALU.mult, op1=ALU.mult), writes=[R_stat])
        P.op("act", lambda e: e.activation(out=zs[0:np_, :], in_=zs[0:np_, :], func=AF.Identity, scale=stat[0:np_, 16:17], bias=stat[0:np_, 17:18]),
             reads=[R_stat], writes=[R_z])
        eng = "pool" if pool_affine else "dve"
        P.op(eng, lambda e: e.tensor_tensor(out=zs[0:np_, :], in0=zs[0:np_, :], in1=lng[0:np_, :], op=ALU.mult), reads=[R_ln], writes=[R_z])
        P.op(eng, lambda e: e.tensor_tensor(out=ys[0:np_, :], in0=zs[0:np_, :], in1=lnb[0:np_, :], op=ALU.add), reads=[R_ln, R_z], writes=[R_y])

    def sample_phase():
        hs = carve([128, 3328], F32, reset=True)
        xsT = carve([128, 8, NS], BF16)
        wsb = [carve([128, 8, 512], BF16) for _ in range(2)]
        kvt = [carve([128, 1024], F32) for _ in range(2)]
        kvt += [wsb[1][:, 0:4, :].rearrange("p k f -> p (k f)").bitcast(F32), wsb[1][:, 4:8, :].rearrange("p k f -> p (k f)").bitcast(F32)]
        kvb = [carve([128, 256], F32) for _ in range(4)]
        prod = carve([128, 512], F32)
        s8 = [carve([128, 8], F32) for _ in range(2)]
        p8 = [carve([128, 8], BF16) for _ in range(2)]
        vbf = [carve([128, 512], BF16) for _ in range(2)]
        R_vbf = [Res(), Res()]
        qbf = carve([128, 1024], BF16)
        i16b = carve([128, 16], BF16)
        onesb = carve([128, 2], BF16)
        masked = [carve([128, 512], F32) for _ in range(2)]
        dd8 = [carve([128, 8], F32) for _ in range(2)]
        en = carve([128, NS * NS], F32)
        bd = carve([128, 512], F32)
        i8 = carve([128, 8], F32)
        i16 = carve([128, 16], F32)
        css = carve([128, 16], F32)
        tmp = [carve([128, 16, 8], F32) for _ in range(4)]
        small = carve([128, 64], F32)
        mixs = carve([128, D], F32)
        mixsT = carve([128, 8, NS], BF16)
        sink16 = carve([128, 8], F32)
        xs_sb = carve([128, D], F32)
        stat = carve([128, 32], F32)
        lng = carve([128, D], F32)
        lnb = carve([128, D], F32)
        R_ln = Res()
        R = {k: Res() for k in ("hs", "xsT", "c", "prod", "small", "mixs", "mixsT", "xs", "stat", "ys", "tmp")}
        R_wsb = [Res(), Res()]
        R_kvt = [Res() for _ in range(4)]
        R_kvb = [Res() for _ in range(4)]
        R_s8 = [Res(), Res()]
        R_p8 = [Res(), Res()]
        R_msk = [Res(), Res()]
        R_dd = [Res(), Res()]

        dma("pool", xsT[:], d_xsT.rearrange("(kc p) n -> p kc n", p=128), writes=[R["xsT"]])
        dma("sp", lng[0:NS, :], d_lng[0, :].partition_broadcast(NS), writes=[R_ln])
        dma("sp", lnb[0:NS, :], d_lnb[0, :].partition_broadcast(NS), writes=[R_ln])
        dma("sp", en[0:8, :], dc["en"][:, :], writes=[R["c"]])
        dma("sp", bd[0:8, :], dc["bd"][:, :], writes=[R["c"]])
        dma("sp", i8[0:8, :], dc["i8"][:, :], writes=[R["c"]])
        dma("sp", i16[0:16, :], dc["i16"][:, :], writes=[R["c"]])
        dma("pool", i16b[0:16, :], dc["i16"][:, :], writes=[R["c"]])
        dma("pool", onesb[:, 0:1], dc["onescol"][:, :], writes=[R["c"]])
        dma("sp", css[0:NS, :], dc["cs_s"][:, :], writes=[R["c"]])
        dma("sp", sink16[0:NS, :], d_sink[0, :].partition_broadcast(NS), writes=[R["c"]])
        dma("sp", xs_sb[0:NS, :], d_xs[:, :], writes=[R["xs"]])
        P.op("act", lambda e: e.activation(out=sink16[0:NS, :], in_=sink16[0:NS, :], func=AF.Exp), writes=[R["c"]])

        d_w3 = d_wins.rearrange("(kc p) f -> p kc f", p=128)
        chunks = [(0, 512), (512, 512), (1024, 512), (1536, 512), (2048, 512), (2560, 512), (3072, 256)]
        for ci, (c0, cw) in enumerate(chunks):
            wb = wsb[ci % 2]
            dma("pool", wb[:, :, 0:cw], d_w3[:, :, c0:c0 + cw], writes=[R_wsb[ci % 2]])
            bk = ci % 2
            for kc in range(8):
                P.op("pe", lambda e, kc=kc, wb=wb, cw=cw, bk=bk: e.matmul(banks[bk][0:NS, 0:cw], lhsT=xsT[:, kc, :], rhs=wb[:, kc, 0:cw],
                                                                         start=(kc == 0), stop=(kc == 7)),
                     reads=[R["xsT"], R_wsb[ci % 2]], writes=[R_bank[bk]])
            P.op("act", lambda e, c0=c0, cw=cw, bk=bk: e.copy(out=hs[0:NS, c0:c0 + cw], in_=banks[bk][0:NS, 0:cw]),
                 reads=[R_bank[bk]], writes=[R["hs"]])

        def rope_tm(np_, view, nh, cosb, sinb):
            x1 = view[:, :, 0:8]
            x2 = view[:, :, 8:16]
            t = [tt[0:np_, 0:nh, :] for tt in tmp]
            P.op("dve", lambda e: e.tensor_tensor(out=t[0], in0=x1, in1=cosb, op=ALU.mult), reads=[R["hs"], R["c"]], writes=[R["tmp"]])
            P.op("dve", lambda e: e.tensor_tensor(out=t[1], in0=x2, in1=sinb, op=ALU.mult), reads=[R["hs"]], writes=[R["tmp"]])
            P.op("dve", lambda e: e.tensor_tensor(out=t[2], in0=x2, in1=cosb, op=ALU.mult), reads=[R["hs"]], writes=[R["tmp"]])
            P.op("dve", lambda e: e.tensor_tensor(out=t[3], in0=x1, in1=sinb, op=ALU.mult), reads=[R["hs"]], writes=[R["tmp"]])
            P.op("dve", lambda e: e.tensor_tensor(out=x1, in0=t[0], in1=t[1], op=ALU.subtract), reads=[R["tmp"]], writes=[R["hs"]])
            P.op("dve", lambda e: e.tensor_tensor(out=x2, in0=t[2], in1=t[3], op=ALU.add), reads=[R["tmp"]], writes=[R["hs"]])

        for (c0, nh) in ((0, 16), (2048, 8), (2560, 2)):
            view = hs[0:NS, c0:c0 + nh * 64].rearrange("p (h d) -> p h d", d=64)
            cosb = css[0:NS, 0:8].unsqueeze(1).broadcast_to([NS, nh, 8])
            sinb = css[0:NS, 8:16].unsqueeze(1).broadcast_to([NS, nh, 8])
            rope_tm(NS, view, nh, cosb, sinb)
        R["qbf"] = Res()
        P.op("act", lambda e: e.copy(out=qbf[0:NS, 0:512], in_=hs[0:NS, 0:512]), reads=[R["hs"]], writes=[R["qbf"]])
        P.op("act", lambda e: e.copy(out=qbf[0:NS, 512:1024], in_=hs[0:NS, 2048:2560]), reads=[R["hs"]], writes=[R["qbf"]])
        out_tokens.append(dma("sp", o_sa[:, :], hs[0:NS, 512:1536], reads=[R["hs"]]))
        out_tokens.append(dma("sp", o_sb[:, :], hs[0:NS, 2560:2816], reads=[R["hs"]]))

        B_QBC, B_OUT, B_DEN, B_ACC, B_ACD = 2, 3, 4, 5, 6
        it = 0
        for mixer in ("A", "B"):
            qoff = 0 if mixer == "A" else 2048
            QB = (2, 7)

            def emit_qbc(n, qoff=qoff):
                bq = QB[n % 2]
                qo2 = 0 if qoff == 0 else 512
                P.op("pe", lambda e, n=n, qo2=qo2, bq=bq: e.matmul(banks[bq][:, :], lhsT=i16b[0:NS, n:n + 1].broadcast_to([NS, 128]),
                                                                   rhs=qbf[0:NS, qo2:qo2 + 512], start=True, stop=True),
                     reads=[R["qbf"], R["c"]], writes=[R_bank[bq]])
            emit_qbc(0)
            for n in range(NS):
                B_QBC = QB[n % 2]
                dl = DILS if mixer == "A" else (1,)
                for di, d in enumerate(dl):
                    i2 = it % 2
                    if mixer == "A":
                        kb_ = it % 4
                        t_kv = kvt[kb_]
                        rk = R_kvt[kb_]
                        src = d_ca[n, 2048 - 128 * d:2048, :].rearrange("(j r) f -> j r f", r=d)[:, 0, :]
                        dma("sp", t_kv[:], src, writes=[rk] + ([R_wsb[1]] if (kb_ >= 2 and it < 4) else []))
                        kk = t_kv[:, 0:512]
                        vv = t_kv[:, 512:1024]
                        qq = banks[B_QBC][:, :]
                        nv = 512
                        P.op("dve", lambda e, kk=kk, qq=qq: e.tensor_tensor(out=prod[:], in0=qq, in1=kk, op=ALU.mult),
                             reads=[rk, R_bank[B_QBC]], writes=[R["prod"]])
                    else:
                        t_kv = kvb[it % 4]
                        rk = R_kvb[it % 4]
                        dma("sp", t_kv[:], d_cb[n, :, :], writes=[rk])
                        kk = t_kv[:, 0:128].rearrange("p (g d) -> p g d", g=2).unsqueeze(2).broadcast_to([128, 2, 4, 64])
                        vv = t_kv[:, 128:256]
                        qq = banks[B_QBC][:, :].rearrange("p (g u d) -> p g u d", g=2, u=4)
                        nv = 128
                        P.op("dve", lambda e, kk=kk, qq=qq: e.tensor_tensor(out=prod[:].rearrange("p (g u d) -> p g u d", g=2, u=4),
                                                                            in0=qq, in1=kk, op=ALU.mult),
                             reads=[rk, R_bank[B_QBC]], writes=[R["prod"]])
                    P.op("dve", lambda e, i2=i2: e.tensor_reduce(out=s8[i2][:], in_=prod[:].rearrange("p (h d) -> p h d", d=64),
                                                                 axis=AX.X, op=ALU.add),
                         reads=[R["prod"]], writes=[R_s8[i2]])
                    P.op("act", lambda e, i2=i2: e.activation(out=p8[i2][:], in_=s8[i2][:], func=AF.Exp, scale=0.125),
                         reads=[R_s8[i2]], writes=[R_p8[i2]])
                    first = (di == 0)
                    last = (di == len(dl) - 1)
                    P.op("act", lambda e, i2=i2, vv=vv, nv=nv: e.copy(out=vbf[i2][:, 0:nv], in_=vv), reads=[rk], writes=[R_vbf[i2]])
                    P.op("pe", lambda e, i2=i2, nv=nv, first=first, last=last: e.matmul(
                        banks[B_OUT][0:8, 0:nv], lhsT=p8[i2][:], rhs=vbf[i2][:, 0:nv], start=first, stop=last),
                        reads=[R_p8[i2], R_vbf[i2]], writes=[R_bank[B_OUT]])
                    P.op("pe", lambda e, i2=i2, first=first, last=last: e.matmul(
                        banks[B_DEN][0:8, 0:1], lhsT=p8[i2][:], rhs=onesb[:, 0:1], start=first, stop=last),
                        reads=[R_p8[i2], R["c"]], writes=[R_bank[B_DEN]])
                    it += 1
                if n + 1 < NS:
                    emit_qbc(n + 1)
                if mixer == "A":
                    load_xT(1)
                m2 = n % 2
                if mixer == "A":
                    P.op("dve", lambda e, m2=m2: e.tensor_tensor(out=masked[m2][0:8, :], in0=banks[B_OUT][0:8, :], in1=bd[0:8, :], op=ALU.mult),
                         reads=[R_bank[B_OUT], R["c"]], writes=[R_msk[m2]])
                else:
                    src = banks[B_OUT][0:8, 0:128].rearrange("p (g d) -> p g d", g=2).unsqueeze(2).broadcast_to([8, 2, 4, 64])
                    P.op("dve", lambda e, m2=m2, src=src: e.tensor_tensor(out=masked[m2][0:8, :].rearrange("p (g u d) -> p g u d", g=2, u=4),
                                                                          in0=src, in1=bd[0:8, :].rearrange("p (g u d) -> p g u d", g=2, u=4),
                                                                          op=ALU.mult),
                         reads=[R_bank[B_OUT], R["c"]], writes=[R_msk[m2]])
                P.op("dve", lambda e, m2=m2: e.tensor_scalar(out=dd8[m2][0:8, :], in0=i8[0:8, :], scalar1=banks[B_DEN][0:8, 0:1], scalar2=None,
                                                            op0=ALU.mult),
                     reads=[R_bank[B_DEN], R["c"]], writes=[R_dd[m2]])
                P.op("pe", lambda e, n=n, m2=m2: e.matmul(banks[B_ACC][0:NS, :], lhsT=en[0:8, n * NS:(n + 1) * NS], rhs=masked[m2][0:8, :],
                                                          start=(n == 0), stop=(n == NS - 1)),
                     reads=[R_msk[m2], R["c"]], writes=[R_bank[B_ACC]])
                P.op("pe", lambda e, n=n, m2=m2: e.matmul(banks[B_ACD][0:NS, 0:8], lhsT=en[0:8, n * NS:(n + 1) * NS], rhs=dd8[m2][0:8, :],
                                                          start=(n == 0), stop=(n == NS - 1)),
                     reads=[R_dd[m2], R["c"]], writes=[R_bank[B_ACD]])
            if mixer == "A":
                qv = hs[0:NS, 0:512]
                kv_ = hs[0:NS, 512:1024]
                vnew = hs[0:NS, 1024:1536].rearrange("p (h d) -> p h d", d=64)
                gcol, mcol, wnew = 1536, 0, 3.0
                P.op("dve", lambda e: e.tensor_tensor(out=prod[0:NS, :], in0=qv, in1=kv_, op=ALU.mult), reads=[R["hs"]], writes=[R["prod"]])
            else:
                qv = hs[0:NS, 2048:2560].rearrange("p (g u d) -> p g u d", g=2, u=4)
                kv_ = hs[0:NS, 2560:2688].rearrange("p (g d) -> p g d", g=2).unsqueeze(2).broadcast_to([NS, 2, 4, 64])
                vnew = hs[0:NS, 2688:2816].rearrange("p (g d) -> p g d", g=2).unsqueeze(2).broadcast_to([NS, 2, 4, 64])
                gcol, mcol, wnew = 2816, 512, 1.0
                P.op("dve", lambda e: e.tensor_tensor(out=prod[0:NS, :].rearrange("p (g u d) -> p g u d", g=2, u=4), in0=qv, in1=kv_, op=ALU.mult),
                     reads=[R["hs"]], writes=[R["prod"]])
            sn = small[0:NS, 0:8]
            pn = small[0:NS, 8:16]
            dn = small[0:NS, 16:24]
            rn = small[0:NS, 24:32]
            P.op("dve", lambda e: e.tensor_reduce(out=sn, in_=prod[0:NS, :].rearrange("p (h d) -> p h d", d=64), axis=AX.X, op=ALU.add),
                 reads=[R["prod"]], writes=[R["small"]])
            P.op("act", lambda e: e.activation(out=pn, in_=sn, func=AF.Exp, scale=0.125), writes=[R["small"]])
            P.op("dve", lambda e, wnew=wnew: e.scalar_tensor_tensor(out=dn, in0=pn, scalar=wnew, in1=banks[B_ACD][0:NS, 0:8],
                                                                   op0=ALU.mult, op1=ALU.add),
                 reads=[R_bank[B_ACD]], writes=[R["small"]])
            if mixer == "B":
                P.op("dve", lambda e: e.tensor_tensor(out=dn, in0=dn, in1=sink16[0:NS, :], op=ALU.add), reads=[R["c"]], writes=[R["small"]])
            P.op("dve", lambda e: e.reciprocal(out=rn, in_=dn), writes=[R["small"]])
            pnb = pn.unsqueeze(2).broadcast_to([NS, 8, 64])
            rnb = rn.unsqueeze(2).broadcast_to([NS, 8, 64])
            if mixer == "A":
                pv3 = prod[0:NS, :].rearrange("p (h d) -> p h d", d=64)
                P.op("dve", lambda e: e.tensor_tensor(out=pv3, in0=vnew, in1=pnb, op=ALU.mult), reads=[R["hs"], R["small"]], writes=[R["prod"]])
            else:
                pv4 = prod[0:NS, :].rearrange("p (g u d) -> p g u d", g=2, u=4)
                pnb4 = pn.rearrange("p (g u) -> p g u", g=2).unsqueeze(3).broadcast_to([NS, 2, 4, 64])
                P.op("dve", lambda e: e.tensor_tensor(out=pv4, in0=vnew, in1=pnb4, op=ALU.mult), reads=[R["hs"], R["small"]], writes=[R["prod"]])
            P.op("dve", lambda e, wnew=wnew: e.scalar_tensor_tensor(out=prod[0:NS, :], in0=prod[0:NS, :], scalar=wnew, in1=banks[B_ACC][0:NS, :],
                                                                   op0=ALU.mult, op1=ALU.add),
                 reads=[R_bank[B_ACC]], writes=[R["prod"]])
            P.op("dve", lambda e: e.tensor_tensor(out=prod[0:NS, :].rearrange("p (h d) -> p h d", d=64),
                                                  in0=prod[0:NS, :].rearrange("p (h d) -> p h d", d=64), in1=rnb, op=ALU.mult),
                 reads=[R["small"]], writes=[R["prod"]])
            P.op("act", lambda e, gcol=gcol: e.activation(out=hs[0:NS, gcol:gcol + 512], in_=hs[0:NS, gcol:gcol + 512], func=AF.Silu),
                 writes=[R["hs"]])
            P.op("dve", lambda e, gcol=gcol, mcol=mcol: e.tensor_tensor(out=mixs[0:NS, mcol:mcol + 512], in0=prod[0:NS, :],
                                                                       in1=hs[0:NS, gcol:gcol + 512], op=ALU.mult),
                 reads=[R["prod"], R["hs"]], writes=[R["mixs"]])
        wos = [carve([128, 4, D], BF16, reset=(i == 0)) for i in range(2)]
        R_wos = [Res(), Res()]
        d_wos3 = d_wos.rearrange("(kc p) f -> p kc f", p=128)
        dep_all = [R["hs"], R["xsT"], R_wsb[0], R_wsb[1], R["mixs"]]
        for i in range(2):
            dma("pool", wos[i][:], d_wos3[:, i * 4:(i + 1) * 4, :], reads=[], writes=[R_wos[i]] + (dep_all if i == 0 else []))
        for c in range(8):
            P.op("pe", lambda e, c=c: e.transpose(out=banks[2][:, c * NS:(c + 1) * NS], in_=mixs[0:NS, c * 128:(c + 1) * 128], identity=i16[0:NS, 0:NS]),
                 reads=[R["mixs"], R["c"]], writes=[R_bank[2]])
        P.op("dve", lambda e: e.tensor_copy(out=mixsT[:].rearrange("p c n -> p (c n)"), in_=banks[2][:, 0:8 * NS]),
             reads=[R_bank[2]], writes=[R["mixsT"]])
        for hh in range(2):
            for c in range(8):
                P.op("pe", lambda e, c=c, hh=hh: e.matmul(banks[hh][0:NS, :], lhsT=mixsT[:, c, :], rhs=wos[c // 4][:, c % 4, hh * 512:(hh + 1) * 512],
                                                          start=(c == 0), stop=(c == 7)),
                     reads=[R["mixsT"], R_wos[c // 4]], writes=[R_bank[hh]])
            P.op("dve", lambda e, hh=hh: e.scalar_tensor_tensor(out=mixs[0:NS, hh * 512:(hh + 1) * 512], in0=xs_sb[0:NS, hh * 512:(hh + 1) * 512],
                                                               scalar=ALPHA, in1=banks[hh][0:NS, :], op0=ALU.mult, op1=ALU.add),
                 reads=[R_bank[hh], R["xs"], R["mixsT"]], writes=[R["mixs"]])
        layer_norm_rows(NS, mixs, stat, R["mixs"], R["stat"], xs_sb, R["ys"], lng, lnb, R_ln)
        out_tokens.append(dma("sp", o_ys[:, :], xs_sb[0:NS, :], reads=[R["ys"]]))
        return list(R.values()) + R_wos + R_kvt + R_kvb + R_wsb + R_s8 + R_p8 + R_msk + R_dd + [R_ln] + R_vbf

    def kvout_phase(prev):
        wkv = carve([128, 8, 1280], BF16, reset=True)
        kvo = [carve([128, 1024], F32) for _ in range(2)]
        cosT = carve([128, 16, 8], F32)
        sinT = carve([128, 16, 8], F32)
        cosTb = carve([128, 8], F32)
        sinTb = carve([128, 8], F32)
        tmp = [carve([128, 8, 8], F32) for _ in range(4)]
        R_w = Res()
        R_c = Res()
        R_kvo = [Res(), Res()]
        R_tmp = Res()
        dma("pool", wkv[:], d_wkv.rearrange("(kc p) f -> p kc f", p=128), writes=[R_w] + prev)
        dma("sp", cosT[:], dc["cosT"][:, :, :], writes=[R_c])
        dma("sp", sinT[:], dc["sinT"][:, :, :], writes=[R_c])
        dma("sp", cosTb[:], dc["cosTb"][:, :], writes=[R_c])
        dma("sp", sinTb[:], dc["sinTb"][:, :], writes=[R_c])
        for blk in range(17):
            isb = (blk == 16)
            t0 = (S - 128) if isb else (2048 + blk * 128)
            nh = 2 if isb else 8
            wcol = 1024 if isb else 0
            kw = nh * 64
            ob = kvo[blk % 2]
            rk = R_kvo[blk % 2]
            kb0 = 2 * (blk % 2)
            for part in range(2):
                bk = kb0 + part
                for kc in range(8):
                    P.op("pe", lambda e, kc=kc, bk=bk, t0=t0, wcol=wcol, kw=kw, part=part: e.matmul(
                        banks[bk][:, 0:kw], lhsT=xT[:, kc, t0:t0 + 128], rhs=wkv[:, kc, wcol + part * kw: wcol + (part + 1) * kw],
                        start=(kc == 0), stop=(kc == 7)),
                        reads=[R_xT[kc], R_w], writes=[R_bank[bk]])
            P.op("dve", lambda e, ob=ob, kw=kw, kb0=kb0: e.tensor_copy(out=ob[:, 0:kw], in_=banks[kb0][:, 0:kw]), reads=[R_bank[kb0]], writes=[rk])
            P.op("act", lambda e, ob=ob, kw=kw, kb0=kb0: e.copy(out=ob[:, kw:2 * kw], in_=banks[kb0 + 1][:, 0:kw]), reads=[R_bank[kb0 + 1]], writes=[rk])
            kps = banks[kb0][:, 0:kw].rearrange("p (h d) -> p h d", d=64)
            x1 = kps[:, :, 0:8]
            x2 = kps[:, :, 8:16]
            if isb:
                cb_ = cosTb[:, :].unsqueeze(1).broadcast_to([128, nh, 8])
                sb_ = sinTb[:, :].unsqueeze(1).broadcast_to([128, nh, 8])
            else:
                cb_ = cosT[:, blk, :].unsqueeze(1).broadcast_to([128, nh, 8])
                sb_ = sinT[:, blk, :].unsqueeze(1).broadcast_to([128, nh, 8])
            t = [tt[:, 0:nh, :] for tt in tmp]
            P.op("dve", lambda e, t=t, x1=x1, cb_=cb_: e.tensor_tensor(out=t[0], in0=x1, in1=cb_, op=ALU.mult), reads=[R_bank[kb0], R_c], writes=[R_tmp])
            P.op("dve", lambda e, t=t, x2=x2, sb_=sb_: e.tensor_tensor(out=t[1], in0=x2, in1=sb_, op=ALU.mult), reads=[R_bank[kb0]], writes=[R_tmp])
            P.op("dve", lambda e, t=t, x2=x2, cb_=cb_: e.tensor_tensor(out=t[2], in0=x2, in1=cb_, op=ALU.mult), reads=[R_bank[kb0]], writes=[R_tmp])
            P.op("dve", lambda e, t=t, x1=x1, sb_=sb_: e.tensor_tensor(out=t[3], in0=x1, in1=sb_, op=ALU.mult), reads=[R_bank[kb0]], writes=[R_tmp])
            ov = ob[:, 0:kw].rearrange("p (h d) -> p h d", d=64)
            P.op("dve", lambda e, t=t, ov=ov: e.tensor_tensor(out=ov[:, :, 0:8], in0=t[0], in1=t[1], op=ALU.subtract), reads=[R_tmp], writes=[rk])
            P.op("dve", lambda e, t=t, ov=ov: e.tensor_tensor(out=ov[:, :, 8:16], in0=t[2], in1=t[3], op=ALU.add), reads=[R_tmp], writes=[rk])
            if isb:
                out_tokens.append(dma("sp", o_pb[:, :], ob[:, 0:256], reads=[rk]))
            else:
                out_tokens.append(dma("sp", o_pa[blk * 128:(blk + 1) * 128, :], ob[:, :], reads=[rk]))
        return [R_w, R_c, R_tmp] + R_kvo

    def pair_phase(prev, npair=NPAIR):
        QT = carve([128, S], BF16, reset=True)
        KT = carve([128, S], BF16)
        VT = carve([128, S], BF16)
        Vd = carve([128, 32, 2, 66], BF16)
        acc = carve([128, 2, 2048], F32)
        wbuf0 = carve([128, 8, 640], BF16)
        wbuf1 = mixT[:, 6:8, :].rearrange("p c t -> p (c t)")[:, 3072:3072 + 5120].rearrange("p (k f) -> p k f", k=8)
        pT = [carve([128, 1024], BF16) for _ in range(2)]
        cct = [carve([128, 512], F32)] * 2
        sst = [carve([128, 512], F32)] * 2
        t1 = carve([128, 512], F32)
        t2 = carve([128, 512], F32)
        rot = [carve([128, 512], BF16)] * 2
        pT = pT + [t1.bitcast(BF16)]
        mtmp = [carve([128, 512], F32) for _ in range(2)]
        R_QT = [Res() for _ in range(8)]
        R_KT = [Res() for _ in range(8)]
        R_VT = [Res() for _ in range(8)]
        R_Vd = Res()
        R_acc = [Res(), Res()]
        R_w = [Res(), Res()]
        R_pT = [Res(), Res()]
        R_cs = [Res()] * 2
        R_t = Res()
        R_pT = R_pT + [R_t]
        R_rot = [Res()] * 2
        R_n1 = Res()
        R_m = [Res(), Res()]
        R_mix = [[Res() for _ in range(8)] for _ in range(8)]
        d_wc3 = d_wc.rearrange("(kc p) f -> p kc f", p=128)

        def wbuf_of(jj):
            return (wbuf1, R_w[1]) if (jj % 2 == 1 and jj <= 5) else (wbuf0, R_w[0])

        def load_w(jj):
            wb_, rw_ = wbuf_of(jj)
            dma("pool", wb_, d_wc3[:, :, jj * 640:(jj + 1) * 640], writes=[rw_])
        load_w(0)
        for (hh_, cc_, val_) in ((0, 64, 1.0), (0, 65, 0.0), (1, 64, 0.0), (1, 65, 1.0)):
            P.op("pool", lambda e, hh_=hh_, cc_=cc_, val_=val_: e.memset(Vd[:, :, hh_, cc_:cc_ + 1], val_), writes=[R_Vd])

        pcount = [0]
        scount = [0]
        u_started = set()
        PB = (0, 1)
        SBK = ((2, 3), (0, 1))
        UB = ((4, 5), (6, 7))

        for j in range(npair):
            isB = j >= 4
            has_kv = (not isB) or j == 4
            wb, rw = wbuf_of(j)
            early = (j + 1 < npair) and (wbuf_of(j + 1)[0] is not wb)
            if early:
                load_w(j + 1)
            for tile in range(8):
                tsl = slice(tile * 512, (tile + 1) * 512)
                ci2 = tile % 2
                dma("sp", cct[ci2][0:64, :], dc["CC"][:, tsl], writes=[R_cs[ci2]])
                dma("sp", sst[ci2][0:64, :], dc["SS"][:, tsl], writes=[R_cs[ci2]])
                for ci, kind in enumerate("QKVGR"):
                    if kind in "KV" and not has_kv:
                        continue
                    bk = (0, 1, 2, 3)[pcount[0] % 4]
                    pcount[0] += 1
                    for kc in range(8):
                        P.op("pe", lambda e, kc=kc, bk=bk, ci=ci, wb=wb, tsl=tsl: e.matmul(
                            banks[bk][:, :], lhsT=wb[:, kc, ci * 128:(ci + 1) * 128], rhs=xT[:, kc, tsl], start=(kc == 0), stop=(kc == 7)),
                            reads=[R_xT[kc], rw], writes=[R_bank[bk]])
                    if kind == "Q":
                        P.op("act", lambda e, bk=bk, tsl=tsl: e.copy(out=QT[:, tsl], in_=banks[bk][:, :]), reads=[R_bank[bk]], writes=[R_QT[tile]])
                    elif kind == "K":
                        P.op("dve", lambda e, bk=bk, tsl=tsl: e.tensor_copy(out=KT[:, tsl], in_=banks[bk][:, :]), reads=[R_bank[bk]], writes=[R_KT[tile]])
                    elif kind == "V":
                        P.op("act", lambda e, bk=bk, tsl=tsl: e.copy(out=VT[:, tsl], in_=banks[bk][:, :]), reads=[R_bank[bk]], writes=[R_VT[tile]])
                    elif kind == "G":
                        P.op("act", lambda e, bk=bk, tsl=tsl, j=j: e.activation(out=mixT[:, j, tsl], in_=banks[bk][:, :], func=AF.Silu),
                             reads=[R_bank[bk]], writes=[R_mix[j][tile]])
                    else:
                        P.op("dve", lambda e, bk=bk, ci2=ci2: e.tensor_tensor(out=t1[0:64, :], in0=banks[bk][0:64, :], in1=cct[ci2][0:64, :], op=ALU.mult),
                             reads=[R_bank[bk], R_cs[ci2]], writes=[R_t])
                        P.op("dve", lambda e, bk=bk, ci2=ci2: e.tensor_tensor(out=t2[0:64, :], in0=banks[bk][64:128, :], in1=sst[ci2][0:64, :], op=ALU.mult),
                             reads=[R_bank[bk], R_cs[ci2]], writes=[R_t])
                        P.op("dve", lambda e, ci2=ci2: e.tensor_tensor(out=rot[ci2][0:64, :], in0=t1[0:64, :], in1=t2[0:64, :], op=ALU.add),
                             reads=[R_t], writes=[R_rot[ci2]])
                        dma("sp", QT[0:16, tsl], rot[ci2][0:16, :], reads=[R_rot[ci2]], writes=[R_QT[tile]])
                        dma("sp", QT[64:80, tsl], rot[ci2][16:32, :], reads=[R_rot[ci2]], writes=[R_QT[tile]])
                        if has_kv:
                            dma("sp", KT[0:16, tsl], rot[ci2][32:48, :], reads=[R_rot[ci2]], writes=[R_KT[tile]])
                            dma("sp", KT[64:80, tsl], rot[ci2][48:64, :], reads=[R_rot[ci2]], writes=[R_KT[tile]])
            if j + 1 < npair and not early:
                load_w(j + 1)
            KSTOP = int(os.environ.get("KSTOP", "9"))
            if KSTOP <= 1:
                continue
            dils = (1,) if isB else DILS
            for H in range(2):
                for di, d in enumerate(dils):
                    nb = 32 // d
                    hb = nb // 2
                    build_vd = (not isB) or (j == 4 and H == 0)
                    need = [sl for sl in range(32) if (isB or H == 1 or (sl % nb) < hb)]
                    grp8 = [need[q:q + 4] for q in range(0, len(need), 4)]
                    for g8, slots4 in enumerate(grp8 if build_vd else []):
                        bk = PB[pcount[0] % 2]
                        pcount[0] += 1
                        bfv = banks[bk][:, :].bitcast(BF16)
                        for i, slot in enumerate(slots4):
                            r, b = slot // nb, slot % nb
                            t0 = d * 128 * b + r
                            P.op("pe", lambda e, bfv=bfv, i=i, t0=t0, d=d: e.transpose(out=bfv[:, i * 128:(i + 1) * 128],
                                                                                 in_=strided(VT[:, :], t0, d, 128), identity=ident[:, :]),
                                 reads=R_VT + [R_const], writes=[R_bank[bk]])
                        eng = "act" if g8 % 2 == 0 else "dve"
                        src = bfv[:, 0:512].rearrange("p (s h d) -> p s h d", s=4, h=2)
                        if slots4[3] - slots4[0] == 3:
                            dst = Vd[:, slots4[0]:slots4[0] + 4, :, 0:64]
                        else:
                            assert [x - slots4[0] for x in slots4] == [0, 2, 4, 6], slots4
                            dst = Vd[:, slots4[0]:slots4[0] + 8, :, 0:64].rearrange("p (s two) h d -> p s two h d", two=2)[:, :, 0, :, :]
                        if eng == "act":
                            P.op("act", lambda e, src=src, dst=dst: e.copy(out=dst, in_=src), reads=[R_bank[bk]], writes=[R_Vd])
                        else:
                            P.op("dve", lambda e, src=src, dst=dst: e.tensor_copy(out=dst, in_=src), reads=[R_bank[bk]], writes=[R_Vd])
                    if KSTOP <= 2:
                        continue
                    subs = []
                    for r in range(d):
                        b_lo, b_hi = H * hb, (H + 1) * hb
                        kbs = list(range(b_lo, b_hi))
                        if H == 1:
                            kbs = [b_lo - 1] + kbs
                        for kb in kbs:
                            has_cur = kb >= b_lo
                            has_next = (kb + 1 < b_hi)
                            if not (has_cur or has_next):
                                continue
                            qb0 = kb if has_cur else kb + 1
                            subs.append(dict(r=r, kb=kb, has_cur=has_cur, has_next=has_next,
                                             nq=128 * (int(has_cur) + int(has_next)), c0=0 if has_cur else 128,
                                             kt0=d * 128 * kb + r, qt0=d * 128 * qb0 + r,
                                             slot=r * nb + kb, qi=r * hb + (kb - b_lo)))
                    groups = [subs[m0:m0 + 2] for m0 in range(0, len(subs), 2)]
                    gsidx = []
                    for _g in groups:
                        gsidx.append((scount[0] % 2, scount[0] % 3))
                        scount[0] += 1

                    def stage_a(grp, sidx2):
                        sbk = SBK[sidx2[0]]
                        sidx = sidx2[1]
                        full = (len(grp) == 2 and all(u["nq"] == 256 for u in grp))
                        for i, u in enumerate(grp):
                            for h in range(2):
                                P.op("pe", lambda e, sbk=sbk, h=h, i=i, u=u, d=d: e.matmul(
                                    banks[sbk[h]][:, i * 256 + u["c0"]: i * 256 + u["c0"] + u["nq"]],
                                    lhsT=strided(KT[h * 64:(h + 1) * 64, :], u["kt0"], d, 128),
                                    rhs=strided(QT[h * 64:(h + 1) * 64, :], u["qt0"], d, u["nq"]), start=True, stop=True),
                                    reads=R_QT + R_KT, writes=[R_bank[sbk[h]]])
                        if full:
                            for h in range(2):
                                P.op("act", lambda e, sbk=sbk, h=h, sidx=sidx: e.activation(
                                    out=pT[sidx][:, h * 512:(h + 1) * 512], in_=banks[sbk[h]][:, :], func=AF.Exp, scale=0.125),
                                    reads=[R_bank[sbk[h]]], writes=[R_pT[sidx]])
                            P.op("dve", lambda e, sidx=sidx: e.tensor_tensor(out=pT[sidx][:, :], in0=pT[sidx][:, :], in1=mask2[:, :], op=ALU.mult),
                                 reads=[R_const], writes=[R_pT[sidx]])
                        else:
                            for i, u in enumerate(grp):
                                lo = i * 256 + u["c0"]
                                for h in range(2):
                                    P.op("act", lambda e, sbk=sbk, h=h, lo=lo, u=u, sidx=sidx: e.activation(
                                        out=pT[sidx][:, h * 512 + lo: h * 512 + lo + u["nq"]], in_=banks[sbk[h]][:, lo:lo + u["nq"]],
                                        func=AF.Exp, scale=0.125),
                                        reads=[R_bank[sbk[h]]], writes=[R_pT[sidx]])
                            for i, u in enumerate(grp):
                                lo = i * 256 + u["c0"]
                                pv = pT[sidx][:, :].rearrange("p (h q) -> p h q", h=2)[:, :, lo:lo + u["nq"]]
                                mv = mask2[:, :].rearrange("p (h q) -> p h q", h=2)[:, :, lo:lo + u["nq"]]
                                P.op("dve", lambda e, pv=pv, mv=mv: e.tensor_tensor(out=pv, in0=pv, in1=mv, op=ALU.mult),
                                     reads=[R_const], writes=[R_pT[sidx]])

                    def stage_b(grp, sidx2):
                        sidx = sidx2[1]
                        for i, u in enumerate(grp):
                            qi, slot, kb = u["qi"], u["slot"], u["kb"]
                            for h in range(2):
                                base = h * 512 + i * 256
                                pieces = []
                                if u["has_cur"] and u["has_next"] and qi % 4 != 3:
                                    pieces.append((qi, 2, 0))
                                else:
                                    if u["has_cur"]:
                                        pieces.append((qi, 1, 0))
                                    if u["has_next"]:
                                        pieces.append((qi + 1, 1, 128))
                                for (q0, nqb, poff) in pieces:
                                    ub = UB[h][(q0 // 4) % 2]
                                    key = (j, H, d, h, q0 // 4)
                                    fresh = key not in u_started
                                    u_started.add(key)
                                    P.op("pe", lambda e, ub=ub, q0=q0, nqb=nqb, poff=poff, slot=slot, h=h, sidx=sidx, base=base, fresh=fresh: e.matmul(
                                        banks[ub][0:66, (q0 % 4) * 128:(q0 % 4) * 128 + 128 * nqb], lhsT=Vd[:, slot, h, :],
                                        rhs=pT[sidx][:, base + poff: base + poff + 128 * nqb], start=fresh, stop=True, skip_group_check=True),
                                        reads=[R_Vd, R_pT[sidx]], writes=[R_bank[ub]])
                            if u["has_cur"] and qi % 4 == 3:
                                g = qi // 4
                                for h in range(2):
                                    ub = UB[h][g % 2]
                                    a2 = acc[0:66, h, :]
                                    if d == 1:
                                        dst = a2[:, g * 512:(g + 1) * 512]
                                        src = banks[ub][0:66, :]
                                    elif d == 4:
                                        dst = a2.rearrange("p (m r) -> p m r", r=4)[:, :, g]
                                        src = banks[ub][0:66, :]
                                    else:
                                        dst = a2.rearrange("p (m r) -> p r m", r=16)[:, g * 4:(g + 1) * 4, :]
                                        src = banks[ub][0:66, :].rearrange("p (r m) -> p r m", r=4)
                                    if di == 0:
                                        P.op("act", lambda e, dst=dst, src=src: e.copy(out=dst, in_=src), reads=[R_bank[ub]], writes=[R_acc[h]])
                                    else:
                                        P.op("dve", lambda e, dst=dst, src=src: e.tensor_tensor(out=dst, in0=src, in1=dst, op=ALU.add),
                                             reads=[R_bank[ub]], writes=[R_acc[h]])

                    for gi in range(min(2, len(groups))):
                        stage_a(groups[gi], gsidx[gi])
                    for gi in range(len(groups)):
                        if gi + 2 < len(groups):
                            stage_a(groups[gi + 2], gsidx[gi + 2])
                        stage_b(groups[gi], gsidx[gi])
                for tq in range(4 if KSTOP > 3 else 0):
                    bk = PB[pcount[0] % 2]
                    pcount[0] += 1
                    tile = H * 4 + tq
                    gsl = slice(tile * 512, (tile + 1) * 512)
                    lsl = slice(tq * 512, (tq + 1) * 512)
                    den = acc[64:66, 0, lsl]
                    if isB:
                        i_ = j - 4
                        P.op("dve", lambda e, i_=i_, den=den, lsl=lsl: e.scalar_tensor_tensor(out=den, in0=den, scalar=esink[64:66, i_:i_ + 1],
                                                                                         in1=acc[64:66, 1, lsl], op0=ALU.add, op1=ALU.add),
                             reads=[R_acc[1], R_const], writes=[R_acc[0]])
                    else:
                        P.op("dve", lambda e, den=den, lsl=lsl: e.tensor_tensor(out=den, in0=den, in1=acc[64:66, 1, lsl], op=ALU.add),
                             reads=[R_acc[1]], writes=[R_acc[0]])
                    P.op("act", lambda e, den=den: e.activation(out=den, in_=den, func=AF.Ln), writes=[R_acc[0]])
                    P.op("act", lambda e, den=den: e.activation(out=den, in_=den, func=AF.Exp, scale=-1.0), writes=[R_acc[0]])
                    P.op("pe", lambda e, bk=bk, lsl=lsl: e.matmul(banks[bk][:, :], lhsT=sel2[64:66, :], rhs=acc[64:66, 0, lsl], start=True, stop=True),
                         reads=[R_acc[0], R_const], writes=[R_bank[bk]])
                    mi = tq % 2
                    P.op("dve", lambda e, bk=bk, lsl=lsl, mi=mi: e.tensor_tensor(out=mtmp[mi][0:64, :], in0=banks[bk][0:64, :], in1=acc[0:64, 0, lsl], op=ALU.mult),
                         reads=[R_bank[bk], R_acc[0]], writes=[R_m[mi]])
                    P.op("dve", lambda e, bk=bk, lsl=lsl, mi=mi: e.tensor_tensor(out=mtmp[mi][64:128, :], in0=banks[bk][64:128, :], in1=acc[0:64, 1, lsl], op=ALU.mult),
                         reads=[R_bank[bk], R_acc[1]], writes=[R_m[mi]])
                    P.op("pool", lambda e, mi=mi, gsl=gsl, j=j: e.tensor_tensor(out=mixT[:, j, gsl], in0=mtmp[mi][:, :], in1=mixT[:, j, gsl], op=ALU.mult),
                         reads=[R_m[mi]], writes=[R_mix[j][tile]])
        allres = R_QT + R_KT + R_VT + [R_Vd] + R_acc + R_w + R_pT + R_cs + [R_t] + R_rot + R_m
        return allres, R_mix

    def out_phase(prev, R_mix):
        wo = carve([128, 8, D], BF16, reset=True)
        NB_ = 4
        xr = [carve([128, D], F32) for _ in range(NB_)]
        zz = [carve([128, D], F32) for _ in range(NB_)]
        yy = zz
        stat = [carve([128, 32], F32) for _ in range(NB_)]
        lng = carve([128, D], F32)
        lnb = carve([128, D], F32)
        R_ln = Res()
        R_wo = Res()
        R_xr = [Res() for _ in range(NB_)]
        R_z = [Res() for _ in range(NB_)]
        R_y = R_z
        R_st = [Res() for _ in range(NB_)]
        dma("pool", wo[:], d_wo.rearrange("(kc p) f -> p kc f", p=128), writes=[R_wo] + prev)
        dma("sp", lng[:], d_lng[0, :].partition_broadcast(128), writes=[R_ln])
        dma("sp", lnb[:], d_lnb[0, :].partition_broadcast(128), writes=[R_ln])
        def load_x(tb):
            if tb < 32:
                dma("sp", xr[tb % NB_][:], d_x[tb * 128:(tb + 1) * 128, :], writes=[R_xr[tb % NB_]])
        for tb in range(NB_ - 1):
            load_x(tb)
        for tb in range(32):
            i2 = tb % NB_
            tile = tb // 4
            load_x(tb + NB_ - 1)
            for hh in range(2):
                bk = 2 * (tb % 2) + hh
                for c in range(8):
                    P.op("pe", lambda e, bk=bk, c=c, tb=tb, hh=hh: e.matmul(banks[bk][:, :], lhsT=mixT[:, c, tb * 128:(tb + 1) * 128],
                                                                        rhs=wo[:, c, hh * 512:(hh + 1) * 512], start=(c == 0), stop=(c == 7)),
                         reads=[R_mix[c][tile], R_wo], writes=[R_bank[bk]])
                P.op("dve", lambda e, bk=bk, hh=hh, i2=i2: e.scalar_tensor_tensor(out=zz[i2][:, hh * 512:(hh + 1) * 512], in0=xr[i2][:, hh * 512:(hh + 1) * 512],
                                                                              scalar=ALPHA, in1=banks[bk][:, :], op0=ALU.mult, op1=ALU.add),
                     reads=[R_bank[bk], R_xr[i2]], writes=[R_z[i2]])
            layer_norm_rows(128, zz[i2], stat[i2], R_z[i2], R_st[i2], yy[i2], R_y[i2], lng, lnb, R_ln)
            out_tokens.append(dma("sp", o_y[tb * 128:(tb + 1) * 128, :], yy[i2][:], reads=[R_y[i2]]))

    PH = os.environ.get("KPH", "sample,kv,pair,out").split(",")
    NP_ = int(os.environ.get("KNP", str(NPAIR)))
    if "sample" in PH:
        r1 = sample_phase()
    load_xT()
    if "sample" in PH:
        phase_barrier(r1)
    if "kv" in PH:
        r2 = kvout_phase([])
        phase_barrier(r2)
    R_mix = [[Res() for _ in range(8)] for _ in range(8)]
    if "pair" in PH:
        r3, R_mix = pair_phase([], NP_)
        phase_barrier(r3)
    if "out" in PH:
        out_phase([], R_mix)
    Res.default_w = None
    P.q["sp"].append((None, list(out_tokens), None))

    rank = {}
    for eng in Prog.ENG:
        cnt = 0
        for idx in range(len(P.q[eng])):
            if (eng, idx) in P.needed:
                cnt += 1
                rank[(eng, idx)] = cnt

    def replay(eng, e):
        waited = {}
        for idx, (fn, ws, dtok) in enumerate(P.q[eng]):
            for w in ws:
                if w[0] == "dma":
                    key, val, sem = ("dma", w[1]), w[2], sem_dma[w[1]]
                else:
                    key, val, sem = w[0], rank[w], sem_eng[w[0]]
                if waited.get(key, 0) >= val:
                    continue
                e.wait_ge(sem, val)
                waited[key] = val
            if fn is None:
                continue
            ins = fn(e)
            if dtok is not None:
                ins.then_inc(sem_dma[dtok[1]], 16)
            elif (eng, idx) in P.needed:
                ins.then_inc(sem_eng[eng], 1)

    with nc.Block() as block:
        @block.tensor
        def _(e):
            replay("pe", e)

        @block.scalar
        def _(e):
            replay("act", e)

        @block.vector
        def _(e):
            replay("dve", e)

        @block.gpsimd
        def _(e):
            replay("pool", e)

        @block.sync
        def _(e):
            replay("sp", e)
    es.close()
    return nc, consts


def kernel(x_prompt, x_sample, cache_a_kv, cache_b_kv, w_in, attn_sinks, w_o, ln_g, ln_b):
    x_prompt = np.asarray(x_prompt, np.float32)
    x_sample = np.asarray(x_sample, np.float32)
    cache_a_kv = np.asarray(cache_a_kv, np.float32)
    cache_b_kv = np.asarray(cache_b_kv, np.float32)
    w_in0 = np.asarray(w_in, np.float32)[0]
    w_o0 = np.asarray(w_o, np.float32)[0]
    sinks = np.asarray(attn_sinks, np.float32)[0]
    if "nc" not in _CACHE:
        _CACHE["nc"] = build()
    nc, consts = _CACHE["nc"]
    in_maps = _prep(consts, x_prompt, x_sample, cache_a_kv, cache_b_kv, w_in0, w_o0, sinks, ln_g, ln_b)
    return _run(nc, in_maps)


def _prep(consts, x_prompt, x_sample, cache_a_kv, cache_b_kv, w_in0, w_o0, sinks, ln_g, ln_b, cores=range(NCORES)):

    cols = []
    for j in range(NPAIR):
        cols += _pair_cols(j)
    wc = np.ascontiguousarray(w_in0[:, cols])
    wkv = np.ascontiguousarray(np.concatenate([w_in0[:, 512:1536], w_in0[:, 2560:2816]], axis=1))
    wo_p = np.ascontiguousarray(w_o0[_wo_rows(), :])
    sinkp = np.zeros((128, 4), np.float32)
    sinkp[64, :] = sinks[0:4]
    sinkp[65, :] = sinks[4:8]
    shared = {
        "wc": wc, "wkv": wkv, "wo": wo_p, "wos": np.ascontiguousarray(w_o0), "wins": np.ascontiguousarray(w_in0),
        "lng": np.ascontiguousarray(np.asarray(ln_g, np.float32).reshape(1, D)),
        "lnb": np.ascontiguousarray(np.asarray(ln_b, np.float32).reshape(1, D)),
        "sink": np.ascontiguousarray(sinks.reshape(1, 8)), "sinkp": sinkp,
    }
    for k, v in consts.items():
        shared["c_" + k] = v
    in_maps = []
    for c in cores:
        m = dict(shared)
        m["xT"] = np.ascontiguousarray(x_prompt[c].T)
        m["x"] = np.ascontiguousarray(x_prompt[c])
        xs = x_sample[c * NS:(c + 1) * NS, 0, :]
        m["xsT"] = np.ascontiguousarray(xs.T)
        m["xs"] = np.ascontiguousarray(xs)
        m["ca"] = np.ascontiguousarray(cache_a_kv[0, c * NS:(c + 1) * NS].reshape(NS, 2048, 1024))
        m["cb"] = np.ascontiguousarray(cache_b_kv[0, c * NS:(c + 1) * NS].reshape(NS, 128, 256))
        in_maps.append(m)
    return in_maps


def _run(nc, in_maps):
    res = run_bass_kernel_spmd(nc, in_maps, core_ids=list(range(NCORES)))
    rs = res.results
    y = np.stack([rs[c]["y"] for c in range(NCORES)]).reshape(8, S, D)
    ys = np.concatenate([rs[c]["ys"] for c in range(NCORES)]).reshape(128, 1, D)
    pa = np.stack([rs[c]["pkva"] for c in range(NCORES)]).reshape(1, 8, 2048, 2, 8, 64)
    pb = np.stack([rs[c]["pkvb"] for c in range(NCORES)]).reshape(1, 8, 128, 2, 2, 64)
    sa = np.concatenate([rs[c]["skva"] for c in range(NCORES)]).reshape(1, 128, 1, 2, 8, 64)
    sb_ = np.concatenate([rs[c]["skvb"] for c in range(NCORES)]).reshape(1, 128, 1, 2, 2, 64)
    return (y.astype(np.float32), ys.astype(np.float32), pa.astype(np.float32), pb.astype(np.float32),
            sa.astype(np.float32), sb_.astype(np.float32))
```

```python
import os
import numpy as np
import concourse.bass as bass
import concourse.mybir as mybir
from concourse.bass_utils import run_bass_kernel_spmd

F32 = mybir.dt.float32
BF16 = mybir.dt.bfloat16
ALU = mybir.AluOpType
AF = mybir.ActivationFunctionType
AX = mybir.AxisListType

NCORES = 8
D = 1024
S = 4096
NS = 16
PAST = 16384
ALPHA = 2.0 ** 0.25
EPS = 1e-5
DILS = (1, 4, 16)
NPAIR = 8
NDMA_SEM = 24

_CACHE = {}


class Res:
    __slots__ = ("w", "r")
    default_w = None

    def __init__(self):
        self.w = Res.default_w
        self.r = []


class Prog:
    ENG = ("pe", "act", "dve", "pool", "sp")

    def __init__(self):
        self.q = {e: [] for e in self.ENG}
        self.needed = set()
        self.dma_slot_tok = [None] * NDMA_SEM
        self.dma_slot_cnt = [0] * NDMA_SEM
        self.dma_next = {"sp": 0, "pool": 0, "act": 0}
        self.dma_rng = {"sp": (0, 14), "pool": (14, 10), "act": (0, 14)}

    def emit(self, eng, fn, waits, dma=False):
        ws = [w for w in waits if w is not None]
        if dma:
            base, cnt = self.dma_rng[eng]
            slot = base + self.dma_next[eng]
            self.dma_next[eng] = (self.dma_next[eng] + 1) % cnt
            if self.dma_slot_tok[slot] is not None:
                ws.append(self.dma_slot_tok[slot])
            self.dma_slot_cnt[slot] += 16
            tok = ("dma", slot, self.dma_slot_cnt[slot])
            self.dma_slot_tok[slot] = tok
        else:
            tok = (eng, len(self.q[eng]))
        ws2 = []
        for w in ws:
            if eng == "pe" and w[0] == "pe":
                continue
            ws2.append(w)
            if w[0] != "dma":
                self.needed.add(w)
        self.q[eng].append((fn, ws2, tok if dma else None))
        return tok

    def op(self, eng, fn, reads=(), writes=(), dma=False, extra=()):
        ws = list(extra)
        for r in reads:
            ws.append(r.w)
        for r in writes:
            ws.append(r.w)
            ws.extend(r.r)
        tok = self.emit(eng, fn, ws, dma)
        for r in reads:
            if tok[0] != "dma":
                r.r = [t for t in r.r if t[0] != tok[0]]
            r.r.append(tok)
        for r in writes:
            r.w = tok
            r.r = []
        return tok


def _consts():
    c = {}
    c["ident_bf"] = np.eye(128, dtype=np.float32)
    k = np.arange(128)[:, None]
    q = np.arange(256)[None, :]
    m = np.where(q < 128, (q >= k), ((q - 128) <= k)).astype(np.float32)
    c["mask2"] = np.concatenate([m, m, m, m], axis=1)
    inv = 500000.0 ** (-np.arange(0, 16, 2, dtype=np.float32) / 16.0)
    inv = inv.astype(np.float32)
    pos = np.arange(S, dtype=np.float32)
    ang = pos[:, None] * inv[None, :]
    cos = np.cos(ang).astype(np.float32)
    sin = np.sin(ang).astype(np.float32)
    rows = np.arange(64)
    f = rows % 8
    sign = np.where((rows % 16) < 8, -1.0, 1.0).astype(np.float32)
    c["CC"] = np.ascontiguousarray(cos[:, f].T)
    c["SS"] = np.ascontiguousarray((sin[:, f] * sign[None, :]).T)
    c["cosT"] = np.ascontiguousarray(cos[2048:].reshape(16, 128, 8).transpose(1, 0, 2))
    c["sinT"] = np.ascontiguousarray(sin[2048:].reshape(16, 128, 8).transpose(1, 0, 2))
    c["cosTb"] = np.ascontiguousarray(cos[S - 128:])
    c["sinTb"] = np.ascontiguousarray(sin[S - 128:])
    angs = (np.float32(PAST) * inv).astype(np.float32)
    c["cs_s"] = np.repeat(np.concatenate([np.cos(angs), np.sin(angs)])[None, :].astype(np.float32), NS, 0)
    misc = np.zeros((128, 128), np.float32)
    misc[64, 0:64] = 1.0
    misc[65, 64:128] = 1.0
    c["sel2"] = misc
    c["neg1"] = -np.ones((128, 512), np.float32)
    en = np.zeros((8, NS, NS), np.float32)
    for n in range(NS):
        en[:, n, n] = 1.0
    c["en"] = en.reshape(8, NS * NS)
    bd = np.zeros((8, 8, 64), np.float32)
    for h in range(8):
        bd[h, h, :] = 1.0
    c["bd"] = bd.reshape(8, 512)
    c["i8"] = np.eye(8, dtype=np.float32)
    c["i16"] = np.eye(16, dtype=np.float32)
    c["onescol"] = np.ones((128, 1), np.float32)
    return c


CONST_SHAPES = None


def _pair_cols(j):
    def rope_rows(base_a, base_b):
        return list(range(base_a, base_a + 16)) + list(range(base_b, base_b + 16))

    def swap16(cols):
        out = []
        for g in range(0, len(cols), 16):
            blk = cols[g:g + 16]
            out += blk[8:16] + blk[0:8]
        return out
    if j < 4:
        ha, hb = 2 * j, 2 * j + 1
        q = list(range(0 + 128 * j, 128 * j + 128))
        k = list(range(512 + 128 * j, 512 + 128 * j + 128))
        v = list(range(1024 + 128 * j, 1024 + 128 * j + 128))
        g = list(range(1536 + 128 * j, 1536 + 128 * j + 128))
        r = rope_rows(64 * ha, 64 * hb) + rope_rows(512 + 64 * ha, 512 + 64 * hb)
    else:
        i = j - 4
        ha, hb = i, 4 + i
        q = list(range(2048 + 64 * ha, 2048 + 64 * ha + 64)) + list(range(2048 + 64 * hb, 2048 + 64 * hb + 64))
        k = list(range(2560, 2688))
        v = list(range(2688, 2816))
        g = list(range(2816 + 64 * ha, 2816 + 64 * ha + 64)) + list(range(2816 + 64 * hb, 2816 + 64 * hb + 64))
        r = rope_rows(2048 + 64 * ha, 2048 + 64 * hb) + rope_rows(2560, 2560 + 64)
    r = r + swap16(r)
    return q + k + v + g + r


def _wo_rows():
    rows = list(range(512))
    for i in range(4):
        ha, hb = i, 4 + i
        rows += list(range(512 + 64 * ha, 512 + 64 * ha + 64)) + list(range(512 + 64 * hb, 512 + 64 * hb + 64))
    return rows


def build():
    nc = bass.Bass("TRN2", target_bir_lowering=False)
    consts = _consts()
    P = Prog()

    def din(name, shape):
        return nc.dram_tensor(name, list(shape), F32, kind="ExternalInput").ap()

    def dout(name, shape):
        return nc.dram_tensor(name, list(shape), F32, kind="ExternalOutput").ap()

    d_xT = din("xT", [D, S])
    d_x = din("x", [S, D])
    d_wc = din("wc", [D, NPAIR * 640])
    d_wkv = din("wkv", [D, 1280])
    d_wo = din("wo", [D, D])
    d_wos = din("wos", [D, D])
    d_wins = din("wins", [D, 3328])
    d_lng = din("lng", [1, D])
    d_lnb = din("lnb", [1, D])
    d_sink = din("sink", [1, 8])
    d_sinkp = din("sinkp", [128, 4])
    d_xsT = din("xsT", [D, NS])
    d_xs = din("xs", [NS, D])
    d_ca = din("ca", [NS, 2048, 1024])
    d_cb = din("cb", [NS, 128, 256])
    dc = {k: din("c_" + k, v.shape) for k, v in consts.items()}
    o_y = dout("y", [S, D])
    o_pa = dout("pkva", [2048, 1024])
    o_pb = dout("pkvb", [128, 256])
    o_ys = dout("ys", [NS, D])
    o_sa = dout("skva", [NS, 1024])
    o_sb = dout("skvb", [NS, 256])

    import contextlib
    es = contextlib.ExitStack()

    def sb(name, shape, dt=F32):
        return es.enter_context(nc.sbuf_tensor(name, list(shape), dt))

    xT = sb("xT_sb", [128, 8, S], BF16)
    mixT = sb("mixT", [128, 8, S], BF16)
    ident = sb("ident", [128, 128], BF16)
    mask2 = sb("mask2", [128, 1024], BF16)
    sel2 = sb("sel2", [128, 128], F32)
    onescol = sb("onescol", [128, 1], F32)
    esink = sb("esink", [128, 4], F32)
    bar = sb("bar", [128, 2], F32)
    SCR = 76 * 1024
    scr = sb("scr", [128, SCR // 2], BF16)
    scr_off = [0]

    def carve(shape, dt, reset=False):
        if reset:
            scr_off[0] = 0
        n = int(np.prod(shape[1:]))
        esz = 2 if dt == BF16 else 4
        nbytes = n * esz
        off = (scr_off[0] + 3) // 4 * 4
        assert off + nbytes <= SCR, (off, nbytes)
        scr_off[0] = off + nbytes
        ap = scr[:, off // 2: (off + nbytes) // 2]
        if dt != BF16:
            ap = ap.bitcast(dt)
        if len(shape) > 2:
            names = " ".join("a%d" % i for i in range(len(shape) - 1))
            kw = {"a%d" % i: shape[i + 1] for i in range(len(shape) - 1)}
            ap = ap.rearrange("p (%s) -> p %s" % (names, names), **kw)
        return ap

    banks = [es.enter_context(nc.psum_tensor("ps%d" % i, [128, 512], F32)) for i in range(8)]
    R_bank = [Res() for _ in range(8)]
    sem_eng = {e: es.enter_context(nc.semaphore("s_" + e)) for e in Prog.ENG}
    sem_dma = [es.enter_context(nc.semaphore("s_dma%d" % i)) for i in range(NDMA_SEM)]

    out_tokens = []
    Res.default_w = None

    def phase_barrier(prev):
        tok = P.op("dve", lambda e: e.memset(bar[:, 0:1], 0.0), writes=list(prev))
        Res.default_w = tok

    def dma(eng, out, in_, reads=(), writes=(), **kw):
        return P.op(eng, lambda e: e.dma_start(out=out, in_=in_, **kw), reads, writes, dma=True)

    def strided(ap2, start, step, n):
        if step == 1:
            return ap2[:, start:start + n]
        v = ap2.rearrange("p (m r) -> p m r", r=step)
        return v[:, start // step: start // step + n, start % step]

    R_const = Res()
    R_xT = [Res() for _ in range(8)]
    dma("pool", ident[:], dc["ident_bf"][:, :], writes=[R_const])
    dma("pool", mask2[:], dc["mask2"][:, :], writes=[R_const])
    dma("sp", sel2[:], dc["sel2"][:, :], writes=[R_const])
    dma("sp", onescol[:], dc["onescol"][:, :], writes=[R_const])
    dma("sp", esink[:], d_sinkp[:, :], writes=[R_const])
    P.op("act", lambda e: e.activation(out=esink[:], in_=esink[:], func=AF.Exp), writes=[R_const])
    d_xT3 = d_xT.rearrange("(kc p) t -> p kc t", p=128)
    xT_next = [0]

    def load_xT(npieces=16):
        for _ in range(npieces):
            i = xT_next[0]
            if i >= 16:
                return
            xT_next[0] += 1
            kc, hh = i // 2, i % 2
            dma("pool", xT[:, kc, hh * 2048:(hh + 1) * 2048], d_xT3[:, kc, hh * 2048:(hh + 1) * 2048],
                writes=[R_xT[kc]], max_dma_last_dim=8192)

    def layer_norm_rows(np_, zs, stat, R_z, R_stat, ys, R_y, lng, lnb, R_ln, pool_affine=True):
        for hh in range(2):
            P.op("dve", lambda e, hh=hh: e.bn_stats(out=stat[0:np_, hh * 6:(hh + 1) * 6], in_=zs[0:np_, hh * 512:(hh + 1) * 512]),
                 reads=[R_z], writes=[R_stat])
        P.op("dve", lambda e: e.bn_aggr(out=stat[0:np_, 12:14], in_=stat[0:np_, 0:12]), writes=[R_stat])
        P.op("dve", lambda e: e.tensor_scalar(out=stat[0:np_, 14:15], in0=stat[0:np_, 13:14], scalar1=EPS, scalar2=None, op0=ALU.add),
             writes=[R_stat])
        P.op("act", lambda e: e.activation(out=stat[0:np_, 15:16], in_=stat[0:np_, 14:15], func=AF.Sqrt), writes=[R_stat])
        P.op("dve", lambda e: e.reciprocal(out=stat[0:np_, 16:17], in_=stat[0:np_, 15:16]), writes=[R_stat])
        P.op("dve", lambda e: e.tensor_scalar(out=stat[0:np_, 17:18], in0=stat[0:np_, 12:13], scalar1=stat[0:np_, 16:17], scalar2=-1.0,
                                              op0=ALU.mult, op1=ALU.mult), writes=[R_stat])
        P.op("act", lambda e: e.activation(out=zs[0:np_, :], in_=zs[0:np_, :], func=AF.Identity, scale=stat[0:np_, 16:17], bias=stat[0:np_, 17:18]),
             reads=[R_stat], writes=[R_z])
        eng = "pool" if pool_affine else "dve"
        P.op(eng, lambda e: e.tensor_tensor(out=zs[0:np_, :], in0=zs[0:np_, :], in1=lng[0:np_, :], op=ALU.mult), reads=[R_ln], writes=[R_z])
        P.op(eng, lambda e: e.tensor_tensor(out=ys[0:np_, :], in0=zs[0:np_, :], in1=lnb[0:np_, :], op=ALU.add), reads=[R_ln, R_z], writes=[R_y])

    def sample_phase():
        hs = carve([128, 3328], F32, reset=True)
        xsT = carve([128, 8, NS], BF16)
        wsb = [carve([128, 8, 512], BF16) for _ in range(2)]
        kvt = [carve([128, 1024], F32) for _ in range(2)]
        kvt += [wsb[1][:, 0:4, :].rearrange("p k f -> p (k f)").bitcast(F32), wsb[1][:, 4:8, :].rearrange("p k f -> p (k f)").bitcast(F32)]
        kvb = [carve([128, 256], F32) for _ in range(4)]
        prod = carve([128, 512], F32)
        s8 = [carve([128, 8], F32) for _ in range(2)]
        p8 = [carve([128, 8], BF16) for _ in range(2)]
        vbf = [carve([128, 512], BF16) for _ in range(2)]
        R_vbf = [Res(), Res()]
        qbf = carve([128, 1024], BF16)
        i16b = carve([128, 16], BF16)
        onesb = carve([128, 2], BF16)
        masked = [carve([128, 512], F32) for _ in range(2)]
        dd8 = [carve([128, 8], F32) for _ in range(2)]
        en = carve([128, NS * NS], F32)
        bd = carve([128, 512], F32)
        i8 = carve([128, 8], F32)
        i16 = carve([128, 16], F32)
        css = carve([128, 16], F32)
        tmp = [carve([128, 16, 8], F32) for _ in range(4)]
        small = carve([128, 64], F32)
        mixs = carve([128, D], F32)
        mixsT = carve([128, 8, NS], BF16)
        sink16 = carve([128, 8], F32)
        xs_sb = carve([128, D], F32)
        stat = carve([128, 32], F32)
        lng = carve([128, D], F32)
        lnb = carve([128, D], F32)
        R_ln = Res()
        R = {k: Res() for k in ("hs", "xsT", "c", "prod", "small", "mixs", "mixsT", "xs", "stat", "ys", "tmp")}
        R_wsb = [Res(), Res()]
        R_kvt = [Res() for _ in range(4)]
        R_kvb = [Res() for _ in range(4)]
        R_s8 = [Res(), Res()]
        R_p8 = [Res(), Res()]
        R_msk = [Res(), Res()]
        R_dd = [Res(), Res()]

        dma("pool", xsT[:], d_xsT.rearrange("(kc p) n -> p kc n", p=128), writes=[R["xsT"]])
        dma("sp", lng[0:NS, :], d_lng[0, :].partition_broadcast(NS), writes=[R_ln])
        dma("sp", lnb[0:NS, :], d_lnb[0, :].partition_broadcast(NS), writes=[R_ln])
        dma("sp", en[0:8, :], dc["en"][:, :], writes=[R["c"]])
        dma("sp", bd[0:8, :], dc["bd"][:, :], writes=[R["c"]])
        dma("sp", i8[0:8, :], dc["i8"][:, :], writes=[R["c"]])
        dma("sp", i16[0:16, :], dc["i16"][:, :], writes=[R["c"]])
        dma("pool", i16b[0:16, :], dc["i16"][:, :], writes=[R["c"]])
        dma("pool", onesb[:, 0:1], dc["onescol"][:, :], writes=[R["c"]])
        dma("sp", css[0:NS, :], dc["cs_s"][:, :], writes=[R["c"]])
        dma("sp", sink16[0:NS, :], d_sink[0, :].partition_broadcast(NS), writes=[R["c"]])
        dma("sp", xs_sb[0:NS, :], d_xs[:, :], writes=[R["xs"]])
        P.op("act", lambda e: e.activation(out=sink16[0:NS, :], in_=sink16[0:NS, :], func=AF.Exp), writes=[R["c"]])

        d_w3 = d_wins.rearrange("(kc p) f -> p kc f", p=128)
        chunks = [(0, 512), (512, 512), (1024, 512), (1536, 512), (2048, 512), (2560, 512), (3072, 256)]
        for ci, (c0, cw) in enumerate(chunks):
            wb = wsb[ci % 2]
            dma("pool", wb[:, :, 0:cw], d_w3[:, :, c0:c0 + cw], writes=[R_wsb[ci % 2]])
            bk = ci % 2
            for kc in range(8):
                P.op("pe", lambda e, kc=kc, wb=wb, cw=cw, bk=bk: e.matmul(banks[bk][0:NS, 0:cw], lhsT=xsT[:, kc, :], rhs=wb[:, kc, 0:cw],
                                                                         start=(kc == 0), stop=(kc == 7)),
                     reads=[R["xsT"], R_wsb[ci % 2]], writes=[R_bank[bk]])
            P.op("act", lambda e, c0=c0, cw=cw, bk=bk: e.copy(out=hs[0:NS, c0:c0 + cw], in_=banks[bk][0:NS, 0:cw]),
                 reads=[R_bank[bk]], writes=[R["hs"]])

        def rope_tm(np_, view, nh, cosb, sinb):
            x1 = view[:, :, 0:8]
            x2 = view[:, :, 8:16]
            t = [tt[0:np_, 0:nh, :] for tt in tmp]
            P.op("dve", lambda e: e.tensor_tensor(out=t[0], in0=x1, in1=cosb, op=ALU.mult), reads=[R["hs"], R["c"]], writes=[R["tmp"]])
            P.op("dve", lambda e: e.tensor_tensor(out=t[1], in0=x2, in1=sinb, op=ALU.mult), reads=[R["hs"]], writes=[R["tmp"]])
            P.op("dve", lambda e: e.tensor_tensor(out=t[2], in0=x2, in1=cosb, op=ALU.mult), reads=[R["hs"]], writes=[R["tmp"]])
            P.op("dve", lambda e: e.tensor_tensor(out=t[3], in0=x1, in1=sinb, op=ALU.mult), reads=[R["hs"]], writes=[R["tmp"]])
            P.op("dve", lambda e: e.tensor_tensor(out=x1, in0=t[0], in1=t[1], op=ALU.subtract), reads=[R["tmp"]], writes=[R["hs"]])
            P.op("dve", lambda e: e.tensor_tensor(out=x2, in0=t[2], in1=t[3], op=ALU.add), reads=[R["tmp"]], writes=[R["hs"]])

        for (c0, nh) in ((0, 16), (2048, 8), (2560, 2)):
            view = hs[0:NS, c0:c0 + nh * 64].rearrange("p (h d) -> p h d", d=64)
            cosb = css[0:NS, 0:8].unsqueeze(1).broadcast_to([NS, nh, 8])
            sinb = css[0:NS, 8:16].unsqueeze(1).broadcast_to([NS, nh, 8])
            rope_tm(NS, view, nh, cosb, sinb)
        R["qbf"] = Res()
        P.op("act", lambda e: e.copy(out=qbf[0:NS, 0:512], in_=hs[0:NS, 0:512]), reads=[R["hs"]], writes=[R["qbf"]])
        P.op("act", lambda e: e.copy(out=qbf[0:NS, 512:1024], in_=hs[0:NS, 2048:2560]), reads=[R["hs"]], writes=[R["qbf"]])
        out_tokens.append(dma("sp", o_sa[:, :], hs[0:NS, 512:1536], reads=[R["hs"]]))
        out_tokens.append(dma("sp", o_sb[:, :], hs[0:NS, 2560:2816], reads=[R["hs"]]))

        B_QBC, B_OUT, B_DEN, B_ACC, B_ACD = 2, 3, 4, 5, 6
        it = 0
        for mixer in ("A", "B"):
            qoff = 0 if mixer == "A" else 2048
            QB = (2, 7)

            def emit_qbc(n, qoff=qoff):
                bq = QB[n % 2]
                qo2 = 0 if qoff == 0 else 512
                P.op("pe", lambda e, n=n, qo2=qo2, bq=bq: e.matmul(banks[bq][:, :], lhsT=i16b[0:NS, n:n + 1].broadcast_to([NS, 128]),
                                                                   rhs=qbf[0:NS, qo2:qo2 + 512], start=True, stop=True),
                     reads=[R["qbf"], R["c"]], writes=[R_bank[bq]])
            emit_qbc(0)
            for n in range(NS):
                B_QBC = QB[n % 2]
                dl = DILS if mixer == "A" else (1,)
                for di, d in enumerate(dl):
                    i2 = it % 2
                    if mixer == "A":
                        kb_ = it % 4
                        t_kv = kvt[kb_]
                        rk = R_kvt[kb_]
                        src = d_ca[n, 2048 - 128 * d:2048, :].rearrange("(j r) f -> j r f", r=d)[:, 0, :]
                        dma("sp", t_kv[:], src, writes=[rk] + ([R_wsb[1]] if (kb_ >= 2 and it < 4) else []))
                        kk = t_kv[:, 0:512]
                        vv = t_kv[:, 512:1024]
                        qq = banks[B_QBC][:, :]
                        nv = 512
                        P.op("dve", lambda e, kk=kk, qq=qq: e.tensor_tensor(out=prod[:], in0=qq, in1=kk, op=ALU.mult),
                             reads=[rk, R_bank[B_QBC]], writes=[R["prod"]])
                    else:
                        t_kv = kvb[it % 4]
                        rk = R_kvb[it % 4]
                        dma("sp", t_kv[:], d_cb[n, :, :], writes=[rk])
                        kk = t_kv[:, 0:128].rearrange("p (g d) -> p g d", g=2).unsqueeze(2).broadcast_to([128, 2, 4, 64])
                        vv = t_kv[:, 128:256]
                        qq = banks[B_QBC][:, :].rearrange("p (g u d) -> p g u d", g=2, u=4)
                        nv = 128
                        P.op("dve", lambda e, kk=kk, qq=qq: e.tensor_tensor(out=prod[:].rearrange("p (g u d) -> p g u d", g=2, u=4),
                                                                            in0=qq, in1=kk, op=ALU.mult),
                             reads=[rk, R_bank[B_QBC]], writes=[R["prod"]])
                    P.op("dve", lambda e, i2=i2: e.tensor_reduce(out=s8[i2][:], in_=prod[:].rearrange("p (h d) -> p h d", d=64),
                                                                 axis=AX.X, op=ALU.add),
                         reads=[R["prod"]], writes=[R_s8[i2]])
                    P.op("act", lambda e, i2=i2: e.activation(out=p8[i2][:], in_=s8[i2][:], func=AF.Exp, scale=0.125),
                         reads=[R_s8[i2]], writes=[R_p8[i2]])
                    first = (di == 0)
                    last = (di == len(dl) - 1)
                    P.op("act", lambda e, i2=i2, vv=vv, nv=nv: e.copy(out=vbf[i2][:, 0:nv], in_=vv), reads=[rk], writes=[R_vbf[i2]])
                    P.op("pe", lambda e, i2=i2, nv=nv, first=first, last=last: e.matmul(
                        banks[B_OUT][0:8, 0:nv], lhsT=p8[i2][:], rhs=vbf[i2][:, 0:nv], start=first, stop=last),
                        reads=[R_p8[i2], R_vbf[i2]], writes=[R_bank[B_OUT]])
                    P.op("pe", lambda e, i2=i2, first=first, last=last: e.matmul(
                        banks[B_DEN][0:8, 0:1], lhsT=p8[i2][:], rhs=onesb[:, 0:1], start=first, stop=last),
                        reads=[R_p8[i2], R["c"]], writes=[R_bank[B_DEN]])
                    it += 1
                if n + 1 < NS:
                    emit_qbc(n + 1)
                if mixer == "A":
                    load_xT(1)
                m2 = n % 2
                if mixer == "A":
                    P.op("dve", lambda e, m2=m2: e.tensor_tensor(out=masked[m2][0:8, :], in0=banks[B_OUT][0:8, :], in1=bd[0:8, :], op=ALU.mult),
                         reads=[R_bank[B_OUT], R["c"]], writes=[R_msk[m2]])
                else:
                    src = banks[B_OUT][0:8, 0:128].rearrange("p (g d) -> p g d", g=2).unsqueeze(2).broadcast_to([8, 2, 4, 64])
                    P.op("dve", lambda e, m2=m2, src=src: e.tensor_tensor(out=masked[m2][0:8, :].rearrange("p (g u d) -> p g u d", g=2, u=4),
                                                                          in0=src, in1=bd[0:8, :].rearrange("p (g u d) -> p g u d", g=2, u=4),
                                                                          op=ALU.mult),
                         reads=[R_bank[B_OUT], R["c"]], writes=[R_msk[m2]])
                P.op("dve", lambda e, m2=m2: e.tensor_scalar(out=dd8[m2][0:8, :], in0=i8[0:8, :], scalar1=banks[B_DEN][0:8, 0:1], scalar2=None,
                                                            op0=ALU.mult),
                     reads=[R_bank[B_DEN], R["c"]], writes=[R_dd[m2]])
                P.op("pe", lambda e, n=n, m2=m2: e.matmul(banks[B_ACC][0:NS, :], lhsT=en[0:8, n * NS:(n + 1) * NS], rhs=masked[m2][0:8, :],
                                                          start=(n == 0), stop=(n == NS - 1)),
                     reads=[R_msk[m2], R["c"]], writes=[R_bank[B_ACC]])
                P.op("pe", lambda e, n=n, m2=m2: e.matmul(banks[B_ACD][0:NS, 0:8], lhsT=en[0:8, n * NS:(n + 1) * NS], rhs=dd8[m2][0:8, :],
                                                          start=(n == 0), stop=(n == NS - 1)),
                     reads=[R_dd[m2], R["c"]], writes=[R_bank[B_ACD]])
            if mixer == "A":
                qv = hs[0:NS, 0:512]
                kv_ = hs[0:NS, 512:1024]
                vnew = hs[0:NS, 1024:1536].rearrange("p (h d) -> p h d", d=64)
                gcol, mcol, wnew = 1536, 0, 3.0
                P.op("dve", lambda e: e.tensor_tensor(out=prod[0:NS, :], in0=qv, in1=kv_, op=ALU.mult), reads=[R["hs"]], writes=[R["prod"]])
            else:
                qv = hs[0:NS, 2048:2560].rearrange("p (g u d) -> p g u d", g=2, u=4)
                kv_ = hs[0:NS, 2560:2688].rearrange("p (g d) -> p g d", g=2).unsqueeze(2).broadcast_to([NS, 2, 4, 64])
                vnew = hs[0:NS, 2688:2816].rearrange("p (g d) -> p g d", g=2).unsqueeze(2).broadcast_to([NS, 2, 4, 64])
                gcol, mcol, wnew = 2816, 512, 1.0
                P.op("dve", lambda e: e.tensor_tensor(out=prod[0:NS, :].rearrange("p (g u d) -> p g u d", g=2, u=4), in0=qv, in1=kv_, op=ALU.mult),
                     reads=[R["hs"]], writes=[R["prod"]])
            sn = small[0:NS, 0:8]
            pn = small[0:NS, 8:16]
            dn = small[0:NS, 16:24]
            rn = small[0:NS, 24:32]
            P.op("dve", lambda e: e.tensor_reduce(out=sn, in_=prod[0:NS, :].rearrange("p (h d) -> p h d", d=64), axis=AX.X, op=ALU.add),
                 reads=[R["prod"]], writes=[R["small"]])
            P.op("act", lambda e: e.activation(out=pn, in_=sn, func=AF.Exp, scale=0.125), writes=[R["small"]])
            P.op("dve", lambda e, wnew=wnew: e.scalar_tensor_tensor(out=dn, in0=pn, scalar=wnew, in1=banks[B_ACD][0:NS, 0:8],
                                                                   op0=ALU.mult, op1=ALU.add),
                 reads=[R_bank[B_ACD]], writes=[R["small"]])
            if mixer == "B":
                P.op("dve", lambda e: e.tensor_tensor(out=dn, in0=dn, in1=sink16[0:NS, :], op=ALU.add), reads=[R["c"]], writes=[R["small"]])
            P.op("dve", lambda e: e.reciprocal(out=rn, in_=dn), writes=[R["small"]])
            pnb = pn.unsqueeze(2).broadcast_to([NS, 8, 64])
            rnb = rn.unsqueeze(2).broadcast_to([NS, 8, 64])
            if mixer == "A":
                pv3 = prod[0:NS, :].rearrange("p (h d) -> p h d", d=64)
                P.op("dve", lambda e: e.tensor_tensor(out=pv3, in0=vnew, in1=pnb, op=ALU.mult), reads=[R["hs"], R["small"]], writes=[R["prod"]])
            else:
                pv4 = prod[0:NS, :].rearrange("p (g u d) -> p g u d", g=2, u=4)
                pnb4 = pn.rearrange("p (g u) -> p g u", g=2).unsqueeze(3).broadcast_to([NS, 2, 4, 64])
                P.op("dve", lambda e: e.tensor_tensor(out=pv4, in0=vnew, in1=pnb4, op=ALU.mult), reads=[R["hs"], R["small"]], writes=[R["prod"]])
            P.op("dve", lambda e, wnew=wnew: e.scalar_tensor_tensor(out=prod[0:NS, :], in0=prod[0:NS, :], scalar=wnew, in1=banks[B_ACC][0:NS, :],
                                                                   op0=ALU.mult, op1=ALU.add),
                 reads=[R_bank[B_ACC]], writes=[R["prod"]])
            P.op("dve", lambda e: e.tensor_tensor(out=prod[0:NS, :].rearrange("p (h d) -> p h d", d=64),
                                                  in0=prod[0:NS, :].rearrange("p (h d) -> p h d", d=64), in1=rnb, op=ALU.mult),
                 reads=[R["small"]], writes=[R["prod"]])
            P.op("act", lambda e, gcol=gcol: e.activation(out=hs[0:NS, gcol:gcol + 512], in_=hs[0:NS, gcol:gcol + 512], func=AF.Silu),
                 writes=[R["hs"]])
            P.op("dve", lambda e, gcol=gcol, mcol=mcol: e.tensor_tensor(out=mixs[0:NS, mcol:mcol + 512], in0=prod[0:NS, :],
                                                                       in1=hs[0:NS, gcol:gcol + 512], op=ALU.mult),
                 reads=[R["prod"], R["hs"]], writes=[R["mixs"]])
        wos = [carve([128, 4, D], BF16, reset=(i == 0)) for i in range(2)]
        R_wos = [Res(), Res()]
        d_wos3 = d_wos.rearrange("(kc p) f -> p kc f", p=128)
        dep_all = [R["hs"], R["xsT"], R_wsb[0], R_wsb[1], R["mixs"]]
        for i in range(2):
            dma("pool", wos[i][:], d_wos3[:, i * 4:(i + 1) * 4, :], reads=[], writes=[R_wos[i]] + (dep_all if i == 0 else []))
        for c in range(8):
            P.op("pe", lambda e, c=c: e.transpose(out=banks[2][:, c * NS:(c + 1) * NS], in_=mixs[0:NS, c * 128:(c + 1) * 128], identity=i16[0:NS, 0:NS]),
                 reads=[R["mixs"], R["c"]], writes=[R_bank[2]])
        P.op("dve", lambda e: e.tensor_copy(out=mixsT[:].rearrange("p c n -> p (c n)"), in_=banks[2][:, 0:8 * NS]),
             reads=[R_bank[2]], writes=[R["mixsT"]])
        for hh in range(2):
            for c in range(8):
                P.op("pe", lambda e, c=c, hh=hh: e.matmul(banks[hh][0:NS, :], lhsT=mixsT[:, c, :], rhs=wos[c // 4][:, c % 4, hh * 512:(hh + 1) * 512],
                                                          start=(c == 0), stop=(c == 7)),
                     reads=[R["mixsT"], R_wos[c // 4]], writes=[R_bank[hh]])
            P.op("dve", lambda e, hh=hh: e.scalar_tensor_tensor(out=mixs[0:NS, hh * 512:(hh + 1) * 512], in0=xs_sb[0:NS, hh * 512:(hh + 1) * 512],
                                                               scalar=ALPHA, in1=banks[hh][0:NS, :], op0=ALU.mult, op1=ALU.add),
                 reads=[R_bank[hh], R["xs"], R["mixsT"]], writes=[R["mixs"]])
        layer_norm_rows(NS, mixs, stat, R["mixs"], R["stat"], xs_sb, R["ys"], lng, lnb, R_ln)
        out_tokens.append(dma("sp", o_ys[:, :], xs_sb[0:NS, :], reads=[R["ys"]]))
        return list(R.values()) + R_wos + R_kvt + R_kvb + R_wsb + R_s8 + R_p8 + R_msk + R_dd + [R_ln] + R_vbf

    def kvout_phase(prev):
        wkv = carve([128, 8, 1280], BF16, reset=True)
        kvo = [carve([128, 1024], F32) for _ in range(2)]
        cosT = carve([128, 16, 8], F32)
        sinT = carve([128, 16, 8], F32)
        cosTb = carve([128, 8], F32)
        sinTb = carve([128, 8], F32)
        tmp = [carve([128, 8, 8], F32) for _ in range(4)]
        R_w = Res()
        R_c = Res()
        R_kvo = [Res(), Res()]
        R_tmp = Res()
        dma("pool", wkv[:], d_wkv.rearrange("(kc p) f -> p kc f", p=128), writes=[R_w] + prev)
        dma("sp", cosT[:], dc["cosT"][:, :, :], writes=[R_c])
        dma("sp", sinT[:], dc["sinT"][:, :, :], writes=[R_c])
        dma("sp", cosTb[:], dc["cosTb"][:, :], writes=[R_c])
        dma("sp", sinTb[:], dc["sinTb"][:, :], writes=[R_c])
        for blk in range(17):
            isb = (blk == 16)
            t0 = (S - 128) if isb else (2048 + blk * 128)
            nh = 2 if isb else 8
            wcol = 1024 if isb else 0
            kw = nh * 64
            ob = kvo[blk % 2]
            rk = R_kvo[blk % 2]
            kb0 = 2 * (blk % 2)
            for part in range(2):
                bk = kb0 + part
                for kc in range(8):
                    P.op("pe", lambda e, kc=kc, bk=bk, t0=t0, wcol=wcol, kw=kw, part=part: e.matmul(
                        banks[bk][:, 0:kw], lhsT=xT[:, kc, t0:t0 + 128], rhs=wkv[:, kc, wcol + part * kw: wcol + (part + 1) * kw],
                        start=(kc == 0), stop=(kc == 7)),
                        reads=[R_xT[kc], R_w], writes=[R_bank[bk]])
            P.op("dve", lambda e, ob=ob, kw=kw, kb0=kb0: e.tensor_copy(out=ob[:, 0:kw], in_=banks[kb0][:, 0:kw]), reads=[R_bank[kb0]], writes=[rk])
            P.op("act", lambda e, ob=ob, kw=kw, kb0=kb0: e.copy(out=ob[:, kw:2 * kw], in_=banks[kb0 + 1][:, 0:kw]), reads=[R_bank[kb0 + 1]], writes=[rk])
            kps = banks[kb0][:, 0:kw].rearrange("p (h d) -> p h d", d=64)
            x1 = kps[:, :, 0:8]
            x2 = kps[:, :, 8:16]
            if isb:
                cb_ = cosTb[:, :].unsqueeze(1).broadcast_to([128, nh, 8])
                sb_ = sinTb[:, :].unsqueeze(1).broadcast_to([128, nh, 8])
            else:
                cb_ = cosT[:, blk, :].unsqueeze(1).broadcast_to([128, nh, 8])
                sb_ = sinT[:, blk, :].unsqueeze(1).broadcast_to([128, nh, 8])
            t = [tt[:, 0:nh, :] for tt in tmp]
            P.op("dve", lambda e, t=t, x1=x1, cb_=cb_: e.tensor_tensor(out=t[0], in0=x1, in1=cb_, op=ALU.mult), reads=[R_bank[kb0], R_c], writes=[R_tmp])
            P.op("dve", lambda e, t=t, x2=x2, sb_=sb_: e.tensor_tensor(out=t[1], in0=x2, in1=sb_, op=ALU.mult), reads=[R_bank[kb0]], writes=[R_tmp])
            P.op("dve", lambda e, t=t, x2=x2, cb_=cb_: e.tensor_tensor(out=t[2], in0=x2, in1=cb_, op=ALU.mult), reads=[R_bank[kb0]], writes=[R_tmp])
            P.op("dve", lambda e, t=t, x1=x1, sb_=sb_: e.tensor_tensor(out=t[3], in0=x1, in1=sb_, op=ALU.mult), reads=[R_bank[kb0]], writes=[R_tmp])
            ov = ob[:, 0:kw].rearrange("p (h d) -> p h d", d=64)
            P.op("dve", lambda e, t=t, ov=ov: e.tensor_tensor(out=ov[:, :, 0:8], in0=t[0], in1=t[1], op=ALU.subtract), reads=[R_tmp], writes=[rk])
            P.op("dve", lambda e, t=t, ov=ov: e.tensor_tensor(out=ov[:, :, 8:16], in0=t[2], in1=t[3], op=ALU.add), reads=[R_tmp], writes=[rk])
            if isb:
                out_tokens.append(dma("sp", o_pb[:, :], ob[:, 0:256], reads=[rk]))
            else:
                out_tokens.append(dma("sp", o_pa[blk * 128:(blk + 1) * 128, :], ob[:, :], reads=[rk]))
        return [R_w, R_c, R_tmp] + R_kvo

    def pair_phase(prev, npair=NPAIR):
        QT = carve([128, S], BF16, reset=True)
        KT = carve([128, S], BF16)
        VT = carve([128, S], BF16)
        Vd = carve([128, 32, 2, 66], BF16)
        acc = carve([128, 2, 2048], F32)
        wbuf0 = carve([128, 8, 640], BF16)
        wbuf1 = mixT[:, 6:8, :].rearrange("p c t -> p (c t)")[:, 3072:3072 + 5120].rearrange("p (k f) -> p k f", k=8)
        pT = [carve([128, 1024], BF16) for _ in range(2)]
        cct = [carve([128, 512], F32)] * 2
        sst = [carve([128, 512], F32)] * 2
        t1 = carve([128, 512], F32)
        t2 = carve([128, 512], F32)
        rot = [carve([128, 512], BF16)] * 2
        pT = pT + [t1.bitcast(BF16)]
        mtmp = [carve([128, 512], F32) for _ in range(2)]
        R_QT = [Res() for _ in range(8)]
        R_KT = [Res() for _ in range(8)]
        R_VT = [Res() for _ in range(8)]
        R_Vd = Res()
        R_acc = [Res(), Res()]
        R_den = [Res() for _ in range(4)]

        def P_merge_deps(src, dst):
            dst.r.extend(src.r)
            if src.w is not None:
                dst.r.append(src.w)
            src.r = []
        R_w = [Res(), Res()]
        R_pT = [Res(), Res()]
        R_cs = [Res()] * 2
        R_t = Res()
        R_pT = R_pT + [R_t]
        R_rot = [Res()] * 2
        R_n1 = Res()
        R_m = [Res(), Res()]
        R_mix = [[Res() for _ in range(8)] for _ in range(8)]
        d_wc3 = d_wc.rearrange("(kc p) f -> p kc f", p=128)

        def wbuf_of(jj):
            return (wbuf1, R_w[1]) if (jj % 2 == 1 and jj <= 5) else (wbuf0, R_w[0])

        def load_w(jj):
            wb_, rw_ = wbuf_of(jj)
            dma("pool", wb_, d_wc3[:, :, jj * 640:(jj + 1) * 640], writes=[rw_])
        load_w(0)
        for (hh_, cc_, val_) in ((0, 64, 1.0), (0, 65, 0.0), (1, 64, 0.0), (1, 65, 1.0)):
            P.op("pool", lambda e, hh_=hh_, cc_=cc_, val_=val_: e.memset(Vd[:, :, hh_, cc_:cc_ + 1], val_), writes=[R_Vd])

        pcount = [0]
        scount = [0]
        u_started = set()
        PB = (0, 1)
        SBK = ((2, 3), (0, 1))
        UB = ((4, 5), (6, 7))

        for j in range(npair):
            isB = j >= 4
            has_kv = (not isB) or j == 4
            wb, rw = wbuf_of(j)
            early = (j + 1 < npair) and (wbuf_of(j + 1)[0] is not wb)
            if early:
                load_w(j + 1)
            for tile in range(8):
                tsl = slice(tile * 512, (tile + 1) * 512)
                ci2 = tile % 2
                dma("sp", cct[ci2][0:64, :], dc["CC"][:, tsl], writes=[R_cs[ci2]])
                dma("sp", sst[ci2][0:64, :], dc["SS"][:, tsl], writes=[R_cs[ci2]])
                for ci, kind in enumerate("QKVGR"):
                    if kind in "KV" and not has_kv:
                        continue
                    bk = (0, 1, 2, 3)[pcount[0] % 4]
                    pcount[0] += 1
                    for kc in range(8):
                        P.op("pe", lambda e, kc=kc, bk=bk, ci=ci, wb=wb, tsl=tsl: e.matmul(
                            banks[bk][:, :], lhsT=wb[:, kc, ci * 128:(ci + 1) * 128], rhs=xT[:, kc, tsl], start=(kc == 0), stop=(kc == 7)),
                            reads=[R_xT[kc], rw], writes=[R_bank[bk]])
                    if kind == "Q":
                        P.op("act", lambda e, bk=bk, tsl=tsl: e.copy(out=QT[:, tsl], in_=banks[bk][:, :]), reads=[R_bank[bk]], writes=[R_QT[tile]])
                    elif kind == "K":
                        P.op("dve", lambda e, bk=bk, tsl=tsl: e.tensor_copy(out=KT[:, tsl], in_=banks[bk][:, :]), reads=[R_bank[bk]], writes=[R_KT[tile]])
                    elif kind == "V":
                        P.op("act", lambda e, bk=bk, tsl=tsl: e.copy(out=VT[:, tsl], in_=banks[bk][:, :]), reads=[R_bank[bk]], writes=[R_VT[tile]])
                    elif kind == "G":
                        P.op("act", lambda e, bk=bk, tsl=tsl, j=j: e.activation(out=mixT[:, j, tsl], in_=banks[bk][:, :], func=AF.Silu),
                             reads=[R_bank[bk]], writes=[R_mix[j][tile]])
                    else:
                        P.op("dve", lambda e, bk=bk, ci2=ci2: e.tensor_tensor(out=t1[0:64, :], in0=banks[bk][0:64, :], in1=cct[ci2][0:64, :], op=ALU.mult),
                             reads=[R_bank[bk], R_cs[ci2]], writes=[R_t])
                        P.op("dve", lambda e, bk=bk, ci2=ci2: e.tensor_tensor(out=t2[0:64, :], in0=banks[bk][64:128, :], in1=sst[ci2][0:64, :], op=ALU.mult),
                             reads=[R_bank[bk], R_cs[ci2]], writes=[R_t])
                        P.op("dve", lambda e, ci2=ci2: e.tensor_tensor(out=rot[ci2][0:64, :], in0=t1[0:64, :], in1=t2[0:64, :], op=ALU.add),
                             reads=[R_t], writes=[R_rot[ci2]])
                        dma("sp", QT[0:16, tsl], rot[ci2][0:16, :], reads=[R_rot[ci2]], writes=[R_QT[tile]])
                        dma("sp", QT[64:80, tsl], rot[ci2][16:32, :], reads=[R_rot[ci2]], writes=[R_QT[tile]])
                        if has_kv:
                            dma("sp", KT[0:16, tsl], rot[ci2][32:48, :], reads=[R_rot[ci2]], writes=[R_KT[tile]])
                            dma("sp", KT[64:80, tsl], rot[ci2][48:64, :], reads=[R_rot[ci2]], writes=[R_KT[tile]])
            if j + 1 < npair and not early:
                load_w(j + 1)
            KSTOP = int(os.environ.get("KSTOP", "9"))
            if KSTOP <= 1:
                continue
            dils = (1,) if isB else DILS
            for H in range(2):
                for di, d in enumerate(dils):
                    nb = 32 // d
                    hb = nb // 2
                    build_vd = (not isB) or (j == 4 and H == 0)
                    need = [sl for sl in range(32) if (isB or H == 1 or (sl % nb) < hb)]
                    grp8 = [need[q:q + 4] for q in range(0, len(need), 4)]
                    for g8, slots4 in enumerate(grp8 if build_vd else []):
                        bk = PB[pcount[0] % 2]
                        pcount[0] += 1
                        bfv = banks[bk][:, :].bitcast(BF16)
                        for i, slot in enumerate(slots4):
                            r, b = slot // nb, slot % nb
                            t0 = d * 128 * b + r
                            P.op("pe", lambda e, bfv=bfv, i=i, t0=t0, d=d: e.transpose(out=bfv[:, i * 128:(i + 1) * 128],
                                                                                 in_=strided(VT[:, :], t0, d, 128), identity=ident[:, :]),
                                 reads=R_VT + [R_const], writes=[R_bank[bk]])
                        eng = "act" if g8 % 2 == 0 else "dve"
                        src = bfv[:, 0:512].rearrange("p (s h d) -> p s h d", s=4, h=2)
                        if slots4[3] - slots4[0] == 3:
                            dst = Vd[:, slots4[0]:slots4[0] + 4, :, 0:64]
                        else:
                            assert [x - slots4[0] for x in slots4] == [0, 2, 4, 6], slots4
                            dst = Vd[:, slots4[0]:slots4[0] + 8, :, 0:64].rearrange("p (s two) h d -> p s two h d", two=2)[:, :, 0, :, :]
                        if eng == "act":
                            P.op("act", lambda e, src=src, dst=dst: e.copy(out=dst, in_=src), reads=[R_bank[bk]], writes=[R_Vd])
                        else:
                            P.op("dve", lambda e, src=src, dst=dst: e.tensor_copy(out=dst, in_=src), reads=[R_bank[bk]], writes=[R_Vd])
                    if KSTOP <= 2:
                        continue
                    subs = []
                    for r in range(d):
                        b_lo, b_hi = H * hb, (H + 1) * hb
                        kbs = list(range(b_lo, b_hi))
                        if H == 1:
                            kbs = [b_lo - 1] + kbs
                        for kb in kbs:
                            has_cur = kb >= b_lo
                            has_next = (kb + 1 < b_hi)
                            if not (has_cur or has_next):
                                continue
                            qb0 = kb if has_cur else kb + 1
                            subs.append(dict(r=r, kb=kb, has_cur=has_cur, has_next=has_next,
                                             nq=128 * (int(has_cur) + int(has_next)), c0=0 if has_cur else 128,
                                             kt0=d * 128 * kb + r, qt0=d * 128 * qb0 + r,
                                             slot=r * nb + kb, qi=r * hb + (kb - b_lo)))
                    groups = [subs[m0:m0 + 2] for m0 in range(0, len(subs), 2)]
                    gsidx = []
                    for _g in groups:
                        gsidx.append((scount[0] % 2, scount[0] % 3))
                        scount[0] += 1

                    def stage_a(grp, sidx2):
                        sbk = SBK[sidx2[0]]
                        sidx = sidx2[1]
                        full = (len(grp) == 2 and all(u["nq"] == 256 for u in grp))
                        for i, u in enumerate(grp):
                            for h in range(2):
                                P.op("pe", lambda e, sbk=sbk, h=h, i=i, u=u, d=d: e.matmul(
                                    banks[sbk[h]][:, i * 256 + u["c0"]: i * 256 + u["c0"] + u["nq"]],
                                    lhsT=strided(KT[h * 64:(h + 1) * 64, :], u["kt0"], d, 128),
                                    rhs=strided(QT[h * 64:(h + 1) * 64, :], u["qt0"], d, u["nq"]), start=True, stop=True),
                                    reads=R_QT + R_KT, writes=[R_bank[sbk[h]]])
                        if full:
                            for h in range(2):
                                P.op("act", lambda e, sbk=sbk, h=h, sidx=sidx: e.activation(
                                    out=pT[sidx][:, h * 512:(h + 1) * 512], in_=banks[sbk[h]][:, :], func=AF.Exp, scale=0.125),
                                    reads=[R_bank[sbk[h]]], writes=[R_pT[sidx]])
                            P.op("dve", lambda e, sidx=sidx: e.tensor_tensor(out=pT[sidx][:, :], in0=pT[sidx][:, :], in1=mask2[:, :], op=ALU.mult),
                                 reads=[R_const], writes=[R_pT[sidx]])
                        else:
                            for i, u in enumerate(grp):
                                lo = i * 256 + u["c0"]
                                for h in range(2):
                                    P.op("act", lambda e, sbk=sbk, h=h, lo=lo, u=u, sidx=sidx: e.activation(
                                        out=pT[sidx][:, h * 512 + lo: h * 512 + lo + u["nq"]], in_=banks[sbk[h]][:, lo:lo + u["nq"]],
                                        func=AF.Exp, scale=0.125),
                                        reads=[R_bank[sbk[h]]], writes=[R_pT[sidx]])
                            for i, u in enumerate(grp):
                                lo = i * 256 + u["c0"]
                                pv = pT[sidx][:, :].rearrange("p (h q) -> p h q", h=2)[:, :, lo:lo + u["nq"]]
                                mv = mask2[:, :].rearrange("p (h q) -> p h q", h=2)[:, :, lo:lo + u["nq"]]
                                P.op("dve", lambda e, pv=pv, mv=mv: e.tensor_tensor(out=pv, in0=pv, in1=mv, op=ALU.mult),
                                     reads=[R_const], writes=[R_pT[sidx]])

                    def stage_b(grp, sidx2):
                        sidx = sidx2[1]
                        for i, u in enumerate(grp):
                            qi, slot, kb = u["qi"], u["slot"], u["kb"]
                            for h in range(2):
                                base = h * 512 + i * 256
                                pieces = []
                                if u["has_cur"] and u["has_next"] and qi % 4 != 3:
                                    pieces.append((qi, 2, 0))
                                else:
                                    if u["has_cur"]:
                                        pieces.append((qi, 1, 0))
                                    if u["has_next"]:
                                        pieces.append((qi + 1, 1, 128))
                                for (q0, nqb, poff) in pieces:
                                    ub = UB[h][(q0 // 4) % 2]
                                    key = (j, H, d, h, q0 // 4)
                                    fresh = key not in u_started
                                    u_started.add(key)
                                    P.op("pe", lambda e, ub=ub, q0=q0, nqb=nqb, poff=poff, slot=slot, h=h, sidx=sidx, base=base, fresh=fresh: e.matmul(
                                        banks[ub][0:66, (q0 % 4) * 128:(q0 % 4) * 128 + 128 * nqb], lhsT=Vd[:, slot, h, :],
                                        rhs=pT[sidx][:, base + poff: base + poff + 128 * nqb], start=fresh, stop=True, skip_group_check=True),
                                        reads=[R_Vd, R_pT[sidx]], writes=[R_bank[ub]])
                            if u["has_cur"] and qi % 4 == 3:
                                g = qi // 4
                                for h in range(2):
                                    ub = UB[h][g % 2]
                                    a2 = acc[0:66, h, :]
                                    if d == 1:
                                        dst = a2[:, g * 512:(g + 1) * 512]
                                        src = banks[ub][0:66, :]
                                    elif d == 4:
                                        dst = a2.rearrange("p (m r) -> p m r", r=4)[:, :, g]
                                        src = banks[ub][0:66, :]
                                    else:
                                        dst = a2.rearrange("p (m r) -> p r m", r=16)[:, g * 4:(g + 1) * 4, :]
                                        src = banks[ub][0:66, :].rearrange("p (r m) -> p r m", r=4)
                                    if di == 0:
                                        P.op("act", lambda e, dst=dst, src=src: e.copy(out=dst, in_=src), reads=[R_bank[ub]], writes=[R_acc[h]])
                                    else:
                                        P.op("dve", lambda e, dst=dst, src=src: e.tensor_tensor(out=dst, in0=src, in1=dst, op=ALU.add),
                                             reads=[R_bank[ub]], writes=[R_acc[h]])

                    for gi in range(min(2, len(groups))):
                        stage_a(groups[gi], gsidx[gi])
                    for gi in range(len(groups)):
                        if gi + 2 < len(groups):
                            stage_a(groups[gi + 2], gsidx[gi + 2])
                        stage_b(groups[gi], gsidx[gi])
                ntq = 4 if KSTOP > 3 else 0
                for tq in range(ntq):
                    lsl = slice(tq * 512, (tq + 1) * 512)
                    den = acc[64:66, 0, lsl]
                    if isB:
                        i_ = j - 4
                        P.op("dve", lambda e, i_=i_, den=den, lsl=lsl: e.scalar_tensor_tensor(out=den, in0=den, scalar=esink[64:66, i_:i_ + 1],
                                                                                         in1=acc[64:66, 1, lsl], op0=ALU.add, op1=ALU.add),
                             reads=[R_acc[0], R_acc[1], R_const], writes=[R_den[tq]])
                    else:
                        P.op("dve", lambda e, den=den, lsl=lsl: e.tensor_tensor(out=den, in0=den, in1=acc[64:66, 1, lsl], op=ALU.add),
                             reads=[R_acc[0], R_acc[1]], writes=[R_den[tq]])
                for tq in range(ntq):
                    den = acc[64:66, 0, tq * 512:(tq + 1) * 512]
                    P.op("act", lambda e, den=den: e.activation(out=den, in_=den, func=AF.Ln), writes=[R_den[tq]])
                    P.op("act", lambda e, den=den: e.activation(out=den, in_=den, func=AF.Exp, scale=-1.0), writes=[R_den[tq]])
                for tq in range(ntq):
                    bk = PB[pcount[0] % 2]
                    pcount[0] += 1
                    tile = H * 4 + tq
                    gsl = slice(tile * 512, (tile + 1) * 512)
                    lsl = slice(tq * 512, (tq + 1) * 512)
                    P.op("pe", lambda e, bk=bk, lsl=lsl: e.matmul(banks[bk][:, :], lhsT=sel2[64:66, :], rhs=acc[64:66, 0, lsl], start=True, stop=True),
                         reads=[R_den[tq], R_const], writes=[R_bank[bk]])
                    mi = tq % 2
                    P.op("dve", lambda e, bk=bk, lsl=lsl, mi=mi: e.tensor_tensor(out=mtmp[mi][0:64, :], in0=banks[bk][0:64, :], in1=acc[0:64, 0, lsl], op=ALU.mult),
                         reads=[R_bank[bk], R_acc[0]], writes=[R_m[mi]])
                    P.op("dve", lambda e, bk=bk, lsl=lsl, mi=mi: e.tensor_tensor(out=mtmp[mi][64:128, :], in0=banks[bk][64:128, :], in1=acc[0:64, 1, lsl], op=ALU.mult),
                         reads=[R_bank[bk], R_acc[1]], writes=[R_m[mi]])
                    P.op("pool", lambda e, mi=mi, gsl=gsl, j=j: e.tensor_tensor(out=mixT[:, j, gsl], in0=mtmp[mi][:, :], in1=mixT[:, j, gsl], op=ALU.mult),
                         reads=[R_m[mi]], writes=[R_mix[j][tile]])
                for tq in range(ntq):
                    P_merge_deps(R_den[tq], R_acc[0])
        allres = R_QT + R_KT + R_VT + [R_Vd] + R_acc + R_w + R_pT + R_cs + [R_t] + R_rot + R_m
        return allres, R_mix

    def out_phase(prev, R_mix):
        wo = carve([128, 8, D], BF16, reset=True)
        NB_ = 4
        xr = [carve([128, D], F32) for _ in range(NB_)]
        zz = [carve([128, D], F32) for _ in range(NB_)]
        yy = zz
        stat = [carve([128, 32], F32) for _ in range(NB_)]
        lng = carve([128, D], F32)
        lnb = carve([128, D], F32)
        R_ln = Res()
        R_wo = Res()
        R_xr = [Res() for _ in range(NB_)]
        R_z = [Res() for _ in range(NB_)]
        R_y = R_z
        R_st = [Res() for _ in range(NB_)]
        dma("pool", wo[:], d_wo.rearrange("(kc p) f -> p kc f", p=128), writes=[R_wo] + prev)
        dma("sp", lng[:], d_lng[0, :].partition_broadcast(128), writes=[R_ln])
        dma("sp", lnb[:], d_lnb[0, :].partition_broadcast(128), writes=[R_ln])
        def load_x(tb):
            if tb < 32:
                dma("sp", xr[tb % NB_][:], d_x[tb * 128:(tb + 1) * 128, :], writes=[R_xr[tb % NB_]])
        for tb in range(NB_ - 1):
            load_x(tb)
        for tb in range(32):
            i2 = tb % NB_
            tile = tb // 4
            load_x(tb + NB_ - 1)
            for hh in range(2):
                bk = 2 * (tb % 2) + hh
                for c in range(8):
                    P.op("pe", lambda e, bk=bk, c=c, tb=tb, hh=hh: e.matmul(banks[bk][:, :], lhsT=mixT[:, c, tb * 128:(tb + 1) * 128],
                                                                        rhs=wo[:, c, hh * 512:(hh + 1) * 512], start=(c == 0), stop=(c == 7)),
                         reads=[R_mix[c][tile], R_wo], writes=[R_bank[bk]])
                P.op("dve", lambda e, bk=bk, hh=hh, i2=i2: e.scalar_tensor_tensor(out=zz[i2][:, hh * 512:(hh + 1) * 512], in0=xr[i2][:, hh * 512:(hh + 1) * 512],
                                                                              scalar=ALPHA, in1=banks[bk][:, :], op0=ALU.mult, op1=ALU.add),
                     reads=[R_bank[bk], R_xr[i2]], writes=[R_z[i2]])
            layer_norm_rows(128, zz[i2], stat[i2], R_z[i2], R_st[i2], yy[i2], R_y[i2], lng, lnb, R_ln)
            out_tokens.append(dma("sp", o_y[tb * 128:(tb + 1) * 128, :], yy[i2][:], reads=[R_y[i2]]))

    PH = os.environ.get("KPH", "sample,kv,pair,out").split(",")
    NP_ = int(os.environ.get("KNP", str(NPAIR)))
    if "sample" in PH:
        r1 = sample_phase()
    load_xT()
    if "sample" in PH:
        phase_barrier(r1)
    if "kv" in PH:
        r2 = kvout_phase([])
        phase_barrier(r2)
    R_mix = [[Res() for _ in range(8)] for _ in range(8)]
    if "pair" in PH:
        r3, R_mix = pair_phase([], NP_)
        phase_barrier(r3)
    if "out" in PH:
        out_phase([], R_mix)
    Res.default_w = None
    P.q["sp"].append((None, list(out_tokens), None))

    rank = {}
    for eng in Prog.ENG:
        cnt = 0
        for idx in range(len(P.q[eng])):
            if (eng, idx) in P.needed:
                cnt += 1
                rank[(eng, idx)] = cnt

    def replay(eng, e):
        waited = {}
        for idx, (fn, ws, dtok) in enumerate(P.q[eng]):
            for w in ws:
                if w[0] == "dma":
                    key, val, sem = ("dma", w[1]), w[2], sem_dma[w[1]]
                else:
                    key, val, sem = w[0], rank[w], sem_eng[w[0]]
                if waited.get(key, 0) >= val:
                    continue
                e.wait_ge(sem, val)
                waited[key] = val
            if fn is None:
                continue
            ins = fn(e)
            if dtok is not None:
                ins.then_inc(sem_dma[dtok[1]], 16)
            elif (eng, idx) in P.needed:
                ins.then_inc(sem_eng[eng], 1)

    with nc.Block() as block:
        @block.tensor
        def _(e):
            replay("pe", e)

        @block.scalar
        def _(e):
            replay("act", e)

        @block.vector
        def _(e):
            replay("dve", e)

        @block.gpsimd
        def _(e):
            replay("pool", e)

        @block.sync
        def _(e):
            replay("sp", e)
    es.close()
    return nc, consts


def kernel(x_prompt, x_sample, cache_a_kv, cache_b_kv, w_in, attn_sinks, w_o, ln_g, ln_b):
    x_prompt = np.asarray(x_prompt, np.float32)
    x_sample = np.asarray(x_sample, np.float32)
    cache_a_kv = np.asarray(cache_a_kv, np.float32)
    cache_b_kv = np.asarray(cache_b_kv, np.float32)
    w_in0 = np.asarray(w_in, np.float32)[0]
    w_o0 = np.asarray(w_o, np.float32)[0]
    sinks = np.asarray(attn_sinks, np.float32)[0]
    if "nc" not in _CACHE:
        _CACHE["nc"] = build()
    nc, consts = _CACHE["nc"]
    in_maps = _prep(consts, x_prompt, x_sample, cache_a_kv, cache_b_kv, w_in0, w_o0, sinks, ln_g, ln_b)
    return _run(nc, in_maps)


def _prep(consts, x_prompt, x_sample, cache_a_kv, cache_b_kv, w_in0, w_o0, sinks, ln_g, ln_b, cores=range(NCORES)):

    cols = []
    for j in range(NPAIR):
        cols += _pair_cols(j)
    wc = np.ascontiguousarray(w_in0[:, cols])
    wkv = np.ascontiguousarray(np.concatenate([w_in0[:, 512:1536], w_in0[:, 2560:2816]], axis=1))
    wo_p = np.ascontiguousarray(w_o0[_wo_rows(), :])
    sinkp = np.zeros((128, 4), np.float32)
    sinkp[64, :] = sinks[0:4]
    sinkp[65, :] = sinks[4:8]
    shared = {
        "wc": wc, "wkv": wkv, "wo": wo_p, "wos": np.ascontiguousarray(w_o0), "wins": np.ascontiguousarray(w_in0),
        "lng": np.ascontiguousarray(np.asarray(ln_g, np.float32).reshape(1, D)),
        "lnb": np.ascontiguousarray(np.asarray(ln_b, np.float32).reshape(1, D)),
        "sink": np.ascontiguousarray(sinks.reshape(1, 8)), "sinkp": sinkp,
    }
    for k, v in consts.items():
        shared["c_" + k] = v
    in_maps = []
    for c in cores:
        m = dict(shared)
        m["xT"] = np.ascontiguousarray(x_prompt[c].T)
        m["x"] = np.ascontiguousarray(x_prompt[c])
        xs = x_sample[c * NS:(c + 1) * NS, 0, :]
        m["xsT"] = np.ascontiguousarray(xs.T)
        m["xs"] = np.ascontiguousarray(xs)
        m["ca"] = np.ascontiguousarray(cache_a_kv[0, c * NS:(c + 1) * NS].reshape(NS, 2048, 1024))
        m["cb"] = np.ascontiguousarray(cache_b_kv[0, c * NS:(c + 1) * NS].reshape(NS, 128, 256))
        in_maps.append(m)
    return in_maps


def _run(nc, in_maps):
    res = run_bass_kernel_spmd(nc, in_maps, core_ids=list(range(NCORES)))
    rs = res.results
    y = np.stack([rs[c]["y"] for c in range(NCORES)]).reshape(8, S, D)
    ys = np.concatenate([rs[c]["ys"] for c in range(NCORES)]).reshape(128, 1, D)
    pa = np.stack([rs[c]["pkva"] for c in range(NCORES)]).reshape(1, 8, 2048, 2, 8, 64)
    pb = np.stack([rs[c]["pkvb"] for c in range(NCORES)]).reshape(1, 8, 128, 2, 2, 64)
    sa = np.concatenate([rs[c]["skva"] for c in range(NCORES)]).reshape(1, 128, 1, 2, 8, 64)
    sb_ = np.concatenate([rs[c]["skvb"] for c in range(NCORES)]).reshape(1, 128, 1, 2, 2, 64)
    return (y.astype(np.float32), ys.astype(np.float32), pa.astype(np.float32), pb.astype(np.float32),
            sa.astype(np.float32), sb_.astype(np.float32))
```

```python
import os
import numpy as np
import concourse.bass as bass
import concourse.mybir as mybir
from concourse.bass_utils import run_bass_kernel_spmd

F32 = mybir.dt.float32
BF16 = mybir.dt.bfloat16
ALU = mybir.AluOpType
AF = mybir.ActivationFunctionType
AX = mybir.AxisListType

NCORES = 8
D = 1024
S = 4096
NS = 16
PAST = 16384
ALPHA = 2.0 ** 0.25
EPS = 1e-5
DILS = (1, 4, 16)
NPAIR = 8
NDMA_SEM = 24

_CACHE = {}


class Res:
    __slots__ = ("w", "r")
    default_w = None

    def __init__(self):
        self.w = Res.default_w
        self.r = []


class Prog:
    ENG = ("pe", "act", "dve", "pool", "sp")

    def __init__(self):
        self.q = {e: [] for e in self.ENG}
        self.needed = set()
        self.dma_slot_tok = [None] * NDMA_SEM
        self.dma_slot_cnt = [0] * NDMA_SEM
        self.dma_next = {"sp": 0, "pool": 0, "act": 0}
        self.dma_rng = {"sp": (0, 14), "pool": (14, 10), "act": (0, 14)}

    def emit(self, eng, fn, waits, dma=False):
        ws = [w for w in waits if w is not None]
        if dma:
            base, cnt = self.dma_rng[eng]
            slot = base + self.dma_next[eng]
            self.dma_next[eng] = (self.dma_next[eng] + 1) % cnt
            if self.dma_slot_tok[slot] is not None:
                ws.append(self.dma_slot_tok[slot])
            self.dma_slot_cnt[slot] += 16
            tok = ("dma", slot, self.dma_slot_cnt[slot])
            self.dma_slot_tok[slot] = tok
        else:
            tok = (eng, len(self.q[eng]))
        ws2 = []
        for w in ws:
            if eng == "pe" and w[0] == "pe":
                continue
            ws2.append(w)
            if w[0] != "dma":
                self.needed.add(w)
        self.q[eng].append((fn, ws2, tok if dma else None))
        return tok

    def op(self, eng, fn, reads=(), writes=(), dma=False, extra=()):
        ws = list(extra)
        for r in reads:
            ws.append(r.w)
        for r in writes:
            ws.append(r.w)
            ws.extend(r.r)
        tok = self.emit(eng, fn, ws, dma)
        for r in reads:
            if tok[0] != "dma":
                r.r = [t for t in r.r if t[0] != tok[0]]
            r.r.append(tok)
        for r in writes:
            r.w = tok
            r.r = []
        return tok


def _consts():
    c = {}
    c["ident_bf"] = np.eye(128, dtype=np.float32)
    k = np.arange(128)[:, None]
    q = np.arange(256)[None, :]
    m = np.where(q < 128, (q >= k), ((q - 128) <= k)).astype(np.float32)
    c["mask2"] = np.concatenate([m, m, m, m], axis=1)
    inv = 500000.0 ** (-np.arange(0, 16, 2, dtype=np.float32) / 16.0)
    inv = inv.astype(np.float32)
    pos = np.arange(S, dtype=np.float32)
    ang = pos[:, None] * inv[None, :]
    cos = np.cos(ang).astype(np.float32)
    sin = np.sin(ang).astype(np.float32)
    rows = np.arange(64)
    f = rows % 8
    sign = np.where((rows % 16) < 8, -1.0, 1.0).astype(np.float32)
    c["CC"] = np.ascontiguousarray(cos[:, f].T)
    c["SS"] = np.ascontiguousarray((sin[:, f] * sign[None, :]).T)
    c["cosT"] = np.ascontiguousarray(cos[2048:].reshape(16, 128, 8).transpose(1, 0, 2))
    c["sinT"] = np.ascontiguousarray(sin[2048:].reshape(16, 128, 8).transpose(1, 0, 2))
    c["cosTb"] = np.ascontiguousarray(cos[S - 128:])
    c["sinTb"] = np.ascontiguousarray(sin[S - 128:])
    angs = (np.float32(PAST) * inv).astype(np.float32)
    c["cs_s"] = np.repeat(np.concatenate([np.cos(angs), np.sin(angs)])[None, :].astype(np.float32), NS, 0)
    misc = np.zeros((128, 128), np.float32)
    misc[64, 0:64] = 1.0
    misc[65, 64:128] = 1.0
    c["sel2"] = misc
    c["neg1"] = -np.ones((128, 512), np.float32)
    en = np.zeros((8, NS, NS), np.float32)
    for n in range(NS):
        en[:, n, n] = 1.0
    c["en"] = en.reshape(8, NS * NS)
    bd = np.zeros((8, 8, 64), np.float32)
    for h in range(8):
        bd[h, h, :] = 1.0
    c["bd"] = bd.reshape(8, 512)
    c["i8"] = np.eye(8, dtype=np.float32)
    c["i16"] = np.eye(16, dtype=np.float32)
    c["onescol"] = np.ones((128, 1), np.float32)
    return c


CONST_SHAPES = None


def _pair_cols(j):
    def rope_rows(base_a, base_b):
        return list(range(base_a, base_a + 16)) + list(range(base_b, base_b + 16))

    def swap16(cols):
        out = []
        for g in range(0, len(cols), 16):
            blk = cols[g:g + 16]
            out += blk[8:16] + blk[0:8]
        return out
    if j < 4:
        ha, hb = 2 * j, 2 * j + 1
        q = list(range(0 + 128 * j, 128 * j + 128))
        k = list(range(512 + 128 * j, 512 + 128 * j + 128))
        v = list(range(1024 + 128 * j, 1024 + 128 * j + 128))
        g = list(range(1536 + 128 * j, 1536 + 128 * j + 128))
        r = rope_rows(64 * ha, 64 * hb) + rope_rows(512 + 64 * ha, 512 + 64 * hb)
    else:
        i = j - 4
        ha, hb = i, 4 + i
        q = list(range(2048 + 64 * ha, 2048 + 64 * ha + 64)) + list(range(2048 + 64 * hb, 2048 + 64 * hb + 64))
        k = list(range(2560, 2688))
        v = list(range(2688, 2816))
        g = list(range(2816 + 64 * ha, 2816 + 64 * ha + 64)) + list(range(2816 + 64 * hb, 2816 + 64 * hb + 64))
        r = rope_rows(2048 + 64 * ha, 2048 + 64 * hb) + rope_rows(2560, 2560 + 64)
    r = r + swap16(r)
    return q + k + v + g + r


def _wo_rows():
    rows = list(range(512))
    for i in range(4):
        ha, hb = i, 4 + i
        rows += list(range(512 + 64 * ha, 512 + 64 * ha + 64)) + list(range(512 + 64 * hb, 512 + 64 * hb + 64))
    return rows


def build():
    nc = bass.Bass("TRN2", target_bir_lowering=False)
    consts = _consts()
    P = Prog()

    def din(name, shape):
        return nc.dram_tensor(name, list(shape), F32, kind="ExternalInput").ap()

    def dout(name, shape):
        return nc.dram_tensor(name, list(shape), F32, kind="ExternalOutput").ap()

    d_xT = din("xT", [D, S])
    d_x = din("x", [S, D])
    d_wc = din("wc", [D, NPAIR * 640])
    d_wkv = din("wkv", [D, 1280])
    d_wo = din("wo", [D, D])
    d_wos = din("wos", [D, D])
    d_wins = din("wins", [D, 3328])
    d_lng = din("lng", [1, D])
    d_lnb = din("lnb", [1, D])
    d_sink = din("sink", [1, 8])
    d_sinkp = din("sinkp", [128, 4])
    d_xsT = din("xsT", [D, NS])
    d_xs = din("xs", [NS, D])
    d_ca = din("ca", [NS, 2048, 1024])
    d_cb = din("cb", [NS, 128, 256])
    dc = {k: din("c_" + k, v.shape) for k, v in consts.items()}
    o_y = dout("y", [S, D])
    o_pa = dout("pkva", [2048, 1024])
    o_pb = dout("pkvb", [128, 256])
    o_ys = dout("ys", [NS, D])
    o_sa = dout("skva", [NS, 1024])
    o_sb = dout("skvb", [NS, 256])

    import contextlib
    es = contextlib.ExitStack()

    def sb(name, shape, dt=F32):
        return es.enter_context(nc.sbuf_tensor(name, list(shape), dt))

    xT = sb("xT_sb", [128, 8, S], BF16)
    mixT = sb("mixT", [128, 8, S], BF16)
    ident = sb("ident", [128, 128], BF16)
    mask2 = sb("mask2", [128, 1024], BF16)
    sel2 = sb("sel2", [128, 128], F32)
    onescol = sb("onescol", [128, 1], F32)
    esink = sb("esink", [128, 4], F32)
    bar = sb("bar", [128, 2], F32)
    SCR = 76 * 1024
    scr = sb("scr", [128, SCR // 2], BF16)
    scr_off = [0]

    def carve(shape, dt, reset=False):
        if reset:
            scr_off[0] = 0
        n = int(np.prod(shape[1:]))
        esz = 2 if dt == BF16 else 4
        nbytes = n * esz
        off = (scr_off[0] + 3) // 4 * 4
        assert off + nbytes <= SCR, (off, nbytes)
        scr_off[0] = off + nbytes
        ap = scr[:, off // 2: (off + nbytes) // 2]
        if dt != BF16:
            ap = ap.bitcast(dt)
        if len(shape) > 2:
            names = " ".join("a%d" % i for i in range(len(shape) - 1))
            kw = {"a%d" % i: shape[i + 1] for i in range(len(shape) - 1)}
            ap = ap.rearrange("p (%s) -> p %s" % (names, names), **kw)
        return ap

    banks = [es.enter_context(nc.psum_tensor("ps%d" % i, [128, 512], F32)) for i in range(8)]
    R_bank = [Res() for _ in range(8)]
    sem_eng = {e: es.enter_context(nc.semaphore("s_" + e)) for e in Prog.ENG}
    sem_dma = [es.enter_context(nc.semaphore("s_dma%d" % i)) for i in range(NDMA_SEM)]

    out_tokens = []
    Res.default_w = None

    def phase_barrier(prev):
        tok = P.op("dve", lambda e: e.memset(bar[:, 0:1], 0.0), writes=list(prev))
        Res.default_w = tok

    def dma(eng, out, in_, reads=(), writes=(), **kw):
        return P.op(eng, lambda e: e.dma_start(out=out, in_=in_, **kw), reads, writes, dma=True)

    def strided(ap2, start, step, n):
        if step == 1:
            return ap2[:, start:start + n]
        v = ap2.rearrange("p (m r) -> p m r", r=step)
        return v[:, start // step: start // step + n, start % step]

    R_const = Res()
    R_xT = [Res() for _ in range(8)]
    dma("pool", ident[:], dc["ident_bf"][:, :], writes=[R_const])
    dma("pool", mask2[:], dc["mask2"][:, :], writes=[R_const])
    dma("sp", sel2[:], dc["sel2"][:, :], writes=[R_const])
    dma("sp", onescol[:], dc["onescol"][:, :], writes=[R_const])
    dma("sp", esink[:], d_sinkp[:, :], writes=[R_const])
    P.op("act", lambda e: e.activation(out=esink[:], in_=esink[:], func=AF.Exp), writes=[R_const])
    d_xT3 = d_xT.rearrange("(kc p) t -> p kc t", p=128)
    xT_next = [0]

    def load_xT(npieces=16):
        for _ in range(npieces):
            i = xT_next[0]
            if i >= 16:
                return
            xT_next[0] += 1
            kc, hh = i // 2, i % 2
            dma("pool", xT[:, kc, hh * 2048:(hh + 1) * 2048], d_xT3[:, kc, hh * 2048:(hh + 1) * 2048],
                writes=[R_xT[kc]], max_dma_last_dim=8192)

    def layer_norm_rows(np_, zs, stat, R_z, R_stat, ys, R_y, lng, lnb, R_ln, pool_affine=True):
        for hh in range(2):
            P.op("dve", lambda e, hh=hh: e.bn_stats(out=stat[0:np_, hh * 6:(hh + 1) * 6], in_=zs[0:np_, hh * 512:(hh + 1) * 512]),
                 reads=[R_z], writes=[R_stat])
        P.op("dve", lambda e: e.bn_aggr(out=stat[0:np_, 12:14], in_=stat[0:np_, 0:12]), writes=[R_stat])
        P.op("dve", lambda e: e.tensor_scalar(out=stat[0:np_, 14:15], in0=stat[0:np_, 13:14], scalar1=EPS, scalar2=None, op0=ALU.add),
             writes=[R_stat])
        P.op("act", lambda e: e.activation(out=stat[0:np_, 15:16], in_=stat[0:np_, 14:15], func=AF.Sqrt), writes=[R_stat])
        P.op("dve", lambda e: e.reciprocal(out=stat[0:np_, 16:17], in_=stat[0:np_, 15:16]), writes=[R_stat])
        P.op("dve", lambda e: e.tensor_scalar(out=stat[0:np_, 17:18], in0=stat[0:np_, 12:13], scalar1=stat[0:np_, 16:17], scalar2=-1.0,
                                              op0=ALU.mult, op1=ALU.mult), writes=[R_stat])
        P.op("act", lambda e: e.activation(out=zs[0:np_, :], in_=zs[0:np_, :], func=AF.Identity, scale=stat[0:np_, 16:17], bias=stat[0:np_, 17:18]),
             reads=[R_stat], writes=[R_z])
        eng = "pool" if pool_affine else "dve"
        P.op(eng, lambda e: e.tensor_tensor(out=zs[0:np_, :], in0=zs[0:np_, :], in1=lng[0:np_, :], op=ALU.mult), reads=[R_ln], writes=[R_z])
        P.op(eng, lambda e: e.tensor_tensor(out=ys[0:np_, :], in0=zs[0:np_, :], in1=lnb[0:np_, :], op=ALU.add), reads=[R_ln, R_z], writes=[R_y])

    def sample_phase():
        hs = carve([128, 3328], F32, reset=True)
        xsT = carve([128, 8, NS], BF16)
        wsb = [carve([128, 8, 512], BF16) for _ in range(2)]
        kvt = [carve([128, 1024], F32) for _ in range(2)]
        kvt += [wsb[1][:, 0:4, :].rearrange("p k f -> p (k f)").bitcast(F32), wsb[1][:, 4:8, :].rearrange("p k f -> p (k f)").bitcast(F32)]
        kvb = [carve([128, 256], F32) for _ in range(4)]
        prod = carve([128, 512], F32)
        s8 = [carve([128, 8], F32) for _ in range(2)]
        p8 = [carve([128, 8], BF16) for _ in range(2)]
        vbf = [carve([128, 512], BF16) for _ in range(2)]
        R_vbf = [Res(), Res()]
        qbf = carve([128, 1024], BF16)
        i16b = carve([128, 16], BF16)
        onesb = carve([128, 2], BF16)
        masked = [carve([128, 512], F32) for _ in range(2)]
        dd8 = [carve([128, 8], F32) for _ in range(2)]
        en = carve([128, NS * NS], F32)
        bd = carve([128, 512], F32)
        i8 = carve([128, 8], F32)
        i16 = carve([128, 16], F32)
        css = carve([128, 16], F32)
        tmp = [carve([128, 16, 8], F32) for _ in range(4)]
        small = carve([128, 64], F32)
        mixs = carve([128, D], F32)
        mixsT = carve([128, 8, NS], BF16)
        sink16 = carve([128, 8], F32)
        xs_sb = carve([128, D], F32)
        stat = carve([128, 32], F32)
        lng = carve([128, D], F32)
        lnb = carve([128, D], F32)
        R_ln = Res()
        R = {k: Res() for k in ("hs", "xsT", "c", "prod", "small", "mixs", "mixsT", "xs", "stat", "ys", "tmp")}
        R_wsb = [Res(), Res()]
        R_kvt = [Res() for _ in range(4)]
        R_kvb = [Res() for _ in range(4)]
        R_s8 = [Res(), Res()]
        R_p8 = [Res(), Res()]
        R_msk = [Res(), Res()]
        R_dd = [Res(), Res()]

        dma("pool", xsT[:], d_xsT.rearrange("(kc p) n -> p kc n", p=128), writes=[R["xsT"]])
        dma("sp", lng[0:NS, :], d_lng[0, :].partition_broadcast(NS), writes=[R_ln])
        dma("sp", lnb[0:NS, :], d_lnb[0, :].partition_broadcast(NS), writes=[R_ln])
        dma("sp", en[0:8, :], dc["en"][:, :], writes=[R["c"]])
        dma("sp", bd[0:8, :], dc["bd"][:, :], writes=[R["c"]])
        dma("sp", i8[0:8, :], dc["i8"][:, :], writes=[R["c"]])
        dma("sp", i16[0:16, :], dc["i16"][:, :], writes=[R["c"]])
        dma("pool", i16b[0:16, :], dc["i16"][:, :], writes=[R["c"]])
        dma("pool", onesb[:, 0:1], dc["onescol"][:, :], writes=[R["c"]])
        dma("sp", css[0:NS, :], dc["cs_s"][:, :], writes=[R["c"]])
        dma("sp", sink16[0:NS, :], d_sink[0, :].partition_broadcast(NS), writes=[R["c"]])
        dma("sp", xs_sb[0:NS, :], d_xs[:, :], writes=[R["xs"]])
        P.op("act", lambda e: e.activation(out=sink16[0:NS, :], in_=sink16[0:NS, :], func=AF.Exp), writes=[R["c"]])

        d_w3 = d_wins.rearrange("(kc p) f -> p kc f", p=128)
        chunks = [(0, 512), (512, 512), (1024, 512), (1536, 512), (2048, 512), (2560, 512), (3072, 256)]
        for ci, (c0, cw) in enumerate(chunks):
            wb = wsb[ci % 2]
            dma("pool", wb[:, :, 0:cw], d_w3[:, :, c0:c0 + cw], writes=[R_wsb[ci % 2]])
            bk = ci % 2
            for kc in range(8):
                P.op("pe", lambda e, kc=kc, wb=wb, cw=cw, bk=bk: e.matmul(banks[bk][0:NS, 0:cw], lhsT=xsT[:, kc, :], rhs=wb[:, kc, 0:cw],
                                                                         start=(kc == 0), stop=(kc == 7)),
                     reads=[R["xsT"], R_wsb[ci % 2]], writes=[R_bank[bk]])
            P.op("act", lambda e, c0=c0, cw=cw, bk=bk: e.copy(out=hs[0:NS, c0:c0 + cw], in_=banks[bk][0:NS, 0:cw]),
                 reads=[R_bank[bk]], writes=[R["hs"]])

        def rope_tm(np_, view, nh, cosb, sinb):
            x1 = view[:, :, 0:8]
            x2 = view[:, :, 8:16]
            t = [tt[0:np_, 0:nh, :] for tt in tmp]
            P.op("dve", lambda e: e.tensor_tensor(out=t[0], in0=x1, in1=cosb, op=ALU.mult), reads=[R["hs"], R["c"]], writes=[R["tmp"]])
            P.op("dve", lambda e: e.tensor_tensor(out=t[1], in0=x2, in1=sinb, op=ALU.mult), reads=[R["hs"]], writes=[R["tmp"]])
            P.op("dve", lambda e: e.tensor_tensor(out=t[2], in0=x2, in1=cosb, op=ALU.mult), reads=[R["hs"]], writes=[R["tmp"]])
            P.op("dve", lambda e: e.tensor_tensor(out=t[3], in0=x1, in1=sinb, op=ALU.mult), reads=[R["hs"]], writes=[R["tmp"]])
            P.op("dve", lambda e: e.tensor_tensor(out=x1, in0=t[0], in1=t[1], op=ALU.subtract), reads=[R["tmp"]], writes=[R["hs"]])
            P.op("dve", lambda e: e.tensor_tensor(out=x2, in0=t[2], in1=t[3], op=ALU.add), reads=[R["tmp"]], writes=[R["hs"]])

        for (c0, nh) in ((0, 16), (2048, 8), (2560, 2)):
            view = hs[0:NS, c0:c0 + nh * 64].rearrange("p (h d) -> p h d", d=64)
            cosb = css[0:NS, 0:8].unsqueeze(1).broadcast_to([NS, nh, 8])
            sinb = css[0:NS, 8:16].unsqueeze(1).broadcast_to([NS, nh, 8])
            rope_tm(NS, view, nh, cosb, sinb)
        R["qbf"] = Res()
        P.op("act", lambda e: e.copy(out=qbf[0:NS, 0:512], in_=hs[0:NS, 0:512]), reads=[R["hs"]], writes=[R["qbf"]])
        P.op("act", lambda e: e.copy(out=qbf[0:NS, 512:1024], in_=hs[0:NS, 2048:2560]), reads=[R["hs"]], writes=[R["qbf"]])
        out_tokens.append(dma("sp", o_sa[:, :], hs[0:NS, 512:1536], reads=[R["hs"]]))
        out_tokens.append(dma("sp", o_sb[:, :], hs[0:NS, 2560:2816], reads=[R["hs"]]))

        B_QBC, B_OUT, B_DEN, B_ACC, B_ACD = 2, 3, 4, 5, 6
        it = 0
        for mixer in ("A", "B"):
            qoff = 0 if mixer == "A" else 2048
            QB = (2, 7)

            def emit_qbc(n, qoff=qoff):
                bq = QB[n % 2]
                qo2 = 0 if qoff == 0 else 512
                P.op("pe", lambda e, n=n, qo2=qo2, bq=bq: e.matmul(banks[bq][:, :], lhsT=i16b[0:NS, n:n + 1].broadcast_to([NS, 128]),
                                                                   rhs=qbf[0:NS, qo2:qo2 + 512], start=True, stop=True),
                     reads=[R["qbf"], R["c"]], writes=[R_bank[bq]])
            emit_qbc(0)
            for n in range(NS):
                B_QBC = QB[n % 2]
                dl = DILS if mixer == "A" else (1,)
                for di, d in enumerate(dl):
                    i2 = it % 2
                    if mixer == "A":
                        kb_ = it % 4
                        t_kv = kvt[kb_]
                        rk = R_kvt[kb_]
                        src = d_ca[n, 2048 - 128 * d:2048, :].rearrange("(j r) f -> j r f", r=d)[:, 0, :]
                        dma("sp", t_kv[:], src, writes=[rk] + ([R_wsb[1]] if (kb_ >= 2 and it < 4) else []))
                        kk = t_kv[:, 0:512]
                        vv = t_kv[:, 512:1024]
                        qq = banks[B_QBC][:, :]
                        nv = 512
                        P.op("dve", lambda e, kk=kk, qq=qq: e.tensor_tensor(out=prod[:], in0=qq, in1=kk, op=ALU.mult),
                             reads=[rk, R_bank[B_QBC]], writes=[R["prod"]])
                    else:
                        t_kv = kvb[it % 4]
                        rk = R_kvb[it % 4]
                        dma("sp", t_kv[:], d_cb[n, :, :], writes=[rk])
                        kk = t_kv[:, 0:128].rearrange("p (g d) -> p g d", g=2).unsqueeze(2).broadcast_to([128, 2, 4, 64])
                        vv = t_kv[:, 128:256]
                        qq = banks[B_QBC][:, :].rearrange("p (g u d) -> p g u d", g=2, u=4)
                        nv = 128
                        P.op("dve", lambda e, kk=kk, qq=qq: e.tensor_tensor(out=prod[:].rearrange("p (g u d) -> p g u d", g=2, u=4),
                                                                            in0=qq, in1=kk, op=ALU.mult),
                             reads=[rk, R_bank[B_QBC]], writes=[R["prod"]])
                    P.op("dve", lambda e, i2=i2: e.tensor_reduce(out=s8[i2][:], in_=prod[:].rearrange("p (h d) -> p h d", d=64),
                                                                 axis=AX.X, op=ALU.add),
                         reads=[R["prod"]], writes=[R_s8[i2]])
                    P.op("act", lambda e, i2=i2: e.activation(out=p8[i2][:], in_=s8[i2][:], func=AF.Exp, scale=0.125),
                         reads=[R_s8[i2]], writes=[R_p8[i2]])
                    first = (di == 0)
                    last = (di == len(dl) - 1)
                    P.op("act", lambda e, i2=i2, vv=vv, nv=nv: e.copy(out=vbf[i2][:, 0:nv], in_=vv), reads=[rk], writes=[R_vbf[i2]])
                    P.op("pe", lambda e, i2=i2, nv=nv, first=first, last=last: e.matmul(
                        banks[B_OUT][0:8, 0:nv], lhsT=p8[i2][:], rhs=vbf[i2][:, 0:nv], start=first, stop=last),
                        reads=[R_p8[i2], R_vbf[i2]], writes=[R_bank[B_OUT]])
                    P.op("pe", lambda e, i2=i2, first=first, last=last: e.matmul(
                        banks[B_DEN][0:8, 0:1], lhsT=p8[i2][:], rhs=onesb[:, 0:1], start=first, stop=last),
                        reads=[R_p8[i2], R["c"]], writes=[R_bank[B_DEN]])
                    it += 1
                if n + 1 < NS:
                    emit_qbc(n + 1)
                if mixer == "A":
                    load_xT(1)
                m2 = n % 2
                if mixer == "A":
                    P.op("dve", lambda e, m2=m2: e.tensor_tensor(out=masked[m2][0:8, :], in0=banks[B_OUT][0:8, :], in1=bd[0:8, :], op=ALU.mult),
                         reads=[R_bank[B_OUT], R["c"]], writes=[R_msk[m2]])
                else:
                    src = banks[B_OUT][0:8, 0:128].rearrange("p (g d) -> p g d", g=2).unsqueeze(2).broadcast_to([8, 2, 4, 64])
                    P.op("dve", lambda e, m2=m2, src=src: e.tensor_tensor(out=masked[m2][0:8, :].rearrange("p (g u d) -> p g u d", g=2, u=4),
                                                                          in0=src, in1=bd[0:8, :].rearrange("p (g u d) -> p g u d", g=2, u=4),
                                                                          op=ALU.mult),
                         reads=[R_bank[B_OUT], R["c"]], writes=[R_msk[m2]])
                P.op("dve", lambda e, m2=m2: e.tensor_scalar(out=dd8[m2][0:8, :], in0=i8[0:8, :], scalar1=banks[B_DEN][0:8, 0:1], scalar2=None,
                                                            op0=ALU.mult),
                     reads=[R_bank[B_DEN], R["c"]], writes=[R_dd[m2]])
                P.op("pe", lambda e, n=n, m2=m2: e.matmul(banks[B_ACC][0:NS, :], lhsT=en[0:8, n * NS:(n + 1) * NS], rhs=masked[m2][0:8, :],
                                                          start=(n == 0), stop=(n == NS - 1)),
                     reads=[R_msk[m2], R["c"]], writes=[R_bank[B_ACC]])
                P.op("pe", lambda e, n=n, m2=m2: e.matmul(banks[B_ACD][0:NS, 0:8], lhsT=en[0:8, n * NS:(n + 1) * NS], rhs=dd8[m2][0:8, :],
                                                          start=(n == 0), stop=(n == NS - 1)),
                     reads=[R_dd[m2], R["c"]], writes=[R_bank[B_ACD]])
            if mixer == "A":
                qv = hs[0:NS, 0:512]
                kv_ = hs[0:NS, 512:1024]
                vnew = hs[0:NS, 1024:1536].rearrange("p (h d) -> p h d", d=64)
                gcol, mcol, wnew = 1536, 0, 3.0
                P.op("dve", lambda e: e.tensor_tensor(out=prod[0:NS, :], in0=qv, in1=kv_, op=ALU.mult), reads=[R["hs"]], writes=[R["prod"]])
            else:
                qv = hs[0:NS, 2048:2560].rearrange("p (g u d) -> p g u d", g=2, u=4)
                kv_ = hs[0:NS, 2560:2688].rearrange("p (g d) -> p g d", g=2).unsqueeze(2).broadcast_to([NS, 2, 4, 64])
                vnew = hs[0:NS, 2688:2816].rearrange("p (g d) -> p g d", g=2).unsqueeze(2).broadcast_to([NS, 2, 4, 64])
                gcol, mcol, wnew = 2816, 512, 1.0
                P.op("dve", lambda e: e.tensor_tensor(out=prod[0:NS, :].rearrange("p (g u d) -> p g u d", g=2, u=4), in0=qv, in1=kv_, op=ALU.mult),
                     reads=[R["hs"]], writes=[R["prod"]])
            sn = small[0:NS, 0:8]
            pn = small[0:NS, 8:16]
            dn = small[0:NS, 16:24]
            rn = small[0:NS, 24:32]
            P.op("dve", lambda e: e.tensor_reduce(out=sn, in_=prod[0:NS, :].rearrange("p (h d) -> p h d", d=64), axis=AX.X, op=ALU.add),
                 reads=[R["prod"]], writes=[R["small"]])
            P.op("act", lambda e: e.activation(out=pn, in_=sn, func=AF.Exp, scale=0.125), writes=[R["small"]])
            P.op("dve", lambda e, wnew=wnew: e.scalar_tensor_tensor(out=dn, in0=pn, scalar=wnew, in1=banks[B_ACD][0:NS, 0:8],
                                                                   op0=ALU.mult, op1=ALU.add),
                 reads=[R_bank[B_ACD]], writes=[R["small"]])
            if mixer == "B":
                P.op("dve", lambda e: e.tensor_tensor(out=dn, in0=dn, in1=sink16[0:NS, :], op=ALU.add), reads=[R["c"]], writes=[R["small"]])
            P.op("dve", lambda e: e.reciprocal(out=rn, in_=dn), writes=[R["small"]])
            pnb = pn.unsqueeze(2).broadcast_to([NS, 8, 64])
            rnb = rn.unsqueeze(2).broadcast_to([NS, 8, 64])
            if mixer == "A":
                pv3 = prod[0:NS, :].rearrange("p (h d) -> p h d", d=64)
                P.op("dve", lambda e: e.tensor_tensor(out=pv3, in0=vnew, in1=pnb, op=ALU.mult), reads=[R["hs"], R["small"]], writes=[R["prod"]])
            else:
                pv4 = prod[0:NS, :].rearrange("p (g u d) -> p g u d", g=2, u=4)
                pnb4 = pn.rearrange("p (g u) -> p g u", g=2).unsqueeze(3).broadcast_to([NS, 2, 4, 64])
                P.op("dve", lambda e: e.tensor_tensor(out=pv4, in0=vnew, in1=pnb4, op=ALU.mult), reads=[R["hs"], R["small"]], writes=[R["prod"]])
            P.op("dve", lambda e, wnew=wnew: e.scalar_tensor_tensor(out=prod[0:NS, :], in0=prod[0:NS, :], scalar=wnew, in1=banks[B_ACC][0:NS, :],
                                                                   op0=ALU.mult, op1=ALU.add),
                 reads=[R_bank[B_ACC]], writes=[R["prod"]])
            P.op("dve", lambda e: e.tensor_tensor(out=prod[0:NS, :].rearrange("p (h d) -> p h d", d=64),
                                                  in0=prod[0:NS, :].rearrange("p (h d) -> p h d", d=64), in1=rnb, op=ALU.mult),
                 reads=[R["small"]], writes=[R["prod"]])
            P.op("act", lambda e, gcol=gcol: e.activation(out=hs[0:NS, gcol:gcol + 512], in_=hs[0:NS, gcol:gcol + 512], func=AF.Silu),
                 writes=[R["hs"]])
            P.op("dve", lambda e, gcol=gcol, mcol=mcol: e.tensor_tensor(out=mixs[0:NS, mcol:mcol + 512], in0=prod[0:NS, :],
                                                                       in1=hs[0:NS, gcol:gcol + 512], op=ALU.mult),
                 reads=[R["prod"], R["hs"]], writes=[R["mixs"]])
        wos = [carve([128, 4, D], BF16, reset=(i == 0)) for i in range(2)]
        R_wos = [Res(), Res()]
        d_wos3 = d_wos.rearrange("(kc p) f -> p kc f", p=128)
        dep_all = [R["hs"], R["xsT"], R_wsb[0], R_wsb[1], R["mixs"]]
        for i in range(2):
            dma("pool", wos[i][:], d_wos3[:, i * 4:(i + 1) * 4, :], reads=[], writes=[R_wos[i]] + (dep_all if i == 0 else []))
        for c in range(8):
            P.op("pe", lambda e, c=c: e.transpose(out=banks[2][:, c * NS:(c + 1) * NS], in_=mixs[0:NS, c * 128:(c + 1) * 128], identity=i16[0:NS, 0:NS]),
                 reads=[R["mixs"], R["c"]], writes=[R_bank[2]])
        P.op("dve", lambda e: e.tensor_copy(out=mixsT[:].rearrange("p c n -> p (c n)"), in_=banks[2][:, 0:8 * NS]),
             reads=[R_bank[2]], writes=[R["mixsT"]])
        for hh in range(2):
            for c in range(8):
                P.op("pe", lambda e, c=c, hh=hh: e.matmul(banks[hh][0:NS, :], lhsT=mixsT[:, c, :], rhs=wos[c // 4][:, c % 4, hh * 512:(hh + 1) * 512],
                                                          start=(c == 0), stop=(c == 7)),
                     reads=[R["mixsT"], R_wos[c // 4]], writes=[R_bank[hh]])
            P.op("dve", lambda e, hh=hh: e.scalar_tensor_tensor(out=mixs[0:NS, hh * 512:(hh + 1) * 512], in0=xs_sb[0:NS, hh * 512:(hh + 1) * 512],
                                                               scalar=ALPHA, in1=banks[hh][0:NS, :], op0=ALU.mult, op1=ALU.add),
                 reads=[R_bank[hh], R["xs"], R["mixsT"]], writes=[R["mixs"]])
        layer_norm_rows(NS, mixs, stat, R["mixs"], R["stat"], xs_sb, R["ys"], lng, lnb, R_ln)
        out_tokens.append(dma("sp", o_ys[:, :], xs_sb[0:NS, :], reads=[R["ys"]]))
        return list(R.values()) + R_wos + R_kvt + R_kvb + R_wsb + R_s8 + R_p8 + R_msk + R_dd + [R_ln] + R_vbf

    def kvout_phase(prev):
        wkv = carve([128, 8, 1280], BF16, reset=True)
        kvo = [carve([128, 1024], F32) for _ in range(2)]
        cosT = carve([128, 16, 8], F32)
        sinT = carve([128, 16, 8], F32)
        cosTb = carve([128, 8], F32)
        sinTb = carve([128, 8], F32)
        tmp = [carve([128, 8, 8], F32) for _ in range(4)]
        R_w = Res()
        R_c = Res()
        R_kvo = [Res(), Res()]
        R_tmp = Res()
        dma("pool", wkv[:], d_wkv.rearrange("(kc p) f -> p kc f", p=128), writes=[R_w] + prev)
        dma("sp", cosT[:], dc["cosT"][:, :, :], writes=[R_c])
        dma("sp", sinT[:], dc["sinT"][:, :, :], writes=[R_c])
        dma("sp", cosTb[:], dc["cosTb"][:, :], writes=[R_c])
        dma("sp", sinTb[:], dc["sinTb"][:, :], writes=[R_c])
        for blk in range(17):
            isb = (blk == 16)
            t0 = (S - 128) if isb else (2048 + blk * 128)
            nh = 2 if isb else 8
            wcol = 1024 if isb else 0
            kw = nh * 64
            ob = kvo[blk % 2]
            rk = R_kvo[blk % 2]
            kb0 = 2 * (blk % 2)
            for part in range(2):
                bk = kb0 + part
                for kc in range(8):
                    P.op("pe", lambda e, kc=kc, bk=bk, t0=t0, wcol=wcol, kw=kw, part=part: e.matmul(
                        banks[bk][:, 0:kw], lhsT=xT[:, kc, t0:t0 + 128], rhs=wkv[:, kc, wcol + part * kw: wcol + (part + 1) * kw],
                        start=(kc == 0), stop=(kc == 7)),
                        reads=[R_xT[kc], R_w], writes=[R_bank[bk]])
            P.op("dve", lambda e, ob=ob, kw=kw, kb0=kb0: e.tensor_copy(out=ob[:, 0:kw], in_=banks[kb0][:, 0:kw]), reads=[R_bank[kb0]], writes=[rk])
            P.op("act", lambda e, ob=ob, kw=kw, kb0=kb0: e.copy(out=ob[:, kw:2 * kw], in_=banks[kb0 + 1][:, 0:kw]), reads=[R_bank[kb0 + 1]], writes=[rk])
            kps = banks[kb0][:, 0:kw].rearrange("p (h d) -> p h d", d=64)
            x1 = kps[:, :, 0:8]
            x2 = kps[:, :, 8:16]
            if isb:
                cb_ = cosTb[:, :].unsqueeze(1).broadcast_to([128, nh, 8])
                sb_ = sinTb[:, :].unsqueeze(1).broadcast_to([128, nh, 8])
            else:
                cb_ = cosT[:, blk, :].unsqueeze(1).broadcast_to([128, nh, 8])
                sb_ = sinT[:, blk, :].unsqueeze(1).broadcast_to([128, nh, 8])
            t = [tt[:, 0:nh, :] for tt in tmp]
            P.op("dve", lambda e, t=t, x1=x1, cb_=cb_: e.tensor_tensor(out=t[0], in0=x1, in1=cb_, op=ALU.mult), reads=[R_bank[kb0], R_c], writes=[R_tmp])
            P.op("dve", lambda e, t=t, x2=x2, sb_=sb_: e.tensor_tensor(out=t[1], in0=x2, in1=sb_, op=ALU.mult), reads=[R_bank[kb0]], writes=[R_tmp])
            P.op("dve", lambda e, t=t, x2=x2, cb_=cb_: e.tensor_tensor(out=t[2], in0=x2, in1=cb_, op=ALU.mult), reads=[R_bank[kb0]], writes=[R_tmp])
            P.op("dve", lambda e, t=t, x1=x1, sb_=sb_: e.tensor_tensor(out=t[3], in0=x1, in1=sb_, op=ALU.mult), reads=[R_bank[kb0]], writes=[R_tmp])
            ov = ob[:, 0:kw].rearrange("p (h d) -> p h d", d=64)
            P.op("dve", lambda e, t=t, ov=ov: e.tensor_tensor(out=ov[:, :, 0:8], in0=t[0], in1=t[1], op=ALU.subtract), reads=[R_tmp], writes=[rk])
            P.op("dve", lambda e, t=t, ov=ov: e.tensor_tensor(out=ov[:, :, 8:16], in0=t[2], in1=t[3], op=ALU.add), reads=[R_tmp], writes=[rk])
            if isb:
                out_tokens.append(dma("sp", o_pb[:, :], ob[:, 0:256], reads=[rk]))
            else:
                out_tokens.append(dma("sp", o_pa[blk * 128:(blk + 1) * 128, :], ob[:, :], reads=[rk]))
        return [R_w, R_c, R_tmp] + R_kvo

    def pair_phase(prev, npair=NPAIR):
        QT = carve([128, S], BF16, reset=True)
        KT = carve([128, S], BF16)
        VT = carve([128, S], BF16)
        Vd = carve([128, 32, 2, 66], BF16)
        acc = carve([128, 2, 2048], F32)
        wbuf0 = carve([128, 8, 640], BF16)
        wbuf1 = mixT[:, 6:8, :].rearrange("p c t -> p (c t)")[:, 3072:3072 + 5120].rearrange("p (k f) -> p k f", k=8)
        pT = [carve([128, 1024], BF16) for _ in range(2)]
        cct = [carve([128, 512], F32)] * 2
        sst = [carve([128, 512], F32)] * 2
        t1 = carve([128, 512], F32)
        t2 = carve([128, 512], F32)
        rot = [carve([128, 512], BF16)] * 2
        pT = pT + [t1.bitcast(BF16)]
        mtmp = [carve([128, 512], F32) for _ in range(2)]
        R_QT = [Res() for _ in range(8)]
        R_KT = [Res() for _ in range(8)]
        R_VT = [Res() for _ in range(8)]
        R_Vd = Res()
        R_acc = [Res(), Res()]
        R_den = [Res() for _ in range(4)]

        def P_merge_deps(src, dst):
            dst.r.extend(src.r)
            if src.w is not None:
                dst.r.append(src.w)
            src.r = []
        R_w = [Res(), Res()]
        R_pT = [Res(), Res()]
        R_cs = [Res()] * 2
        R_t = Res()
        R_pT = R_pT + [R_t]
        R_rot = [Res()] * 2
        R_n1 = Res()
        R_m = [Res(), Res()]
        R_mix = [[Res() for _ in range(8)] for _ in range(8)]
        d_wc3 = d_wc.rearrange("(kc p) f -> p kc f", p=128)

        def wbuf_of(jj):
            return (wbuf1, R_w[1]) if (jj % 2 == 1 and jj <= 5) else (wbuf0, R_w[0])

        def load_w(jj):
            wb_, rw_ = wbuf_of(jj)
            dma("pool", wb_, d_wc3[:, :, jj * 640:(jj + 1) * 640], writes=[rw_])
        load_w(0)
        for (hh_, cc_, val_) in ((0, 64, 1.0), (0, 65, 0.0), (1, 64, 0.0), (1, 65, 1.0)):
            P.op("pool", lambda e, hh_=hh_, cc_=cc_, val_=val_: e.memset(Vd[:, :, hh_, cc_:cc_ + 1], val_), writes=[R_Vd])

        pcount = [0]
        scount = [0]
        u_started = set()
        PB = (0, 1)
        SBK = ((2, 3), (0, 1))
        UB = ((4, 5), (6, 7))

        for j in range(npair):
            isB = j >= 4
            has_kv = (not isB) or j == 4
            wb, rw = wbuf_of(j)
            early = (j + 1 < npair) and (wbuf_of(j + 1)[0] is not wb)
            if early:
                load_w(j + 1)
            for tile in range(8):
                tsl = slice(tile * 512, (tile + 1) * 512)
                ci2 = tile % 2
                dma("sp", cct[ci2][0:64, :], dc["CC"][:, tsl], writes=[R_cs[ci2]])
                dma("sp", sst[ci2][0:64, :], dc["SS"][:, tsl], writes=[R_cs[ci2]])
                for ci, kind in enumerate("QKVGR"):
                    if kind in "KV" and not has_kv:
                        continue
                    bk = (0, 1, 2, 3)[pcount[0] % 4]
                    pcount[0] += 1
                    for kc in range(8):
                        P.op("pe", lambda e, kc=kc, bk=bk, ci=ci, wb=wb, tsl=tsl: e.matmul(
                            banks[bk][:, :], lhsT=wb[:, kc, ci * 128:(ci + 1) * 128], rhs=xT[:, kc, tsl], start=(kc == 0), stop=(kc == 7)),
                            reads=[R_xT[kc], rw], writes=[R_bank[bk]])
                    if kind == "Q":
                        P.op("act", lambda e, bk=bk, tsl=tsl: e.copy(out=QT[:, tsl], in_=banks[bk][:, :]), reads=[R_bank[bk]], writes=[R_QT[tile]])
                    elif kind == "K":
                        P.op("dve", lambda e, bk=bk, tsl=tsl: e.tensor_copy(out=KT[:, tsl], in_=banks[bk][:, :]), reads=[R_bank[bk]], writes=[R_KT[tile]])
                    elif kind == "V":
                        P.op("act", lambda e, bk=bk, tsl=tsl: e.copy(out=VT[:, tsl], in_=banks[bk][:, :]), reads=[R_bank[bk]], writes=[R_VT[tile]])
                    elif kind == "G":
                        P.op("act", lambda e, bk=bk, tsl=tsl, j=j: e.activation(out=mixT[:, j, tsl], in_=banks[bk][:, :], func=AF.Silu),
                             reads=[R_bank[bk]], writes=[R_mix[j][tile]])
                    else:
                        P.op("dve", lambda e, bk=bk, ci2=ci2: e.tensor_tensor(out=t1[0:64, :], in0=banks[bk][0:64, :], in1=cct[ci2][0:64, :], op=ALU.mult),
                             reads=[R_bank[bk], R_cs[ci2]], writes=[R_t])
                        P.op("dve", lambda e, bk=bk, ci2=ci2: e.tensor_tensor(out=t2[0:64, :], in0=banks[bk][64:128, :], in1=sst[ci2][0:64, :], op=ALU.mult),
                             reads=[R_bank[bk], R_cs[ci2]], writes=[R_t])
                        P.op("dve", lambda e, ci2=ci2: e.tensor_tensor(out=rot[ci2][0:64, :], in0=t1[0:64, :], in1=t2[0:64, :], op=ALU.add),
                             reads=[R_t], writes=[R_rot[ci2]])
                        dma("sp", QT[0:16, tsl], rot[ci2][0:16, :], reads=[R_rot[ci2]], writes=[R_QT[tile]])
                        dma("sp", QT[64:80, tsl], rot[ci2][16:32, :], reads=[R_rot[ci2]], writes=[R_QT[tile]])
                        if has_kv:
                            dma("sp", KT[0:16, tsl], rot[ci2][32:48, :], reads=[R_rot[ci2]], writes=[R_KT[tile]])
                            dma("sp", KT[64:80, tsl], rot[ci2][48:64, :], reads=[R_rot[ci2]], writes=[R_KT[tile]])
            if j + 1 < npair and not early:
                load_w(j + 1)
            KSTOP = int(os.environ.get("KSTOP", "9"))
            if KSTOP <= 1:
                continue
            dils = (1,) if isB else DILS
            for H in range(2):
                for di, d in enumerate(dils):
                    nb = 32 // d
                    hb = nb // 2
                    build_vd = (not isB) or (j == 4 and H == 0)
                    need = [sl for sl in range(32) if (isB or H == 1 or (sl % nb) < hb)]
                    grp8 = [need[q:q + 4] for q in range(0, len(need), 4)]
                    for g8, slots4 in enumerate(grp8 if build_vd else []):
                        bk = PB[pcount[0] % 2]
                        pcount[0] += 1
                        bfv = banks[bk][:, :].bitcast(BF16)
                        for i, slot in enumerate(slots4):
                            r, b = slot // nb, slot % nb
                            t0 = d * 128 * b + r
                            P.op("pe", lambda e, bfv=bfv, i=i, t0=t0, d=d: e.transpose(out=bfv[:, i * 128:(i + 1) * 128],
                                                                                 in_=strided(VT[:, :], t0, d, 128), identity=ident[:, :]),
                                 reads=R_VT + [R_const], writes=[R_bank[bk]])
                        eng = "act" if g8 % 2 == 0 else "dve"
                        src = bfv[:, 0:512].rearrange("p (s h d) -> p s h d", s=4, h=2)
                        if slots4[3] - slots4[0] == 3:
                            dst = Vd[:, slots4[0]:slots4[0] + 4, :, 0:64]
                        else:
                            assert [x - slots4[0] for x in slots4] == [0, 2, 4, 6], slots4
                            dst = Vd[:, slots4[0]:slots4[0] + 8, :, 0:64].rearrange("p (s two) h d -> p s two h d", two=2)[:, :, 0, :, :]
                        if eng == "act":
                            P.op("act", lambda e, src=src, dst=dst: e.copy(out=dst, in_=src), reads=[R_bank[bk]], writes=[R_Vd])
                        else:
                            P.op("dve", lambda e, src=src, dst=dst: e.tensor_copy(out=dst, in_=src), reads=[R_bank[bk]], writes=[R_Vd])
                    if KSTOP <= 2:
                        continue
                    subs = []
                    for r in range(d):
                        b_lo, b_hi = H * hb, (H + 1) * hb
                        kbs = list(range(b_lo, b_hi))
                        if H == 1:
                            kbs = [b_lo - 1] + kbs
                        for kb in kbs:
                            has_cur = kb >= b_lo
                            has_next = (kb + 1 < b_hi)
                            if not (has_cur or has_next):
                                continue
                            qb0 = kb if has_cur else kb + 1
                            subs.append(dict(r=r, kb=kb, has_cur=has_cur, has_next=has_next,
                                             nq=128 * (int(has_cur) + int(has_next)), c0=0 if has_cur else 128,
                                             kt0=d * 128 * kb + r, qt0=d * 128 * qb0 + r,
                                             slot=r * nb + kb, qi=r * hb + (kb - b_lo)))
                    packed = (d == 16)
                    gsz = 4 if packed else 2
                    groups = [subs[m0:m0 + gsz] for m0 in range(0, len(subs), gsz)]
                    for _g in groups:
                        for i_, u_ in enumerate(_g):
                            u_["col"] = i_ * 128 if packed else i_ * 256 + u_["c0"]
                    gsidx = []
                    for _g in groups:
                        gsidx.append((scount[0] % 2, scount[0] % 3))
                        scount[0] += 1

                    def stage_a(grp, sidx2):
                        sbk = SBK[sidx2[0]]
                        sidx = sidx2[1]
                        full = (not packed and len(grp) == 2 and all(u["nq"] == 256 for u in grp))
                        for i, u in enumerate(grp):
                            for h in range(2):
                                P.op("pe", lambda e, sbk=sbk, h=h, i=i, u=u, d=d: e.matmul(
                                    banks[sbk[h]][:, u["col"]: u["col"] + u["nq"]],
                                    lhsT=strided(KT[h * 64:(h + 1) * 64, :], u["kt0"], d, 128),
                                    rhs=strided(QT[h * 64:(h + 1) * 64, :], u["qt0"], d, u["nq"]), start=True, stop=True),
                                    reads=R_QT + R_KT, writes=[R_bank[sbk[h]]])
                        if full:
                            for h in range(2):
                                P.op("act", lambda e, sbk=sbk, h=h, sidx=sidx: e.activation(
                                    out=pT[sidx][:, h * 512:(h + 1) * 512], in_=banks[sbk[h]][:, :], func=AF.Exp, scale=0.125),
                                    reads=[R_bank[sbk[h]]], writes=[R_pT[sidx]])
                            P.op("dve", lambda e, sidx=sidx: e.tensor_tensor(out=pT[sidx][:, :], in0=pT[sidx][:, :], in1=mask2[:, :], op=ALU.mult),
                                 reads=[R_const], writes=[R_pT[sidx]])
                        elif packed:
                            ncol = 128 * len(grp)
                            for h in range(2):
                                P.op("act", lambda e, sbk=sbk, h=h, ncol=ncol, sidx=sidx: e.activation(
                                    out=pT[sidx][:, h * 512: h * 512 + ncol], in_=banks[sbk[h]][:, 0:ncol], func=AF.Exp, scale=0.125),
                                    reads=[R_bank[sbk[h]]], writes=[R_pT[sidx]])
                            types = [u["has_cur"] for u in grp]
                            pv3 = pT[sidx][:, :].rearrange("p (h q) -> p h q", h=2)[:, :, 0:ncol]
                            if all(types):
                                n_ = len(grp)
                                pvv = pv3.rearrange("p h (n c) -> p h n c", c=128)
                                mvv = mask2[:, 0:128].unsqueeze(1).unsqueeze(1).broadcast_to([128, 2, n_, 128])
                            else:
                                assert types == [False, True] * (len(grp) // 2) and len(grp) % 2 == 0, types
                                n_ = len(grp) // 2
                                pvv = pv3.rearrange("p h (n c) -> p h n c", c=256)
                                mvv = mask2[:, 128:384].unsqueeze(1).unsqueeze(1).broadcast_to([128, 2, n_, 256])
                            P.op("dve", lambda e, pvv=pvv, mvv=mvv: e.tensor_tensor(out=pvv, in0=pvv, in1=mvv, op=ALU.mult),
                                 reads=[R_const], writes=[R_pT[sidx]])
                        else:
                            iv = []
                            for i, u in enumerate(grp):
                                lo = u["col"]
                                if iv and iv[-1][1] == lo:
                                    iv[-1][1] = lo + u["nq"]
                                else:
                                    iv.append([lo, lo + u["nq"]])
                            if len(iv) == 2 and (iv[0][1] - iv[0][0]) == (iv[1][1] - iv[1][0]):
                                w_ = iv[0][1] - iv[0][0]
                                st_ = iv[1][0] - iv[0][0]

                                def cols(ap2, lo=iv[0][0], w_=w_, st_=st_):
                                    a = ap2[:, lo:lo + st_ * 2] if lo + st_ * 2 <= ap2.shape[1] else None
                                    if a is not None:
                                        return a.rearrange("p (t c) -> p t c", t=2)[:, :, 0:w_]
                                    return None
                                views = [cols]
                            else:
                                views = [(lambda ap2, lo=lo_, hi=hi_: ap2[:, lo:hi]) for (lo_, hi_) in iv]
                            ok = all(v(mask2[:, 0:512]) is not None for v in views)
                            if not ok:
                                views = [(lambda ap2, lo=lo_, hi=hi_: ap2[:, lo:hi]) for (lo_, hi_) in iv]
                            for v in views:
                                for h in range(2):
                                    P.op("act", lambda e, sbk=sbk, h=h, v=v, sidx=sidx: e.activation(
                                        out=v(pT[sidx][:, h * 512:(h + 1) * 512]), in_=v(banks[sbk[h]][:, :]), func=AF.Exp, scale=0.125),
                                        reads=[R_bank[sbk[h]]], writes=[R_pT[sidx]])
                            for v in views:
                                for h in range(2):
                                    P.op("dve", lambda e, h=h, v=v, sidx=sidx: e.tensor_tensor(
                                        out=v(pT[sidx][:, h * 512:(h + 1) * 512]), in0=v(pT[sidx][:, h * 512:(h + 1) * 512]),
                                        in1=v(mask2[:, h * 512:(h + 1) * 512]), op=ALU.mult),
                                        reads=[R_const], writes=[R_pT[sidx]])

                    def stage_b(grp, sidx2):
                        sidx = sidx2[1]
                        for i, u in enumerate(grp):
                            qi, slot, kb = u["qi"], u["slot"], u["kb"]
                            for h in range(2):
                                base = h * 512 + u["col"]
                                pieces = []
                                if u["has_cur"] and u["has_next"] and qi % 4 != 3:
                                    pieces.append((qi, 2, 0))
                                else:
                                    if u["has_cur"]:
                                        pieces.append((qi, 1, 0))
                                    if u["has_next"]:
                                        pieces.append((qi + 1, 1, 128 if u["has_cur"] else 0))
                                for (q0, nqb, poff) in pieces:
                                    ub = UB[h][(q0 // 4) % 2]
                                    key = (j, H, d, h, q0 // 4)
                                    fresh = key not in u_started
                                    u_started.add(key)
                                    P.op("pe", lambda e, ub=ub, q0=q0, nqb=nqb, poff=poff, slot=slot, h=h, sidx=sidx, base=base, fresh=fresh: e.matmul(
                                        banks[ub][0:66, (q0 % 4) * 128:(q0 % 4) * 128 + 128 * nqb], lhsT=Vd[:, slot, h, :],
                                        rhs=pT[sidx][:, base + poff: base + poff + 128 * nqb], start=fresh, stop=True, skip_group_check=True),
                                        reads=[R_Vd, R_pT[sidx]], writes=[R_bank[ub]])
                            if u["has_cur"] and qi % 4 == 3:
                                g = qi // 4
                                for h in range(2):
                                    ub = UB[h][g % 2]
                                    a2 = acc[0:66, h, :]
                                    if d == 1:
                                        dst = a2[:, g * 512:(g + 1) * 512]
                                        src = banks[ub][0:66, :]
                                    elif d == 4:
                                        dst = a2.rearrange("p (m r) -> p m r", r=4)[:, :, g]
                                        src = banks[ub][0:66, :]
                                    else:
                                        dst = a2.rearrange("p (m r) -> p r m", r=16)[:, g * 4:(g + 1) * 4, :]
                                        src = banks[ub][0:66, :].rearrange("p (r m) -> p r m", r=4)
                                    if di == 0:
                                        P.op("dve", lambda e, dst=dst, src=src: e.tensor_copy(out=dst, in_=src), reads=[R_bank[ub]], writes=[R_acc[h]])
                                    else:
                                        P.op("dve", lambda e, dst=dst, src=src: e.tensor_tensor(out=dst, in0=src, in1=dst, op=ALU.add),
                                             reads=[R_bank[ub]], writes=[R_acc[h]])

                    for gi in range(min(2, len(groups))):
                        stage_a(groups[gi], gsidx[gi])
                    for gi in range(len(groups)):
                        if gi + 2 < len(groups):
                            stage_a(groups[gi + 2], gsidx[gi + 2])
                        stage_b(groups[gi], gsidx[gi])
                ntq = 4 if KSTOP > 3 else 0
                for tq in range(ntq):
                    lsl = slice(tq * 512, (tq + 1) * 512)
                    den = acc[64:66, 0, lsl]
                    if isB:
                        i_ = j - 4
                        P.op("dve", lambda e, i_=i_, den=den, lsl=lsl: e.scalar_tensor_tensor(out=den, in0=den, scalar=esink[64:66, i_:i_ + 1],
                                                                                         in1=acc[64:66, 1, lsl], op0=ALU.add, op1=ALU.add),
                             reads=[R_acc[0], R_acc[1], R_const], writes=[R_den[tq]])
                    else:
                        P.op("dve", lambda e, den=den, lsl=lsl: e.tensor_tensor(out=den, in0=den, in1=acc[64:66, 1, lsl], op=ALU.add),
                             reads=[R_acc[0], R_acc[1]], writes=[R_den[tq]])
                for tq in range(ntq):
                    den = acc[64:66, 0, tq * 512:(tq + 1) * 512]
                    P.op("act", lambda e, den=den: e.activation(out=den, in_=den, func=AF.Ln), writes=[R_den[tq]])
                    P.op("act", lambda e, den=den: e.activation(out=den, in_=den, func=AF.Exp, scale=-1.0), writes=[R_den[tq]])
                for tq in range(ntq):
                    bk = PB[pcount[0] % 2]
                    pcount[0] += 1
                    tile = H * 4 + tq
                    gsl = slice(tile * 512, (tile + 1) * 512)
                    lsl = slice(tq * 512, (tq + 1) * 512)
                    P.op("pe", lambda e, bk=bk, lsl=lsl: e.matmul(banks[bk][:, :], lhsT=sel2[64:66, :], rhs=acc[64:66, 0, lsl], start=True, stop=True),
                         reads=[R_den[tq], R_const], writes=[R_bank[bk]])
                    mi = tq % 2
                    P.op("dve", lambda e, bk=bk, lsl=lsl, mi=mi: e.tensor_tensor(out=mtmp[mi][0:64, :], in0=banks[bk][0:64, :], in1=acc[0:64, 0, lsl], op=ALU.mult),
                         reads=[R_bank[bk], R_acc[0]], writes=[R_m[mi]])
                    P.op("dve", lambda e, bk=bk, lsl=lsl, mi=mi: e.tensor_tensor(out=mtmp[mi][64:128, :], in0=banks[bk][64:128, :], in1=acc[0:64, 1, lsl], op=ALU.mult),
                         reads=[R_bank[bk], R_acc[1]], writes=[R_m[mi]])
                    P.op("pool", lambda e, mi=mi, gsl=gsl, j=j: e.tensor_tensor(out=mixT[:, j, gsl], in0=mtmp[mi][:, :], in1=mixT[:, j, gsl], op=ALU.mult),
                         reads=[R_m[mi]], writes=[R_mix[j][tile]])
                for tq in range(ntq):
                    P_merge_deps(R_den[tq], R_acc[0])
        allres = R_QT + R_KT + R_VT + [R_Vd] + R_acc + R_w + R_pT + R_cs + [R_t] + R_rot + R_m
        return allres, R_mix

    def out_phase(prev, R_mix):
        wo = carve([128, 8, D], BF16, reset=True)
        NB_ = 4
        xr = [carve([128, D], F32) for _ in range(NB_)]
        zz = [carve([128, D], F32) for _ in range(NB_)]
        yy = zz
        stat = [carve([128, 32], F32) for _ in range(NB_)]
        lng = carve([128, D], F32)
        lnb = carve([128, D], F32)
        R_ln = Res()
        R_wo = Res()
        R_xr = [Res() for _ in range(NB_)]
        R_z = [Res() for _ in range(NB_)]
        R_y = R_z
        R_st = [Res() for _ in range(NB_)]
        dma("pool", wo[:], d_wo.rearrange("(kc p) f -> p kc f", p=128), writes=[R_wo] + prev)
        dma("sp", lng[:], d_lng[0, :].partition_broadcast(128), writes=[R_ln])
        dma("sp", lnb[:], d_lnb[0, :].partition_broadcast(128), writes=[R_ln])
        def load_x(tb):
            if tb < 32:
                dma("sp", xr[tb % NB_][:], d_x[tb * 128:(tb + 1) * 128, :], writes=[R_xr[tb % NB_]])
        for tb in range(NB_ - 1):
            load_x(tb)
        for tb in range(32):
            i2 = tb % NB_
            tile = tb // 4
            load_x(tb + NB_ - 1)
            for hh in range(2):
                bk = 2 * (tb % 2) + hh
                for c in range(8):
                    P.op("pe", lambda e, bk=bk, c=c, tb=tb, hh=hh: e.matmul(banks[bk][:, :], lhsT=mixT[:, c, tb * 128:(tb + 1) * 128],
                                                                        rhs=wo[:, c, hh * 512:(hh + 1) * 512], start=(c == 0), stop=(c == 7)),
                         reads=[R_mix[c][tile], R_wo], writes=[R_bank[bk]])
                P.op("dve", lambda e, bk=bk, hh=hh, i2=i2: e.scalar_tensor_tensor(out=zz[i2][:, hh * 512:(hh + 1) * 512], in0=xr[i2][:, hh * 512:(hh + 1) * 512],
                                                                              scalar=ALPHA, in1=banks[bk][:, :], op0=ALU.mult, op1=ALU.add),
                     reads=[R_bank[bk], R_xr[i2]], writes=[R_z[i2]])
            layer_norm_rows(128, zz[i2], stat[i2], R_z[i2], R_st[i2], yy[i2], R_y[i2], lng, lnb, R_ln)
            out_tokens.append(dma("sp", o_y[tb * 128:(tb + 1) * 128, :], yy[i2][:], reads=[R_y[i2]]))

    PH = os.environ.get("KPH", "sample,kv,pair,out").split(",")
    NP_ = int(os.environ.get("KNP", str(NPAIR)))
    if "sample" in PH:
        r1 = sample_phase()
    load_xT()
    if "sample" in PH:
        phase_barrier(r1)
    if "kv" in PH:
        r2 = kvout_phase([])
        phase_barrier(r2)
    R_mix = [[Res() for _ in range(8)] for _ in range(8)]
    if "pair" in PH:
        r3, R_mix = pair_phase([], NP_)
        phase_barrier(r3)
    if "out" in PH:
        out_phase([], R_mix)
    Res.default_w = None
    P.q["sp"].append((None, list(out_tokens), None))

    rank = {}
    for eng in Prog.ENG:
        cnt = 0
        for idx in range(len(P.q[eng])):
            if (eng, idx) in P.needed:
                cnt += 1
                rank[(eng, idx)] = cnt

    def replay(eng, e):
        waited = {}
        for idx, (fn, ws, dtok) in enumerate(P.q[eng]):
            for w in ws:
                if w[0] == "dma":
                    key, val, sem = ("dma", w[1]), w[2], sem_dma[w[1]]
                else:
                    key, val, sem = w[0], rank[w], sem_eng[w[0]]
                if waited.get(key, 0) >= val:
                    continue
                e.wait_ge(sem, val)
                waited[key] = val
            if fn is None:
                continue
            ins = fn(e)
            if dtok is not None:
                ins.then_inc(sem_dma[dtok[1]], 16)
            elif (eng, idx) in P.needed:
                ins.then_inc(sem_eng[eng], 1)

    with nc.Block() as block:
        @block.tensor
        def _(e):
            replay("pe", e)

        @block.scalar
        def _(e):
            replay("act", e)

        @block.vector
        def _(e):
            replay("dve", e)

        @block.gpsimd
        def _(e):
            replay("pool", e)

        @block.sync
        def _(e):
            replay("sp", e)
    es.close()
    return nc, consts


def kernel(x_prompt, x_sample, cache_a_kv, cache_b_kv, w_in, attn_sinks, w_o, ln_g, ln_b):
    x_prompt = np.asarray(x_prompt, np.float32)
    x_sample = np.asarray(x_sample, np.float32)
    cache_a_kv = np.asarray(cache_a_kv, np.float32)
    cache_b_kv = np.asarray(cache_b_kv, np.float32)
    w_in0 = np.asarray(w_in, np.float32)[0]
    w_o0 = np.asarray(w_o, np.float32)[0]
    sinks = np.asarray(attn_sinks, np.float32)[0]
    if "nc" not in _CACHE:
        _CACHE["nc"] = build()
    nc, consts = _CACHE["nc"]
    in_maps = _prep(consts, x_prompt, x_sample, cache_a_kv, cache_b_kv, w_in0, w_o0, sinks, ln_g, ln_b)
    return _run(nc, in_maps)


def _prep(consts, x_prompt, x_sample, cache_a_kv, cache_b_kv, w_in0, w_o0, sinks, ln_g, ln_b, cores=range(NCORES)):

    cols = []
    for j in range(NPAIR):
        cols += _pair_cols(j)
    wc = np.ascontiguousarray(w_in0[:, cols])
    wkv = np.ascontiguousarray(np.concatenate([w_in0[:, 512:1536], w_in0[:, 2560:2816]], axis=1))
    wo_p = np.ascontiguousarray(w_o0[_wo_rows(), :])
    sinkp = np.zeros((128, 4), np.float32)
    sinkp[64, :] = sinks[0:4]
    sinkp[65, :] = sinks[4:8]
    shared = {
        "wc": wc, "wkv": wkv, "wo": wo_p, "wos": np.ascontiguousarray(w_o0), "wins": np.ascontiguousarray(w_in0),
        "lng": np.ascontiguousarray(np.asarray(ln_g, np.float32).reshape(1, D)),
        "lnb": np.ascontiguousarray(np.asarray(ln_b, np.float32).reshape(1, D)),
        "sink": np.ascontiguousarray(sinks.reshape(1, 8)), "sinkp": sinkp,
    }
    for k, v in consts.items():
        shared["c_" + k] = v
    in_maps = []
    for c in cores:
        m = dict(shared)
        m["xT"] = np.ascontiguousarray(x_prompt[c].T)
        m["x"] = np.ascontiguousarray(x_prompt[c])
        xs = x_sample[c * NS:(c + 1) * NS, 0, :]
        m["xsT"] = np.ascontiguousarray(xs.T)
        m["xs"] = np.ascontiguousarray(xs)
        m["ca"] = np.ascontiguousarray(cache_a_kv[0, c * NS:(c + 1) * NS].reshape(NS, 2048, 1024))
        m["cb"] = np.ascontiguousarray(cache_b_kv[0, c * NS:(c + 1) * NS].reshape(NS, 128, 256))
        in_maps.append(m)
    return in_maps


def _run(nc, in_maps):
    res = run_bass_kernel_spmd(nc, in_maps, core_ids=list(range(NCORES)))
    rs = res.results
    y = np.stack([rs[c]["y"] for c in range(NCORES)]).reshape(8, S, D)
    ys = np.concatenate([rs[c]["ys"] for c in range(NCORES)]).reshape(128, 1, D)
    pa = np.stack([rs[c]["pkva"] for c in range(NCORES)]).reshape(1, 8, 2048, 2, 8, 64)
    pb = np.stack([rs[c]["pkvb"] for c in range(NCORES)]).reshape(1, 8, 128, 2, 2, 64)
    sa = np.concatenate([rs[c]["skva"] for c in range(NCORES)]).reshape(1, 128, 1, 2, 8, 64)
    sb_ = np.concatenate([rs[c]["skvb"] for c in range(NCORES)]).reshape(1, 128, 1, 2, 2, 64)
    return (y.astype(np.float32), ys.astype(np.float32), pa.astype(np.float32), pb.astype(np.float32),
            sa.astype(np.float32), sb_.astype(np.float32))
```
